# Optimizing a Trainium2 kernel written in Bass

```python
import jax, jax.numpy as jnp
from jax import lax
import numpy as np

D_MODEL = 1024
BATCH = 2
SEQ = 16384
DEPTH = 4

N_MIXERS = 3
D_HEAD = 64
ROT_DIM = D_HEAD // 4
ROPE_THETA = 500000.0
D_FF = 2816
PLE_DIM = 256
RMS_EPS = 1e-6
NEG_INF = -1e30
N_NORMS = 8
A_PAIRS = ((128, 1), (512, 4), (2048, 16))
A_HEADS = 8
A_BLOCK = 64
B_HEADS = 16
B_KV_HEADS = 4
B_RADIUS = 128
B_BLOCK = 128
C_HEADS = 16
GRID_W = 64
NA_ROWS = 8
NA_COLS = 16
NA_QC = 16
NA_KC = 2 * NA_QC

kernel_name = "hybrid_dilated_swa_natten_macaron"


def rms_norm(x, g):
    x32 = x.astype(jnp.float32)
    y = x32 * lax.rsqrt(jnp.mean(x32 * x32, axis=-1, keepdims=True) + RMS_EPS)
    return (y * g.astype(jnp.float32)).astype(x.dtype)


def swiglu(x, wi, wo):
    gate, up = jnp.split(x @ wi, 2, axis=-1)
    return (jax.nn.silu(gate) * up) @ wo


def rope_tables(seq):
    pos = jnp.arange(seq, dtype=jnp.float32)
    inv = ROPE_THETA ** (-jnp.arange(0, ROT_DIM, 2, dtype=jnp.float32) / ROT_DIM)
    ang = pos[:, None] * inv[None, :]
    return jnp.cos(ang), jnp.sin(ang)


def apply_rope(x, cos, sin):
    shape = (cos.shape[0],) + (1,) * (x.ndim - 3) + (cos.shape[1],)
    c = cos.reshape(shape).astype(x.dtype)
    s = sin.reshape(shape).astype(x.dtype)
    half = ROT_DIM // 2
    x1, x2, rest = x[..., :half], x[..., half:ROT_DIM], x[..., ROT_DIM:]
    return jnp.concatenate([x1 * c - x2 * s, x2 * c + x1 * s, rest], axis=-1)


def band_attention(q, k, v, radius, block, sink=None):
    n, length, hk, grp, dh = q.shape
    nb = -(-length // block)
    pad = nb * block - length
    qb = jnp.pad(q, ((0, 0), (0, pad), (0, 0), (0, 0), (0, 0))).reshape(n, nb, block, hk, grp, dh)

    def three_blocks(a):
        a = jnp.pad(a, ((0, 0), (block, pad + block), (0, 0), (0, 0))).reshape(n, nb + 2, block, hk, dh)
        return jnp.concatenate([a[:, :-2], a[:, 1:-1], a[:, 2:]], axis=2)

    kb, vb = three_blocks(k), three_blocks(v)
    qpos = jnp.arange(nb * block).reshape(nb, block)
    kpos = jnp.arange(nb)[:, None] * block - block + jnp.arange(3 * block)[None, :]
    rel = kpos[:, None, :] - qpos[:, :, None]
    valid = (jnp.abs(rel) <= radius) & (kpos[:, None, :] >= 0) & (kpos[:, None, :] < length)
    s = jnp.einsum('nbqhgd,nbkhd->nbhgqk', qb, kb).astype(jnp.float32) * (dh ** -0.5)
    s = jnp.where(valid[None, :, None, None], s, NEG_INF)
    m = jnp.max(s, axis=-1)
    if sink is not None:
        sk = sink.astype(jnp.float32)[None, None, :, :, None]
        m = jnp.maximum(m, sk)
    pr = jnp.exp(s - m[..., None])
    den = jnp.sum(pr, axis=-1)
    if sink is not None:
        den = den + jnp.exp(sk - m)
    o = jnp.einsum('nbhgqk,nbkhd->nbqhgd', pr.astype(v.dtype), vb)
    o = o / jnp.transpose(den, (0, 1, 4, 2, 3))[..., None]
    lse = jnp.transpose(m + jnp.log(den), (0, 1, 4, 2, 3))
    o = o.reshape(n, nb * block, hk, grp, dh)[:, :length].astype(q.dtype)
    lse = lse.reshape(n, nb * block, hk, grp)[:, :length]
    return o, lse


def dilated_mixer(h, wqkv, wo, cos, sin):
    b, s, _ = h.shape
    ng = len(A_PAIRS)
    qkv = (h @ wqkv).reshape(b, s, 3, ng, A_HEADS, D_HEAD)
    q = apply_rope(qkv[:, :, 0], cos, sin)
    k = apply_rope(qkv[:, :, 1], cos, sin)
    v = qkv[:, :, 2]
    outs, lses = [], []
    for g, (window, dil) in enumerate(A_PAIRS):
        sub = s // dil

        def to_sub(a):
            return a.reshape(b, sub, dil, A_HEADS, D_HEAD).transpose(0, 2, 1, 3, 4).reshape(b * dil, sub, A_HEADS, D_HEAD)

        o, lse = band_attention(to_sub(q[:, :, g])[:, :, :, None], to_sub(k[:, :, g]), to_sub(v[:, :, g]),
                                window // (2 * dil), A_BLOCK)
        o = o[:, :, :, 0].reshape(b, dil, sub, A_HEADS, D_HEAD).transpose(0, 2, 1, 3, 4).reshape(b, s, A_HEADS, D_HEAD)
        lse = lse[..., 0].reshape(b, dil, sub, A_HEADS).transpose(0, 2, 1, 3).reshape(b, s, A_HEADS)
        outs.append(o)
        lses.append(lse)
    wts = jax.nn.softmax(jnp.stack(lses, axis=0), axis=0)
    o = jnp.sum(wts[..., None] * jnp.stack(outs, axis=0).astype(jnp.float32), axis=0).astype(h.dtype)
    return o.reshape(b, s, A_HEADS * D_HEAD) @ wo


def window_gqa_mixer(h, wqkv, wo, sink, cos, sin):
    b, s, _ = h.shape
    grp = B_HEADS // B_KV_HEADS
    qkv = h @ wqkv
    q = qkv[..., :B_HEADS * D_HEAD].reshape(b, s, B_HEADS, D_HEAD)
    k = qkv[..., B_HEADS * D_HEAD:(B_HEADS + B_KV_HEADS) * D_HEAD].reshape(b, s, B_KV_HEADS, D_HEAD)
    v = qkv[..., (B_HEADS + B_KV_HEADS) * D_HEAD:].reshape(b, s, B_KV_HEADS, D_HEAD)
    q = apply_rope(q, cos, sin).reshape(b, s, B_KV_HEADS, grp, D_HEAD)
    k = apply_rope(k, cos, sin)
    o, _ = band_attention(q, k, v, B_RADIUS, B_BLOCK, sink.reshape(B_KV_HEADS, grp))
    return o.reshape(b, s, B_HEADS * D_HEAD) @ wo


def neighbourhood_mixer(h, wqkv, wo, rpb):
    b, s, _ = h.shape
    rows = s // GRID_W
    kh = min(NA_ROWS, rows)
    ncb = GRID_W // NA_QC
    qkv = (h @ wqkv).reshape(b, rows, GRID_W, 3, C_HEADS, D_HEAD)
    q = qkv[:, :, :, 0] * (D_HEAD ** -0.5)
    k, v = qkv[:, :, :, 1], qkv[:, :, :, 2]
    qcol = np.arange(GRID_W).reshape(ncb, NA_QC)
    kstart = np.clip(np.arange(ncb) * NA_QC - NA_COLS // 2, 0, GRID_W - NA_KC)
    kcol = kstart[:, None] + np.arange(NA_KC)[None, :]
    cstart = np.clip(qcol - NA_COLS // 2, 0, GRID_W - NA_COLS)
    col_valid = (kcol[:, None, :] >= cstart[..., None]) & (kcol[:, None, :] < cstart[..., None] + NA_COLS)
    dc_idx = np.clip(kcol[:, None, :] - qcol[:, :, None] + NA_COLS - 1, 0, 2 * NA_COLS - 2)
    kc = k[:, :, kcol]
    vc = v[:, :, kcol]

    def row_fn(r):
        rs = jnp.clip(r - kh // 2, 0, rows - kh)
        kw = lax.dynamic_slice_in_dim(kc, rs, kh, axis=1)
        vw = lax.dynamic_slice_in_dim(vc, rs, kh, axis=1)
        qr = lax.dynamic_index_in_dim(q, r, axis=1, keepdims=False).reshape(b, ncb, NA_QC, C_HEADS, D_HEAD)
        sc = jnp.einsum('bnqhd,bjnchd->bhnqjc', qr, kw).astype(jnp.float32)
        dr_idx = rs + jnp.arange(kh) - r + NA_ROWS - 1
        bias = rpb[:, dr_idx][:, :, dc_idx].transpose(0, 2, 3, 1, 4)
        sc = sc + bias.astype(jnp.float32)[None]
        sc = jnp.where(col_valid[None, None, :, :, None, :], sc, NEG_INF)
        pr = jax.nn.softmax(sc.reshape(b, C_HEADS, ncb, NA_QC, kh * NA_KC), axis=-1)
        pr = pr.reshape(b, C_HEADS, ncb, NA_QC, kh, NA_KC).astype(vw.dtype)
        o = jnp.einsum('bhnqjc,bjnchd->bnqhd', pr, vw)
        return o.reshape(b, GRID_W, C_HEADS * D_HEAD)

    out = lax.map(row_fn, jnp.arange(rows))
    return out.transpose(1, 0, 2, 3).reshape(b, s, C_HEADS * D_HEAD) @ wo


def setup_inputs(seed: int = 0) -> dict:
    key = jax.random.key(seed)
    ks = jax.random.split(key, 16)
    f32 = jnp.float32

    def dense(k, shape, fan_in):
        return jax.random.normal(k, shape, f32) * (fan_in ** -0.5)

    n_a = len(range(0, DEPTH, N_MIXERS))
    n_b = len(range(1, DEPTH, N_MIXERS))
    n_c = len(range(2, DEPTH, N_MIXERS))
    ng = len(A_PAIRS)
    return {
        "x": jax.random.normal(ks[0], (BATCH, SEQ, D_MODEL), f32),
        "p": jax.random.normal(ks[1], (DEPTH, BATCH, SEQ, PLE_DIM), f32),
        "norm_g": 1.0 + 0.02 * jax.random.normal(ks[2], (DEPTH, N_NORMS, D_MODEL), f32),
        "ffn_wi": dense(ks[3], (DEPTH, 2, D_MODEL, 2 * D_FF), D_MODEL),
        "ffn_wo": dense(ks[4], (DEPTH, 2, D_FF, D_MODEL), D_FF),
        "ple_proj": dense(ks[5], (DEPTH, PLE_DIM, D_MODEL), PLE_DIM),
        "ple_gate": dense(ks[6], (DEPTH, D_MODEL, D_MODEL), D_MODEL),
        "a_wqkv": dense(ks[7], (n_a, D_MODEL, 3 * ng * A_HEADS * D_HEAD), D_MODEL),
        "a_wo": dense(ks[8], (n_a, A_HEADS * D_HEAD, D_MODEL), A_HEADS * D_HEAD),
        "b_wqkv": dense(ks[9], (n_b, D_MODEL, (B_HEADS + 2 * B_KV_HEADS) * D_HEAD), D_MODEL),
        "b_wo": dense(ks[10], (n_b, B_HEADS * D_HEAD, D_MODEL), B_HEADS * D_HEAD),
        "b_sink": 0.5 * jax.random.normal(ks[11], (n_b, B_HEADS), f32),
        "c_wqkv": dense(ks[12], (n_c, D_MODEL, 3 * C_HEADS * D_HEAD), D_MODEL),
        "c_wo": dense(ks[13], (n_c, C_HEADS * D_HEAD, D_MODEL), C_HEADS * D_HEAD),
        "c_rpb": 0.1 * jax.random.normal(ks[14], (n_c, C_HEADS, 2 * NA_ROWS - 1, 2 * NA_COLS - 1), f32),
    }


def reference(x, p, norm_g, ffn_wi, ffn_wo, ple_proj, ple_gate, a_wqkv, a_wo,
              b_wqkv, b_wo, b_sink, c_wqkv, c_wo, c_rpb):
    cos, sin = rope_tables(x.shape[1])
    h = x
    for i in range(DEPTH):
        g = norm_g[i]
        h = h + 0.5 * rms_norm(swiglu(rms_norm(h, g[0]), ffn_wi[i, 0], ffn_wo[i, 0]), g[1])
        hn = rms_norm(h, g[2])
        mixer, j = i % N_MIXERS, i // N_MIXERS
        if mixer == 0:
            y = dilated_mixer(hn, a_wqkv[j], a_wo[j], cos, sin)
        elif mixer == 1:
            y = window_gqa_mixer(hn, b_wqkv[j], b_wo[j], b_sink[j], cos, sin)
        else:
            y = neighbourhood_mixer(hn, c_wqkv[j], c_wo[j], c_rpb[j])
        h = h + rms_norm(y, g[3])
        h = h + 0.5 * rms_norm(swiglu(rms_norm(h, g[4]), ffn_wi[i, 1], ffn_wo[i, 1]), g[5])
        e = p[i].astype(h.dtype) @ ple_proj[i]
        gate = jax.nn.sigmoid(rms_norm(h, g[6]) @ ple_gate[i])
        h = h + rms_norm(e * gate, g[7])
    return h
```

```python
import math
import numpy as np
import concourse.bass as bass
import concourse.mybir as mybir
from concourse.bass_utils import run_bass_kernel_spmd

F32 = mybir.dt.float32
BF16 = mybir.dt.bfloat16
ALU = mybir.AluOpType
AF = mybir.ActivationFunctionType

D = 1024
DFF = 2816
NJ = DFF // 128
PLE = 256
NEG = -30000.0
ENGS = ["pe", "act", "dve", "pool", "sp"]


class Buf:
    __slots__ = ("name", "excl")

    def __init__(self, name, excl=False):
        self.name = name
        self.excl = excl


class Sched:
    NEPOCH = 5

    def __init__(self):
        self.ops = {e: [] for e in ENGS}
        self.semval = {}
        self.waited = {e: {} for e in ENGS}
        self.lastw = {}
        self.readers = {}
        self.dram_rr = {}
        self.epoch = 0

    def ekey(self, eng):
        return "%s@%d" % (eng, (self.epoch % self.NEPOCH) if eng == "pe" else 0)

    def _deps(self, eng, reads, writes):
        toks = []
        for b in reads:
            t = self.lastw.get(b)
            if t is not None:
                toks.append(t)
            if b.excl:
                toks.extend(tk for tk in self.readers.get(b, ()) if tk[0].split("@")[0] != eng)
        for b in writes:
            t = self.lastw.get(b)
            if t is not None:
                toks.append(t)
            toks.extend(self.readers.get(b, ()))
        need = {}
        for (k, v) in toks:
            if eng == "pe" and k.startswith("pe@"):
                continue
            if need.get(k, 0) < v:
                need[k] = v
        out = []
        for k, v in need.items():
            if self.waited[eng].get(k, 0) < v:
                self.waited[eng][k] = v
                out.append((k, v))
        return out

    def _commit(self, tok, reads, writes):
        for b in writes:
            self.lastw[b] = tok
            self.readers[b] = []
        for b in reads:
            self.readers.setdefault(b, []).append(tok)

    def op(self, eng, fn, r=(), w=()):
        waits = self._deps(eng, r, w)
        k = self.ekey(eng)
        v = self.semval.get(k, 0) + 1
        self.semval[k] = v
        tok = (k, v)
        self.ops[eng].append((waits, fn, k, 1))
        self._commit(tok, r, w)
        return tok

    def custom(self, eng, fn, r=(), w=(), key=None, inc=1):
        waits = self._deps(eng, r, w)
        prev = self.semval.get(key, 0)
        if prev and self.waited[eng].get(key, 0) < prev:
            self.waited[eng][key] = prev
            waits.append((key, prev))
        self.semval[key] = prev + inc
        tok = (key, prev + inc)
        self.ops[eng].append((waits, fn, key, inc))
        self._commit(tok, r, w)
        return tok

    def dma(self, eng, out_ap, in_ap, r=(), w=(), key=None):
        if key is None:
            key = "d_" + w[0].name
        return self.custom(eng, (lambda e, o=out_ap, i=in_ap: e.dma_start(out=o, in_=i)), r=r, w=w, key=key, inc=16)

    def dram_key(self, name, n=2):
        i = self.dram_rr.get(name, 0)
        self.dram_rr[name] = i + 1
        return "d_%s_%d" % (name, i % n)

    def barrier(self):
        allv = dict(self.semval)
        for e in ENGS:
            waits = []
            for k, v in allv.items():
                if v and self.waited[e].get(k, 0) < v:
                    self.waited[e][k] = v
                    waits.append((k, v))
            if waits:
                self.ops[e].append((waits, None, None, 0))
        self.lastw.clear()
        self.readers.clear()
        self.epoch += 1

    def keys(self):
        return sorted(self.semval.keys())

    def replay(self, eng, handle, sems):
        for (waits, fn, inc_key, inc) in self.ops[eng]:
            for (k, v) in waits:
                handle.wait_ge(sems[k], v)
            if fn is not None:
                ins = fn(handle)
                ins.then_inc(sems[inc_key], inc)


class Arena:
    def __init__(self, t, nwords):
        self.t = t
        self.n = nwords
        self.off = 0

    def reset(self):
        self.off = 0

    def f32(self, n):
        a = self.off
        self.off += n
        assert self.off <= self.n, "SBUF arena overflow %d > %d" % (self.off, self.n)
        return self.t[:, a:a + n]

    def bf16(self, n):
        w = (n + 1) // 2
        a = self.off
        self.off += w
        assert self.off <= self.n, "SBUF arena overflow %d > %d" % (self.off, self.n)
        return self.t[:, a:a + w].bitcast(BF16)


class Cfg:
    def __init__(self, seq=16384, depth=4):
        self.SEQ = seq
        self.DEPTH = depth
        self.TOK = seq * 2 // 8
        self.NT = self.TOK // 512
        self.ROWS = self.TOK // 64


MIXER_OF = lambda l: l % 3


class Prog:
    def __init__(self, cfg, dbg=None):
        self.cfg = cfg
        self.dbg = dbg or []
        self.S = Sched()
        self.pb = [Buf("bank%d" % i, excl=True) for i in range(8)]
        self.nc = bass.Bass("TRN2", target_bir_lowering=False)
        self.dram = {}

    def dt(self, name, shape, dtype, kind="Internal"):
        if name in self.dbg and kind == "Internal":
            kind = "ExternalOutput"
        t = self.nc.dram_tensor(name, list(shape), dtype, kind=kind).ap()
        self.dram[name] = t
        return t

    def declare(self):
        c = self.cfg
        T = c.TOK
        L = c.DEPTH
        n_a = len(range(0, L, 3)); n_b = len(range(1, L, 3)); n_c = len(range(2, L, 3))
        self.x_in = self.dt("x", [T, D], F32, "ExternalInput")
        self.p_all = self.dt("p", [L, T, PLE], F32, "ExternalInput")
        self.norm_g_all = self.dt("norm_g", [L, 8, D], F32, "ExternalInput")
        self.rope_c = self.dt("rope_c", [128, T], F32, "ExternalInput")
        self.rope_s = self.dt("rope_s", [128, T], F32, "ExternalInput")
        self.flags = self.dt("flags", [128, 6], F32, "ExternalInput")
        self.masks_all = {0: self.dt("masks_a", [9, 128, 128], F32, "ExternalInput")}
        wspecs = [("ffn_wi", [L, 2, D, 2 * DFF]), ("ffn_wo", [L, 2, DFF, D]), ("ple_proj", [L, PLE, D]), ("ple_gate", [L, D, D]),
                  ("a_wqkv", [n_a, D, 4608]), ("a_wo", [n_a, 512, D])]
        if n_b:
            wspecs += [("b_wqkv", [n_b, D, 1536]), ("b_wo", [n_b, D, D])]
            self.b_sink_all = self.dt("b_sink", [n_b, 16], F32, "ExternalInput")
            self.masks_all[1] = self.dt("masks_b", [2, 128, 128], F32, "ExternalInput")
        if n_c:
            wspecs += [("c_wqkv", [n_c, D, 3072]), ("c_wo", [n_c, D, D])]
            self.c_biasg_all = self.dt("c_biasg", [n_c, 16, 7, 128, 128], F32, "ExternalInput")
            self.masks_all[2] = self.dt("masks_c", [35, 128, 128], F32, "ExternalInput")
        self.w32 = {}
        self.w16_all = {}
        for name, shp in wspecs:
            self.w32[name] = self.dt(name, shp, F32, "ExternalInput")
            if name != "ffn_wi":
                self.w16_all[name] = self.dt(name + "_bf", shp, BF16)
        self.wi_slot = self.dt("ffn_wi_slot", [L, 2, NJ // 2, 128, 4096], BF16)
        self.out_final = self.dt("out", [T, D], F32, "ExternalOutput")
        self.hbuf3 = [self.dt("h_scr%d" % i, [T, D], F32) for i in range(3)]
        self.srcs_all = {}
        for mx in range(3):
            if (mx == 1 and not n_b) or (mx == 2 and not n_c):
                continue
            lst = []
            for (nm, F, H, VW) in self.src_specs(mx):
                q = self.dt("q_" + nm, [F, T], BF16)
                k = self.dt("k_" + nm, [F if mx != 1 else 512, T], BF16)
                v = self.dt("v_" + nm, [T, VW], BF16)
                lst.append((nm, q, k, v, H, VW))
            self.srcs_all[mx] = lst
        mxr = 0
        for mx, lst in self.srcs_all.items():
            u, cl = self.plan_exchange(lst)
            mxr = max(mxr, cl[-1][0] + cl[-1][1])
        self.send_rows = mxr
        self.send = self.dt("x_send", [mxr, 512], BF16)
        self.recv = self.dt("x_recv", [4 * mxr, 512], BF16)

    def set_layer(self, l):
        L = self.cfg.DEPTH
        mx = l % 3; j = l // 3
        self.layer = l
        self.mx = mx
        self.norm_g = self.norm_g_all[l]
        self.p = self.p_all[l]
        self.masks = self.masks_all[mx]
        pre = "abc"[mx]
        self.w16 = {"wqkv": self.w16_all[pre + "_wqkv"][j], "wo": self.w16_all[pre + "_wo"][j],
                    "ple_proj": self.w16_all["ple_proj"][l], "ple_gate": self.w16_all["ple_gate"][l]}
        self.wi_l = self.wi_slot[l]
        self.wo_l = self.w16_all["ffn_wo"][l]
        if mx == 1:
            self.b_sink = self.b_sink_all[j:j + 1, :]
        if mx == 2:
            self.c_biasg = self.c_biasg_all[j]
        self.srcs = self.srcs_all[mx]
        self.h_pre_in = self.x_in if l == 0 else self.hbuf3[0]
        self.h_pre_out = self.hbuf3[1]
        self.h_att_out = self.hbuf3[2]
        self.h_mid_out = self.out_final if l == L - 1 else self.hbuf3[0]
        self.unit, self.colls = self.plan_exchange(self.srcs)

    @staticmethod
    def plan_exchange(srcs, maxrows=1000):
        units = []
        for si, (nm, q, k, v, H, VW) in enumerate(srcs):
            nkc = k.shape[0] // 128
            for side in "LR":
                for c in range(nkc):
                    units.append(((si, "K", side, c), 128 * H))
                for j in range(H // 128):
                    units.append(((si, "V", side, j), 128 * VW))
        unit = {}
        colls = []
        start = 0; cur = 0
        for key, n in units:
            assert n % 512 == 0
            r = n // 512
            if cur + r > maxrows:
                colls.append((start, cur)); start += cur; cur = 0
            unit[key] = (len(colls), cur * 512, n)
            cur += r
        colls.append((start, cur))
        return unit, colls

    def src_specs(self, mx):
        if mx == 0:
            return [("g0", 512, 128, 528), ("g1", 512, 256, 528), ("g2", 512, 1024, 528)]
        if mx == 1:
            return [("b", 1024, 128, 264)]
        return [("c", 1024, 256, 1056)]

    def nmask(self):
        return (9, 2, 35)[self.mx]

    def build(self):
        nc = self.nc
        c = self.cfg
        self.declare()
        NW = 48 * 1024 - 64
        with (
            nc.sbuf_tensor("arena", [128, NW], F32) as arena_t,
            nc.sbuf_tensor("consts", [128, 2048], F32) as consts_t,
            nc.psum_tensor("psum", [128, 4096], F32) as psum_t,
        ):
            self.A = Arena(arena_t, NW)
            self.CA = Arena(consts_t, 2048)
            self.ps = psum_t
            self.emit_all()
            keys = self.S.keys()
            print("semaphores:", len(keys))
            assert len(keys) < 140, "too many semaphores: %d" % len(keys)
            sem_cms = [nc.semaphore("s_" + k) for k in keys]
            handles = [cm.__enter__() for cm in sem_cms]
            sems = dict(zip(keys, handles))
            try:
                with nc.Block() as block:
                    @block.tensor
                    def _(e):
                        self.S.replay("pe", e, sems)

                    @block.scalar
                    def _(e):
                        self.S.replay("act", e, sems)

                    @block.vector
                    def _(e):
                        self.S.replay("dve", e, sems)

                    @block.gpsimd
                    def _(e):
                        self.S.replay("pool", e, sems)

                    @block.sync
                    def _(e):
                        self.S.replay("sp", e, sems)
            finally:
                for cm in reversed(sem_cms):
                    cm.__exit__(None, None, None)
        return nc

    def bank(self, i, n=512, dtype=F32):
        a = self.ps[:, i * 512:i * 512 + n]
        return a

    def emit_consts(self):
        S = self.S
        CA = self.CA
        self.ident = CA.bf16(128)
        self.rmat = CA.bf16(128)
        rtmp = CA.bf16(128)
        self.maskLG = CA.bf16(384)
        self.eps = CA.f32(1)
        self.ones65 = None
        self.zt = CA.bf16(1024)
        b = Buf("consts")
        self.cb = b
        ident, rmat, mk = self.ident, self.rmat, self.maskLG
        S.op("pool", lambda e: e.memset(ident, 0.0), w=[b])
        S.op("pool", lambda e: e.affine_select(out=ident, in_=ident, pattern=[[-1, 128]], compare_op=ALU.not_equal,
                                               fill=1.0, base=0, channel_multiplier=1), r=[b], w=[b])
        S.op("pool", lambda e: e.memset(rmat, 0.0), w=[b])
        S.op("pool", lambda e: e.affine_select(out=rmat, in_=rmat, pattern=[[-1, 128]], compare_op=ALU.not_equal,
                                               fill=1.0, base=-8, channel_multiplier=1), r=[b], w=[b])
        for (a0, a1) in ((8, 64), (72, 128)):
            S.op("pool", lambda e, a0=a0, a1=a1: e.memset(rmat[:, a0:a1], 0.0), r=[b], w=[b])
        S.op("pool", lambda e: e.memset(rtmp, 0.0), w=[b])
        S.op("pool", lambda e: e.affine_select(out=rtmp, in_=rtmp, pattern=[[-1, 128]], compare_op=ALU.not_equal,
                                               fill=1.0, base=8, channel_multiplier=1), r=[b], w=[b])
        for (a0, a1) in ((0, 8), (16, 72), (80, 128)):
            S.op("pool", lambda e, a0=a0, a1=a1: e.memset(rtmp[:, a0:a1], 0.0), r=[b], w=[b])
        S.op("pool", lambda e: e.tensor_tensor(out=rmat, in0=rmat, in1=rtmp, op=ALU.add), r=[b], w=[b])
        S.op("pool", lambda e: e.memset(mk, 0.0), w=[b])
        S.op("pool", lambda e: e.affine_select(out=mk[:, 0:128], in_=mk[:, 0:128], pattern=[[1, 128]], compare_op=ALU.is_ge,
                                               fill=NEG, base=0, channel_multiplier=-1), r=[b], w=[b])
        S.op("pool", lambda e: e.affine_select(out=mk[:, 256:384], in_=mk[:, 256:384], pattern=[[-1, 128]], compare_op=ALU.is_ge,
                                               fill=NEG, base=0, channel_multiplier=1), r=[b], w=[b])
        S.op("pool", lambda e: e.memset(self.eps, 1e-6), w=[b])
        S.op("pool", lambda e: e.memset(self.zt, 0.0), w=[b])

    def conv_jobs(self, l):
        mx = l % 3; j = l // 3
        jobs = []
        for f in range(2):
            src = self.w32["ffn_wi"][l, f].rearrange("(k p) (two n) -> p k two n", p=128, two=2)
            dst = self.wi_slot[l, f]
            for jp in range(NJ // 2):
                d3 = dst[jp].rearrange("p (k two c) -> p k two c", k=8, two=2)
                for two in range(2):
                    jobs.append((d3[:, :, two, :], src[:, :, two, jp * 256:(jp + 1) * 256]))
        def plain(dst, src, step=256):
            K = src.shape[0]
            for k0 in range(0, K, step):
                k1 = min(K, k0 + step)
                jobs.append((dst[k0:k1, :], src[k0:k1, :]))
        for f in range(2):
            plain(self.w16_all["ffn_wo"][l, f], self.w32["ffn_wo"][l, f])
        pre = "abc"[mx]
        plain(self.w16_all[pre + "_wqkv"][j], self.w32[pre + "_wqkv"][j])
        plain(self.w16_all[pre + "_wo"][j], self.w32[pre + "_wo"][j])
        plain(self.w16_all["ple_proj"][l], self.w32["ple_proj"][l])
        plain(self.w16_all["ple_gate"][l], self.w32["ple_gate"][l])
        return jobs

    def emit_conv(self, jobs):
        S = self.S
        for (dst, src) in jobs:
            S.dma("pool", dst, src, w=[self.wbuf], key=S.dram_key("w16", 4))

    def load_gvec(self, l, idx, dst, buf):
        src = self.norm_g[idx:idx + 1, :].partition_broadcast(128)
        self.S.dma("sp", dst, src, w=[buf])

    def pre_norm_T(self, src_ap, src_buf, gt, gbuf, xT, xTbuf, col0, R):
        S = self.S
        ss = R["ss"]; ssb = R["ssb"]
        i = R["ni"] = R.get("ni", 0) + 1
        xn = R["xn"][i % 2]; xnb = R["xnb"][i % 2]
        junk = R["junk"]; jb = R["junkb"]
        tb = R["tbank"][i % 2]; tbb = R["tbankb"][i % 2]
        eps = self.eps
        S.op("act", lambda e: e.activation(out=junk, in_=src_ap, func=AF.Square, accum_out=ss[:, 0:1]), r=[src_buf], w=[jb, ssb])
        S.op("act", lambda e: e.activation(out=ss[:, 1:2], in_=ss[:, 0:1], func=AF.Sqrt, scale=1.0 / D, bias=eps), r=[ssb], w=[ssb])
        S.op("dve", lambda e: e.reciprocal(out=ss[:, 2:3], in_=ss[:, 1:2]), r=[ssb], w=[ssb])
        S.op("dve", lambda e: e.scalar_tensor_tensor(out=xn, in0=src_ap, scalar=ss[:, 2:3], in1=gt, op0=ALU.mult, op1=ALU.mult),
             r=[src_buf, ssb, gbuf], w=[xnb])
        tv = tb.bitcast(BF16)
        for k in range(8):
            S.op("pe", lambda e, k=k: e.transpose(out=tv[:, k * 128:(k + 1) * 128], in_=xn[:, k * 128:(k + 1) * 128], identity=self.ident),
                 r=[xnb, self.cb], w=[tbb])
        S.op("act", lambda e: e.copy(out=xT[:, :, col0:col0 + 128], in_=tv.rearrange("p (k t) -> p k t", k=8)), r=[tbb], w=[xTbuf])

    def post_norm_res(self, y_ap, ybuf, gt, gbuf, coef, h_ap, hbuf, R):
        S = self.S
        ybl = ybuf if isinstance(ybuf, list) else [ybuf]
        ss = R["ss2"]; ssb = R["ss2b"]
        junk = R["junk"]; jb = R["junkb"]
        tmp = R["tmp"]; tmpb = R["tmpb"]
        eps = self.eps
        S.op("act", lambda e: e.activation(out=junk, in_=y_ap, func=AF.Square, accum_out=ss[:, 0:1]), r=ybl, w=[jb, ssb])
        S.op("act", lambda e: e.activation(out=ss[:, 1:2], in_=ss[:, 0:1], func=AF.Sqrt, scale=1.0 / D, bias=eps), r=[ssb], w=[ssb])
        S.op("dve", lambda e: e.reciprocal(out=ss[:, 2:3], in_=ss[:, 1:2]), r=[ssb], w=[ssb])
        S.op("dve", lambda e: e.scalar_tensor_tensor(out=tmp, in0=y_ap, scalar=ss[:, 2:3], in1=gt, op0=ALU.mult, op1=ALU.mult),
             r=ybl + [ssb, gbuf], w=[tmpb])
        S.op("pool", lambda e: e.tensor_tensor(out=h_ap, in0=tmp, in1=h_ap, op=ALU.add), r=[tmpb, hbuf], w=[hbuf])

    def alloc_common(self):
        A = self.A
        R = {}
        R["ss"] = A.f32(4); R["ssb"] = Buf("ss")
        R["ss2"] = A.f32(4); R["ss2b"] = Buf("ss2")
        R["xn"] = [A.bf16(1024), A.bf16(1024)]; R["xnb"] = [Buf("xn0"), Buf("xn1")]
        R["junk"] = A.bf16(1024); R["junkb"] = Buf("junk")
        R["tmp"] = A.f32(1024); R["tmpb"] = Buf("tmp")
        R["tbank"] = [self.bank(6), self.bank(7)]; R["tbankb"] = [self.pb[6], self.pb[7]]
        return R

    def alloc_ffn(self):
        A = self.A
        F = {}
        F["xT"] = A.bf16(8 * 512).rearrange("p (k t) -> p k t", k=8); F["xTb"] = Buf("xT")
        F["wi"] = [A.bf16(8 * 512).rearrange("p (k two c) -> p k two c", k=8, two=2) for _ in range(3)]
        F["wib"] = [Buf("wi%d" % i) for i in range(3)]
        F["act"] = A.bf16(NJ * 512).rearrange("p (j t) -> p j t", j=NJ); F["actb"] = Buf("act")
        F["wo"] = A.bf16(NJ * 1024).rearrange("p (j n) -> p j n", j=NJ); F["wob"] = [Buf("wo%d" % j) for j in range(NJ)]
        F["sg"] = [A.f32(512), A.f32(512)]; F["sgb"] = [Buf("sg0"), Buf("sg1")]
        F["gpre"] = A.f32(1024); F["gpreb"] = Buf("gpre")
        F["gpost"] = A.f32(1024); F["gpostb"] = Buf("gpost")
        F["wi_i"] = 0
        F["gu_i"] = 0
        return F

    def ffn_load_resident(self, F, l, f, gi_pre, gi_post):
        S = self.S
        wo16 = self.wo_l[f]
        for j in range(NJ):
            S.dma("sp", F["wo"][:, j, :], wo16[j * 128:(j + 1) * 128, :], r=[self.wbuf], w=[F["wob"][j]], key="d_wores%d" % (j % 4))
        self.load_gvec(l, gi_pre, F["gpre"], F["gpreb"])
        self.load_gvec(l, gi_post, F["gpost"], F["gpostb"])
        self.scale_gvec(F["gpost"], F["gpostb"], 0.5)

    def ffn_tile(self, F, R, l, f, ht, hbuf):
        S = self.S
        xT = F["xT"]; xTb = F["xTb"]
        for s in range(4):
            self.pre_norm_T(ht[:, s, :], hbuf, F["gpre"], F["gpreb"], xT, xTb, s * 128, R)
        act = F["act"]; actb = F["actb"]
        for jp in range(NJ // 2):
            wi_i = F["wi_i"]; F["wi_i"] += 1
            ws = F["wi"][wi_i % 3]; wsb = F["wib"][wi_i % 3]
            S.dma("sp", ws.rearrange("p k two c -> p (k two c)"), self.wi_l[f][jp], r=[self.wbuf], w=[wsb])
            for jj in range(2):
                j = jp * 2 + jj
                gi = F["gu_i"]; F["gu_i"] += 1
                pg = self.bank((gi % 2) * 2); pu = self.bank((gi % 2) * 2 + 1)
                pgb = self.pb[(gi % 2) * 2]
                pub = self.pb[(gi % 2) * 2 + 1]
                sg = F["sg"][gi % 2]; sgb = F["sgb"][gi % 2]
                for k in range(8):
                    S.op("pe", lambda e, k=k, ws=ws, jj=jj, pg=pg: e.matmul(pg, lhsT=ws[:, k, 0, jj * 128:(jj + 1) * 128], rhs=xT[:, k, :],
                                                                      start=(k == 0), stop=(k == 7)), r=[wsb, xTb], w=[pgb])
                for k in range(8):
                    S.op("pe", lambda e, k=k, ws=ws, jj=jj, pu=pu: e.matmul(pu, lhsT=ws[:, k, 1, jj * 128:(jj + 1) * 128], rhs=xT[:, k, :],
                                                                      start=(k == 0), stop=(k == 7)), r=[wsb, xTb], w=[pub])
                S.op("act", lambda e, sg=sg, pg=pg: e.activation(out=sg, in_=pg, func=AF.Silu), r=[pgb], w=[sgb])
                S.op("dve", lambda e, sg=sg, pu=pu, j=j: e.tensor_tensor(out=act[:, j, :], in0=sg, in1=pu, op=ALU.mult), r=[sgb, pub], w=[actb])
        for s in range(4):
            b0 = (4, 0, 2)[s % 3]
            y = self.ps[:, b0 * 512:(b0 + 2) * 512]
            for dh in range(2):
                for j in range(NJ):
                    S.op("pe", lambda e, j=j, s=s, dh=dh, b0=b0: e.matmul(self.ps[:, (b0 + dh) * 512:(b0 + dh + 1) * 512], lhsT=act[:, j, s * 128:(s + 1) * 128],
                                                                        rhs=F["wo"][:, j, dh * 512:(dh + 1) * 512], start=(j == 0), stop=(j == NJ - 1)),
                         r=[actb, F["wob"][j]], w=[self.pb[b0 + dh]])
            self.post_norm_res(y, [self.pb[b0], self.pb[b0 + 1]], F["gpost"], F["gpostb"], 0.5, ht[:, s, :], hbuf, R)

    def qkv_jobs(self):
        mx = self.mx
        W = self.w16["wqkv"].rearrange("(k p) n -> p k n", p=128)
        jobs = []
        vjobs = []
        if mx == 0:
            for g in range(3):
                nm, q, k, v, H, VW = self.srcs[g]
                jobs.append(([(0, g * 512, 512)], [(c * 128, q[c * 128:(c + 1) * 128, :]) for c in range(4)], True))
                jobs.append(([(0, 1536 + g * 512, 512)], [(c * 128, k[c * 128:(c + 1) * 128, :]) for c in range(4)], True))
                vjobs.append((3072 + g * 512, 512, v, 0, 8))
        elif mx == 1:
            nm, q, k, v, H, VW = self.srcs[0]
            for hf in range(2):
                jobs.append(([(0, hf * 512, 512)], [(c * 128, q[(hf * 4 + c) * 128:(hf * 4 + c + 1) * 128, :]) for c in range(4)], True))
            pieces = []
            for kv in range(4):
                pieces += [(kv * 128, 1024 + kv * 64, 64), (kv * 128 + 64, 1024 + kv * 64, 64)]
            jobs.append((pieces, [(kv * 128, k[kv * 128:(kv + 1) * 128, :]) for kv in range(4)], True))
            vjobs.append((1280, 256, v, 0, 4))
        else:
            nm, q, k, v, H, VW = self.srcs[0]
            for hf in range(2):
                jobs.append(([(0, hf * 512, 512)], [(c * 128, q[(hf * 4 + c) * 128:(hf * 4 + c + 1) * 128, :]) for c in range(4)], False))
            for hf in range(2):
                jobs.append(([(0, 1024 + hf * 512, 512)], [(c * 128, k[(hf * 4 + c) * 128:(hf * 4 + c + 1) * 128, :]) for c in range(4)], False))
            for vb in range(2):
                vjobs.append((2048 + vb * 512, 512, v, vb * 528, 8))
        return W, jobs, vjobs

    def phase_pre(self):
        S = self.S; A = self.A
        A.reset()
        R = self.alloc_common()
        F = self.alloc_ffn()
        hts = [A.f32(4 * 1024).rearrange("p (s d) -> p s d", s=4) for _ in range(2)]
        hbs = [Buf("ht0"), Buf("ht1")]
        g2 = A.f32(1024); g2b = Buf("g2")
        wq = [A.bf16(8 * 512).rearrange("p (k n) -> p k n", k=8) for _ in range(2)]; wqb = [Buf("wq0"), Buf("wq1")]
        ct = A.f32(512); st = A.f32(512); ctb = Buf("ct"); stb = Buf("st")
        qb = [A.bf16(512), A.bf16(512)]; qbb = [Buf("qb0"), Buf("qb1")]
        t1 = A.f32(512); t2 = A.f32(512); t1b = Buf("t1"); t2b = Buf("t2")
        qr = [A.bf16(512), A.bf16(512)]; qrb = [Buf("qr0"), Buf("qr1")]
        vsf = [A.bf16(8 * 66) for _ in range(2)]
        vs = [v_.rearrange("p (h d) -> p h d", h=8) for v_ in vsf]; vsb = [Buf("vs0"), Buf("vs1")]
        for i in range(2):
            S.op("dve", lambda e, i=i: e.memset(vsf[i], 1.0), w=[vsb[i]])
        self.ffn_load_resident(F, 0, 0, 0, 1)
        self.load_gvec(0, 2, g2, g2b)
        W, jobs, vjobs = self.qkv_jobs()
        od = Buf("out_dram")
        cnt = {"w": 0, "c": 0, "v": 0}
        for t in range(self.cfg.NT):
            ht = hts[t % 2]; hb = hbs[t % 2]
            tok = slice(t * 512, (t + 1) * 512)
            S.dma("sp", ht, self.h_pre_in[tok, :].rearrange("(s p) d -> p s d", p=128), w=[hb])
            self.ffn_tile(F, R, 0, 0, ht, hb)
            S.dma("pool", self.h_pre_out[tok, :].rearrange("(s p) d -> p s d", p=128), ht, r=[hb], w=[od], key=S.dram_key("h"))
            xT = F["xT"]; xTb = F["xTb"]
            for s4 in range(4):
                self.pre_norm_T(ht[:, s4, :], hb, g2, g2b, xT, xTb, s4 * 128, R)
            if self.mx != 2:
                S.dma("sp", ct, self.rope_c[:, tok], w=[ctb])
                S.dma("sp", st, self.rope_s[:, tok], w=[stb])
            import os
            for (pieces, chunks, rope) in (jobs if not os.environ.get('SKIP_QK') else []):
                wi_ = cnt["w"]; cnt["w"] += 1
                ws = wq[wi_ % 2]; wsb = wqb[wi_ % 2]
                for (sc, src, n) in pieces:
                    S.dma("sp", ws[:, :, sc:sc + n], W[:, :, src:src + n], r=[self.wbuf], w=[wsb])
                for (sc, dest) in chunks:
                    ci = cnt["c"]; cnt["c"] += 1
                    pq = self.bank(ci % 2); pqb = self.pb[ci % 2]
                    for k in range(8):
                        S.op("pe", lambda e, k=k, ws=ws, sc=sc, pq=pq: e.matmul(pq, lhsT=ws[:, k, sc:sc + 128], rhs=xT[:, k, :], start=(k == 0), stop=(k == 7)),
                             r=[wsb, xTb], w=[pqb])
                    q1 = qr[ci % 2]; q1b = qrb[ci % 2]
                    if rope and not os.environ.get('SKIP_ROPE'):
                        b1 = qb[ci % 2]; b1b = qbb[ci % 2]
                        pr = self.bank(2 + ci % 2); prb = self.pb[2 + ci % 2]
                        S.op("act", lambda e, b1=b1, pq=pq: e.copy(out=b1, in_=pq), r=[pqb], w=[b1b])
                        if not os.environ.get('SKIP_RM'):
                            S.op("pe", lambda e, b1=b1, pr=pr: e.matmul(pr, lhsT=self.rmat, rhs=b1, start=True, stop=True), r=[b1b, self.cb], w=[prb])
                        S.op("dve", lambda e, pq=pq: e.tensor_tensor(out=t1, in0=ct, in1=pq, op=ALU.mult), r=[pqb, ctb], w=[t1b])
                        S.op("dve", lambda e, pr=pr: e.tensor_tensor(out=t2, in0=st, in1=pr, op=ALU.mult), r=[prb, stb], w=[t2b])
                        S.op("dve", lambda e, q1=q1: e.tensor_tensor(out=q1, in0=t1, in1=t2, op=ALU.add), r=[t1b, t2b], w=[q1b])
                    else:
                        S.op("act", lambda e, q1=q1, pq=pq: e.copy(out=q1, in_=pq), r=[pqb], w=[q1b])
                    S.dma("pool", dest[:, tok], q1, r=[q1b], w=[od], key=S.dram_key("qk", 3))
            for (src, ncols, vdst, dc0, nh) in (vjobs if not os.environ.get('SKIP_V') else []):
                wi_ = cnt["w"]; cnt["w"] += 1
                ws = wq[wi_ % 2]; wsb = wqb[wi_ % 2]
                S.dma("sp", ws[:, :, 0:ncols], W[:, :, src:src + ncols], r=[self.wbuf], w=[wsb])
                for s4 in range(4):
                    vi = cnt["v"]; cnt["v"] += 1
                    pv = self.bank(4 + vi % 2)[:, 0:ncols]; pvb = self.pb[4 + vi % 2]
                    for k in range(8):
                        S.op("pe", lambda e, k=k, ws=ws, pv=pv, s4=s4, ncols=ncols: e.matmul(pv, lhsT=xT[:, k, s4 * 128:(s4 + 1) * 128], rhs=ws[:, k, 0:ncols],
                                                                                    start=(k == 0), stop=(k == 7)), r=[wsb, xTb], w=[pvb])
                    v1 = vs[vi % 2]; v1b = vsb[vi % 2]
                    S.op("act", lambda e, v1=v1, pv=pv, nh=nh: e.copy(out=v1[:, 0:nh, 0:64], in_=pv.rearrange("p (h d) -> p h d", h=nh)), r=[pvb], w=[v1b])
                    r0 = t * 512 + s4 * 128
                    S.dma("pool", vdst[r0:r0 + 128, dc0:dc0 + nh * 66], v1[:, 0:nh, :].rearrange("p h d -> p (h d)"), r=[v1b], w=[od], key=S.dram_key("v"))
            if self.pending_conv:
                nper = -(-len(self.pending_conv) // max(1, self.cfg.NT - t))
                self.emit_conv(self.pending_conv[:nper])
                self.pending_conv = self.pending_conv[nper:]
        S.barrier()

    def head_plan(self, i, NQ):
        mx = self.mx
        M = self.msk
        groups = []
        if mx == 0:
            dl = [[(-1, 0), (0, 1), (1, 2)], [(-2, 3), (-1, 4), (0, 4), (1, 4), (2, 5)], [(-8, 6)] + [(d, 7) for d in range(-7, 8)] + [(8, 8)]]
            for hg in range(2):
                heads = []
                for h in range(hg * 4, hg * 4 + 4):
                    tl = []
                    for g in range(3):
                        for (d, m) in dl[g]:
                            tl.append((g, h // 2, h % 2, h // 2, h * 66, d, [M[:, m, :]]))
                    heads.append((h, tl))
                groups.append(heads)
        elif mx == 1:
            for hg in range(4):
                heads = []
                for h in range(hg * 4, hg * 4 + 4):
                    tl = [(0, h // 2, h % 2, h // 4, (h // 4) * 66, -1, [M[:, 0, :]]), (0, h // 2, h % 2, h // 4, (h // 4) * 66, 0, []),
                          (0, h // 2, h % 2, h // 4, (h // 4) * 66, 1, [M[:, 1, :]])]
                    heads.append((h, tl))
                groups.append(heads)
        else:
            cls = 0
            if i == 0: cls = 1
            elif i == 1: cls = 2
            elif i == NQ - 2: cls = 3
            elif i == NQ - 1: cls = 4
            ds = list(range(-2, 3))
            if i == 0: ds.append(3)
            if i == NQ - 1: ds = [-3] + ds
            for hg in range(4):
                heads = []
                for h in range(hg * 4, hg * 4 + 4):
                    tl = [(0, h // 2, h % 2, h // 2, h * 66, d, [self.cbias[:, h * 7 + d + 3, :], M[:, cls * 7 + d + 3, :]]) for d in ds]
                    heads.append((h, tl))
                groups.append(heads)
        return groups

    def phase_att(self):
        S = self.S; A = self.A
        A.reset()
        R = self.alloc_common()
        T = self.cfg.TOK
        NQ = T // 128
        mx = self.mx
        NM = self.nmask()
        self.msk = A.bf16(NM * 128).rearrange("p (m q) -> p m q", m=NM); mb = Buf("msk")
        for m0 in range(0, NM, 8):
            m1 = min(NM, m0 + 8)
            S.dma("pool", self.msk[:, m0:m1, :], self.masks[m0:m1].rearrange("m p q -> p m q"), w=[mb])
        cbb = Buf("cbias")
        if mx == 2:
            self.cbias = A.bf16(112 * 128).rearrange("p (m q) -> p m q", m=112)
            cbg = self.c_biasg.rearrange("h d p q -> p (h d) q")
            for m0 in range(0, 112, 8):
                S.dma("pool", self.cbias[:, m0:m0 + 8, :], cbg[:, m0:m0 + 8, :], w=[cbb])
            S.op("dve", lambda e, cbt=self.cbias: e.tensor_scalar(out=cbt, in0=cbt, scalar1=8.0, scalar2=None, op0=ALU.mult), r=[cbb], w=[cbb])
        Fo = (512, 1024, 1024)[mx]
        nco = Fo // 128
        wo = A.bf16(nco * 1024).rearrange("p (c n) -> p c n", c=nco); wob = Buf("wo_mix")
        S.dma("sp", wo, self.w16["wo"].rearrange("(c p) n -> p c n", p=128), r=[self.wbuf], w=[wob])
        g3 = A.f32(1024); g3b = Buf("g3")
        self.load_gvec(0, 3, g3, g3b)
        es = None
        if mx == 1:
            es = A.f32(16); esb = Buf("esink")
            S.dma("sp", es, self.b_sink.partition_broadcast(128), w=[esb])
            S.op("act", lambda e: e.activation(out=es, in_=es, func=AF.Exp), r=[esb], w=[esb])
        fl = A.f32(6); flb = Buf("flags")
        S.dma("sp", fl, self.flags, w=[flb])
        ckmax = max(k.shape[0] for (nm, q, k, v, H, VW) in self.srcs)
        vwmax = max(VW for (nm, q, k, v, H, VW) in self.srcs)
        candk = [A.bf16(ckmax) for _ in range(3)]; candv = [A.bf16(vwmax) for _ in range(3)]
        candb = [Buf("cand%d" % j) for j in range(3)]
        recv_flat = self.recv.rearrange("r c -> (r c)")
        rings = []
        for si_, (nm, q, k, v, H, VW) in enumerate(self.srcs):
            nkc = k.shape[0] // 128
            nqc = q.shape[0] // 128
            Hb = H // 128
            dmax = Hb if mx != 2 else 2
            rs = 2 * dmax + 1 + 2
            kslots = [A.bf16(nkc * 128).rearrange("p (c t) -> p c t", c=nkc) for _ in range(rs)]
            vslots = [A.bf16(VW) for _ in range(rs)]
            kb = [Buf("kv_%s_%d" % (nm, j)) for j in range(rs)]
            vb = kb
            qs = [A.bf16(nqc * 128).rearrange("p (c t) -> p c t", c=nqc) for _ in range(2)]
            qbf = [Buf("q_%d_%d" % (si_, j)) for j in range(2)]
            rings.append(dict(q=q.rearrange("(c p) t -> p c t", p=128), k=k.rearrange("(c p) t -> p c t", p=128), v=v, Hb=Hb, dmax=dmax, rs=rs,
                              ks=kslots, vs=vslots, kb=kb, vb=vb, qs=qs, qb=qbf, loaded=-1, NKT=(T + 2 * H) // 128,
                              nm=nm, si=si_, H=H, VW=VW, Fk=k.shape[0], nkc=nkc))

        def load_kv(rg, kt, sl):
            Hb_ = rg["Hb"]; H_ = rg["H"]; VW_ = rg["VW"]; Fk_ = rg["Fk"]; nkc_ = rg["nkc"]
            key = "d_ring_%d_%d" % (rg["si"], sl % 5)
            kdst = rg["ks"][sl]; vdst = rg["vs"][sl]; sb_ = rg["kb"][sl]
            if Hb_ <= kt < Hb_ + NQ:
                j = kt - Hb_
                S.dma("sp", kdst, rg["k"][:, :, j * 128:(j + 1) * 128], w=[sb_], key=key)
                S.dma("sp", vdst, rg["v"][j * 128:(j + 1) * 128, :], w=[sb_], key=key)
                return
            left = kt < Hb_
            j = kt if left else kt - Hb_ - NQ
            side = "R" if left else "L"
            blocks = (0, 1, 2) if left else (1, 2, 3)
            f0 = 0 if left else 3
            for ci, bj in enumerate(blocks):
                ck = candk[ci][:, 0:nkc_ * 128].rearrange("p (c t) -> p c t", c=nkc_)
                cv = candv[ci][:, 0:VW_]
                for c in range(nkc_):
                    g, og, n = self.unit[(rg["si"], "K", side, c)]
                    r0, nr = self.colls[g]
                    o = (4 * r0 + bj * nr) * 512 + og
                    S.dma("sp", ck[:, c, :], recv_flat[o:o + n].rearrange("(p t) -> p t", t=H_)[:, j * 128:(j + 1) * 128], w=[candb[ci]])
                g, og, n = self.unit[(rg["si"], "V", side, j)]
                r0, nr = self.colls[g]
                o = (4 * r0 + bj * nr) * 512 + og
                S.dma("sp", cv, recv_flat[o:o + n].rearrange("(t w) -> t w", w=VW_), w=[candb[ci]])
                fcol = fl[:, f0 + ci:f0 + ci + 1]
                if ci == 0:
                    S.op("dve", lambda e, o_=kdst, i=ck, f=fcol: e.tensor_scalar(out=o_, in0=i, scalar1=f, scalar2=None, op0=ALU.mult), r=[candb[ci], flb], w=[sb_])
                    S.op("dve", lambda e, o_=vdst, i=cv, f=fcol: e.tensor_scalar(out=o_, in0=i, scalar1=f, scalar2=None, op0=ALU.mult), r=[candb[ci], flb], w=[sb_])
                else:
                    S.op("dve", lambda e, o_=kdst, i=ck, f=fcol: e.scalar_tensor_tensor(out=o_, in0=i, scalar=f, in1=o_, op0=ALU.mult, op1=ALU.add), r=[candb[ci], flb], w=[sb_])
                    S.op("dve", lambda e, o_=vdst, i=cv, f=fcol: e.scalar_tensor_tensor(out=o_, in0=i, scalar=f, in1=o_, op0=ALU.mult, op1=ALU.add), r=[candb[ci], flb], w=[sb_])

        pts = [A.bf16(512) for _ in range(3)]; ptb = [Buf("pt%d" % j) for j in range(3)]
        ob = [A.bf16(Fo) for _ in range(2)]; obb = [Buf("o%d" % j) for j in range(2)]
        oT = A.bf16(nco * 128).rearrange("p (c t) -> p c t", c=nco); oTb = Buf("oT")
        rd = A.f32(8); rdb = Buf("rd")
        hts = [A.f32(1024) for _ in range(2)]; hbs = [Buf("hq0"), Buf("hq1")]
        hd = Buf("hscr")
        sc_i = 0; pt_i = 0; og_i = 0
        pending_pv = []
        pending_fin = []
        for i in range(NQ):
            ht = hts[i % 2]; hb = hbs[i % 2]
            S.dma("sp", ht, self.h_pre_out[i * 128:(i + 1) * 128, :], w=[hb])
            for rg in rings:
                upto = min(rg["NKT"] - 1, i + rg["Hb"] + rg["dmax"] + 1)
                while rg["loaded"] < upto:
                    kt = rg["loaded"] + 1
                    sl = kt % rg["rs"]
                    load_kv(rg, kt, sl)
                    rg["loaded"] = kt
                S.dma("sp", rg["qs"][i % 2], rg["q"][:, :, i * 128:(i + 1) * 128], w=[rg["qb"][i % 2]])
            o1 = ob[i % 2]; o1b = obb[i % 2]
            for hg_idx, heads in enumerate(self.head_plan(i, NQ)):
                og = og_i; og_i += 1
                orb = self.pb[3 + og % 2]
                oreg = self.bank(3 + og % 2)
                for hh, (h, tl) in enumerate(heads):
                    ocol = hh * 65
                    nt = len(tl)
                    for c0 in range(0, nt, 4):
                        grp = tl[c0:c0 + 4]
                        sb_ = sc_i % 3; sc_i += 1
                        sbank = self.bank(sb_); sbb = self.pb[sb_]
                        reads_kv = []
                        for j, (si, qc, half, kc, vcol, d, mts) in enumerate(grp):
                            rg = rings[si]
                            kt = i + rg["Hb"] + d
                            sl = kt % rg["rs"]
                            hp = slice(half * 64, half * 64 + 64)
                            kap = rg["ks"][sl][hp, kc, :]
                            qap = rg["qs"][i % 2][hp, qc, :]
                            dst = sbank[:, j * 128:(j + 1) * 128]
                            S.op("pe", lambda e, dst=dst, kap=kap, qap=qap, last=(len(mts) == 0): e.matmul(dst, lhsT=kap, rhs=qap, start=True, stop=last),
                                 r=[rg["kb"][sl], rg["qb"][i % 2]], w=[sbb])
                            for mi, mt in enumerate(mts):
                                S.op("pe", lambda e, dst=dst, mt=mt, last=(mi == len(mts) - 1): e.matmul(dst, lhsT=self.ident, rhs=mt, start=False, stop=last),
                                     r=[mb, cbb, self.cb], w=[sbb])
                        n = len(grp)
                        pi = pt_i % 3; pt_i += 1
                        pt = pts[pi]; ptbuf = ptb[pi]
                        S.op("act", lambda e, pt=pt, sbank=sbank, n=n: e.activation(out=pt[:, 0:n * 128], in_=sbank[:, 0:n * 128], func=AF.Exp, scale=0.125),
                             r=[sbb], w=[ptbuf])
                        def emit_pv(grp=grp, pt=pt, ptbuf=ptbuf, c0=c0, nt=nt, oreg=oreg, ocol=ocol, orb=orb):
                            for j, (si, qc, half, kc, vcol, d, mts) in enumerate(grp):
                                rg = rings[si]
                                kt = i + rg["Hb"] + d
                                sl = kt % rg["rs"]
                                first = (c0 + j == 0); last = (c0 + j == nt - 1)
                                S.op("pe", lambda e, pt=pt, j=j, vap=rg["vs"][sl][:, vcol:vcol + 65], oreg=oreg, ocol=ocol, first=first, last=last:
                                     e.matmul(oreg[:, ocol:ocol + 65], lhsT=pt[:, j * 128:(j + 1) * 128], rhs=vap, start=first, stop=last),
                                     r=[ptbuf, rg["vb"][sl]], w=[orb])
                        if pending_pv:
                            pending_pv.pop()()
                        pending_pv.append(emit_pv)
                if pending_pv:
                    pending_pv.pop()()
                nh = len(heads)
                den = oreg[:, 0:nh * 65].rearrange("p (h d) -> p h d", h=nh)[:, :, 64]
                if es is not None:
                    h0 = heads[0][0]
                    S.op("dve", lambda e, den=den, h0=h0, nh=nh: e.tensor_tensor(out=rd[:, 0:nh], in0=den, in1=es[:, h0:h0 + nh], op=ALU.add), r=[orb, esb], w=[rdb])
                    S.op("dve", lambda e, nh=nh: e.reciprocal(out=rd[:, 0:nh], in_=rd[:, 0:nh]), r=[rdb], w=[rdb])
                else:
                    S.op("dve", lambda e, den=den, nh=nh: e.reciprocal(out=rd[:, 0:nh], in_=den), r=[orb], w=[rdb])
                for hh, (h, tl) in enumerate(heads):
                    S.op("dve", lambda e, hh=hh, h=h, oreg=oreg, o1=o1: e.tensor_scalar(out=o1[:, h * 64:(h + 1) * 64], in0=oreg[:, hh * 65:hh * 65 + 64],
                                                                                 scalar1=rd[:, hh:hh + 1], scalar2=None, op0=ALU.mult), r=[orb, rdb], w=[o1b])
                if pending_fin and hg_idx < 2:
                    pending_fin.pop(0)()
            def fin_a(o1=o1, o1b=o1b):
                tv = self.bank(5).bitcast(BF16)
                for c in range(nco):
                    S.op("pe", lambda e, c=c, o1=o1, tv=tv: e.transpose(out=tv[:, c * 128:(c + 1) * 128], in_=o1[:, c * 128:(c + 1) * 128], identity=self.ident),
                         r=[o1b, self.cb], w=[self.pb[5]])
                S.op("act", lambda e, tv=tv: e.copy(out=oT, in_=tv[:, 0:nco * 128].rearrange("p (c t) -> p c t", c=nco)), r=[self.pb[5]], w=[oTb])

            def fin_b(i=i, ht=ht, hb=hb):
                for dh in range(2):
                    for c in range(nco):
                        S.op("pe", lambda e, c=c, dh=dh: e.matmul(self.bank(6 + dh), lhsT=oT[:, c, :], rhs=wo[:, c, dh * 512:(dh + 1) * 512], start=(c == 0), stop=(c == nco - 1)),
                             r=[oTb, wob], w=[self.pb[6 + dh]])
                self.post_norm_res(self.ps[:, 6 * 512:8 * 512], [self.pb[6], self.pb[7]], g3, g3b, 1.0, ht, hb, R)
                S.dma("pool", self.h_att_out[i * 128:(i + 1) * 128, :], ht, r=[hb], w=[hd], key=S.dram_key("hs"))
            pending_fin.extend([fin_a, fin_b])
        while pending_fin:
            pending_fin.pop(0)()
        S.barrier()

    def phase_mid(self):
        S = self.S; A = self.A
        A.reset()
        R = self.alloc_common()
        F = self.alloc_ffn()
        hts = [A.f32(4 * 1024).rearrange("p (s d) -> p s d", s=4) for _ in range(2)]
        hbs = [Buf("ht0"), Buf("ht1")]
        wg = A.bf16(8 * 1024).rearrange("p (k n) -> p k n", k=8); wgb = Buf("wgate")
        wp = A.bf16(2 * 1024).rearrange("p (k n) -> p k n", k=2); wpb = Buf("wproj")
        g6 = A.f32(1024); g6b = Buf("g6"); g7 = A.f32(1024); g7b = Buf("g7")
        sgt = A.f32(1024); sgtb = Buf("sgt")
        pt_ = A.f32(256); ptb_ = Buf("ptile")
        pbf = A.bf16(256); pbfb = Buf("pbf")
        pT = A.bf16(256).rearrange("p (k t) -> p k t", k=2); pTb = Buf("pT")
        S.dma("sp", wg, self.w16["ple_gate"].rearrange("(k p) n -> p k n", p=128), r=[self.wbuf], w=[wgb])
        S.dma("sp", wp, self.w16["ple_proj"].rearrange("(k p) n -> p k n", p=128), r=[self.wbuf], w=[wpb])
        self.ffn_load_resident(F, 0, 1, 4, 5)
        self.load_gvec(0, 6, g6, g6b)
        self.load_gvec(0, 7, g7, g7b)
        od = Buf("out_dram")
        xT = F["xT"]; xTb = F["xTb"]
        eg = R["tmp"]
        for t in range(self.cfg.NT):
            ht = hts[t % 2]; hb = hbs[t % 2]
            tok = slice(t * 512, (t + 1) * 512)
            S.dma("sp", ht, self.h_att_out[tok, :].rearrange("(s p) d -> p s d", p=128), w=[hb])
            self.ffn_tile(F, R, 0, 1, ht, hb)
            for s4 in range(4):
                r0 = t * 512 + s4 * 128
                self.pre_norm_T(ht[:, s4, :], hb, g6, g6b, xT, xTb, s4 * 128, R)
                for dh in range(2):
                    for k in range(8):
                        S.op("pe", lambda e, k=k, dh=dh, s4=s4: e.matmul(self.bank(4 + dh), lhsT=xT[:, k, s4 * 128:(s4 + 1) * 128], rhs=wg[:, k, dh * 512:(dh + 1) * 512],
                                                                      start=(k == 0), stop=(k == 7)), r=[xTb, wgb], w=[self.pb[4 + dh]])
                S.op("act", lambda e: e.activation(out=sgt, in_=self.ps[:, 4 * 512:6 * 512], func=AF.Sigmoid), r=[self.pb[4], self.pb[5]], w=[sgtb])
                S.dma("sp", pt_, self.p[r0:r0 + 128, :], w=[ptb_])
                S.op("dve", lambda e: e.tensor_copy(out=pbf, in_=pt_), r=[ptb_], w=[pbfb])
                tv = self.bank(6).bitcast(BF16)
                for k in range(2):
                    S.op("pe", lambda e, k=k, tv=tv: e.transpose(out=tv[:, k * 128:(k + 1) * 128], in_=pbf[:, k * 128:(k + 1) * 128], identity=self.ident),
                         r=[pbfb, self.cb], w=[self.pb[6]])
                S.op("act", lambda e, tv=tv: e.copy(out=pT, in_=tv[:, 0:256].rearrange("p (k t) -> p k t", k=2)), r=[self.pb[6]], w=[pTb])
                for dh in range(2):
                    for k in range(2):
                        S.op("pe", lambda e, k=k, dh=dh: e.matmul(self.bank(dh), lhsT=pT[:, k, :], rhs=wp[:, k, dh * 512:(dh + 1) * 512], start=(k == 0), stop=(k == 1)),
                             r=[pTb, wpb], w=[self.pb[dh]])
                S.op("dve", lambda e: e.tensor_tensor(out=sgt, in0=self.ps[:, 0:1024], in1=sgt, op=ALU.mult), r=[self.pb[0], self.pb[1], sgtb], w=[sgtb])
                self.post_norm_res(sgt, sgtb, g7, g7b, 1.0, ht[:, s4, :], hb, R)
            S.dma("pool", self.h_mid_out[tok, :].rearrange("(s p) d -> p s d", p=128), ht, r=[hb], w=[od], key=S.dram_key("h"))
        S.barrier()

    def scale_gvec(self, gt, gb, coef):
        self.S.op("dve", lambda e: e.tensor_scalar(out=gt, in0=gt, scalar1=float(coef), scalar2=None, op0=ALU.mult), r=[gb], w=[gb])

    def phase_exchange(self):
        S = self.S
        T = self.cfg.TOK
        send_flat = self.send.rearrange("r c -> (r c)")
        sb = Buf("sendbuf"); rb = Buf("recvbuf")
        for (si, kind, side, idx), (g, og, n) in self.unit.items():
            nm, q, k, v, H, VW = self.srcs[si]
            o = self.colls[g][0] * 512 + og
            t0 = 0 if side == "L" else T - H
            if kind == "K":
                src = k[idx * 128:(idx + 1) * 128, t0:t0 + H]
                dst = send_flat[o:o + n].rearrange("(p t) -> p t", t=H)
            else:
                src = v[t0 + idx * 128:t0 + (idx + 1) * 128, :]
                dst = send_flat[o:o + n].rearrange("(t w) -> t w", w=VW)
            S.dma("sp", dst, src, w=[sb], key=S.dram_key("xs", 2))
        for (r0, nr) in self.colls:
            S.custom("pool", lambda e, r0=r0, nr=nr: e.collective_compute("AllGather", ALU.bypass, replica_groups=[[0, 1, 2, 3], [4, 5, 6, 7]],
                                                                     ins=[self.send[r0:r0 + nr, :]], outs=[self.recv[4 * r0:4 * r0 + 4 * nr, :]]),
                     r=[sb], w=[rb], key="cc", inc=1)
        S.barrier()

    def emit_all(self):
        self.emit_consts()
        self.wbuf = Buf("w16")
        self.emit_conv(self.conv_jobs(0))
        self.S.barrier()
        import os
        nstop = int(os.environ.get("FUSE_STOP", "999"))
        n = 0
        for l in range(self.cfg.DEPTH):
            self.set_layer(l)
            self.pending_conv = self.conv_jobs(l + 1) if l + 1 < self.cfg.DEPTH else []
            for ph in (self.phase_pre, self.phase_exchange, self.phase_att, self.phase_mid):
                if n < nstop:
                    ph()
                n += 1
            self.emit_conv(self.pending_conv)
            self.pending_conv = []
        self.S.barrier()


_PROG_CACHE = {}


def _get_prog(seq, depth):
    key = (seq, depth)
    if key not in _PROG_CACHE:
        P = Prog(Cfg(seq=seq, depth=depth))
        nc = P.build()
        _PROG_CACHE[key] = (P, nc)
    return _PROG_CACHE[key]


def _rope_tables(T, pos0):
    pos = (pos0 + np.arange(T)).astype(np.float32)
    inv = (np.float32(500000.0) ** (-np.arange(0, 16, 2, dtype=np.float32) / np.float32(16))).astype(np.float32)
    ang = pos[None, :] * inv[:, None]
    c = np.cos(ang).astype(np.float32); s = np.sin(ang).astype(np.float32)
    C = np.ones((128, T), np.float32); Sg = np.zeros((128, T), np.float32)
    for p in range(128):
        f = p % 64
        if f < 8:
            C[p] = c[f]; Sg[p] = -s[f]
        elif f < 16:
            C[p] = c[f - 8]; Sg[p] = s[f - 8]
    return C, Sg


def _masks(mx, rank, T, seq):
    kk = np.arange(128)[:, None]; qq = np.arange(128)[None, :]
    def tile(valid):
        return np.where(valid, 0.0, NEG).astype(np.float32)
    out = []
    if mx == 0:
        for d in (-1, 0, 1):
            rel = 128 * d + kk - qq
            out.append(tile(np.abs(rel) <= 64))
        for d in (-2, 0, 2):
            rel = 128 * d + kk - qq
            out.append(tile((rel % 4 == 0) & (np.abs(rel) <= 256)))
        for d in (-8, 0, 8):
            rel = 128 * d + kk - qq
            out.append(tile((rel % 16 == 0) & (np.abs(rel) <= 1024)))
    elif mx == 1:
        for d in (-1, 1):
            rel = 128 * d + kk - qq
            out.append(tile(np.abs(rel) <= 128))
    else:
        NQ = T // 128
        rows_total = seq // 64
        row0 = (rank % 4) * (T // 64)
        kr = kk // 64; kc = kk % 64; qr = qq // 64; qc = qq % 64
        cstart = np.clip(qc - 8, 0, 64 - 16)
        colv = (kc >= cstart) & (kc < cstart + 16)
        for i in (2, 0, 1, NQ - 2, NQ - 1):
            for d in range(-3, 4):
                Rq = row0 + 2 * i + qr
                Rk = row0 + 2 * (i + d) + kr
                rs = np.clip(Rq - 4, 0, rows_total - 8)
                rowv = (Rk >= rs) & (Rk < rs + 8)
                out.append(tile(rowv & colv))
    return np.stack(out, 0)


def _c_bias_gather(rpb):
    kk = np.arange(128)[:, None]; qq = np.arange(128)[None, :]
    kr = kk // 64; kc = kk % 64; qr = qq // 64; qc = qq % 64
    out = np.zeros((16, 7, 128, 128), np.float32)
    for d in range(-3, 4):
        dr = 2 * d + kr - qr
        dc = kc - qc
        ok = (np.abs(dr) <= 7) & (np.abs(dc) <= 15)
        g = rpb[:, np.clip(dr + 7, 0, 14), np.clip(dc + 15, 0, 30)]
        out[:, d + 3] = np.where(ok[None], g, 0.0)
    return out


def kernel(x, p, norm_g, ffn_wi, ffn_wo, ple_proj, ple_gate, a_wqkv, a_wo, b_wqkv, b_wo, b_sink, c_wqkv, c_wo, c_rpb):
    x = np.asarray(x, np.float32)
    B, SEQ, _ = x.shape
    depth = np.asarray(norm_g).shape[0]
    T = SEQ * B // 8
    NC = 8
    f = lambda a: np.ascontiguousarray(np.asarray(a, np.float32))
    p = f(p)
    n_b = len(range(1, depth, 3)); n_c = len(range(2, depth, 3))
    shared = {"norm_g": f(norm_g), "ffn_wi": f(ffn_wi), "ffn_wo": f(ffn_wo), "ple_proj": f(ple_proj), "ple_gate": f(ple_gate),
              "a_wqkv": f(a_wqkv), "a_wo": f(a_wo)}
    if n_b:
        shared.update({"b_wqkv": f(b_wqkv)[:n_b], "b_wo": f(b_wo)[:n_b], "b_sink": f(b_sink)[:n_b]})
    if n_c:
        shared.update({"c_wqkv": f(c_wqkv)[:n_c], "c_wo": f(c_wo)[:n_c],
                       "c_biasg": np.stack([_c_bias_gather(f(c_rpb)[j]) for j in range(n_c)], 0)})
    xf = x.reshape(B * SEQ, D)
    pf = p.reshape(depth, B * SEQ, PLE)
    P, nc = _get_prog(SEQ, depth)
    in_maps = []
    for r in range(NC):
        q = r % 4
        m = dict(shared)
        m["x"] = np.ascontiguousarray(xf[r * T:(r + 1) * T])
        m["p"] = np.ascontiguousarray(pf[:, r * T:(r + 1) * T])
        C_, S_ = _rope_tables(T, q * T)
        m["rope_c"] = C_; m["rope_s"] = S_
        fl = np.zeros((128, 6), np.float32)
        if q >= 1:
            fl[:, q - 1] = 1.0
        if q <= 2:
            fl[:, 3 + q] = 1.0
        m["flags"] = fl
        m["masks_a"] = _masks(0, r, T, SEQ)
        if n_b:
            m["masks_b"] = _masks(1, r, T, SEQ)
        if n_c:
            m["masks_c"] = _masks(2, r, T, SEQ)
        in_maps.append(m)
    res = run_bass_kernel_spmd(nc, in_maps, core_ids=list(range(NC))).results
    out = np.concatenate([np.asarray(res[r]["out"]) for r in range(NC)], 0).reshape(B, SEQ, D).astype(np.float32)
    return out
```

```python
import math
import numpy as np
import concourse.bass as bass
import concourse.mybir as mybir
from concourse.bass_utils import run_bass_kernel_spmd

F32 = mybir.dt.float32
BF16 = mybir.dt.bfloat16
ALU = mybir.AluOpType
AF = mybir.ActivationFunctionType

D = 1024
DFF = 2816
NJ = DFF // 128
PLE = 256
NEG = -30000.0
ENGS = ["pe", "act", "dve", "pool", "sp"]


class Buf:
    __slots__ = ("name", "excl")

    def __init__(self, name, excl=False):
        self.name = name
        self.excl = excl


class Sched:
    NEPOCH = 5

    def __init__(self):
        self.ops = {e: [] for e in ENGS}
        self.semval = {}
        self.waited = {e: {} for e in ENGS}
        self.lastw = {}
        self.readers = {}
        self.dram_rr = {}
        self.epoch = 0

    def ekey(self, eng):
        return "%s@%d" % (eng, (self.epoch % self.NEPOCH) if eng == "pe" else 0)

    def _deps(self, eng, reads, writes):
        toks = []
        for b in reads:
            t = self.lastw.get(b)
            if t is not None:
                toks.append(t)
            if b.excl:
                toks.extend(tk for tk in self.readers.get(b, ()) if tk[0].split("@")[0] != eng)
        for b in writes:
            t = self.lastw.get(b)
            if t is not None:
                toks.append(t)
            toks.extend(self.readers.get(b, ()))
        need = {}
        for (k, v) in toks:
            if eng == "pe" and k.startswith("pe@"):
                continue
            if need.get(k, 0) < v:
                need[k] = v
        out = []
        for k, v in need.items():
            if self.waited[eng].get(k, 0) < v:
                self.waited[eng][k] = v
                out.append((k, v))
        return out

    def _commit(self, tok, reads, writes):
        for b in writes:
            self.lastw[b] = tok
            self.readers[b] = []
        for b in reads:
            self.readers.setdefault(b, []).append(tok)

    def op(self, eng, fn, r=(), w=()):
        waits = self._deps(eng, r, w)
        k = self.ekey(eng)
        v = self.semval.get(k, 0) + 1
        self.semval[k] = v
        tok = (k, v)
        self.ops[eng].append((waits, fn, k, 1))
        self._commit(tok, r, w)
        return tok

    def custom(self, eng, fn, r=(), w=(), key=None, inc=1):
        waits = self._deps(eng, r, w)
        prev = self.semval.get(key, 0)
        if prev and self.waited[eng].get(key, 0) < prev:
            self.waited[eng][key] = prev
            waits.append((key, prev))
        self.semval[key] = prev + inc
        tok = (key, prev + inc)
        self.ops[eng].append((waits, fn, key, inc))
        self._commit(tok, r, w)
        return tok

    def dma(self, eng, out_ap, in_ap, r=(), w=(), key=None):
        if key is None:
            key = "d_" + w[0].name
        return self.custom(eng, (lambda e, o=out_ap, i=in_ap: e.dma_start(out=o, in_=i)), r=r, w=w, key=key, inc=16)

    def dram_key(self, name, n=2):
        i = self.dram_rr.get(name, 0)
        self.dram_rr[name] = i + 1
        return "d_%s_%d" % (name, i % n)

    def barrier(self):
        allv = dict(self.semval)
        for e in ENGS:
            waits = []
            for k, v in allv.items():
                if v and self.waited[e].get(k, 0) < v:
                    self.waited[e][k] = v
                    waits.append((k, v))
            if waits:
                self.ops[e].append((waits, None, None, 0))
        self.lastw.clear()
        self.readers.clear()
        self.epoch += 1

    def keys(self):
        return sorted(self.semval.keys())

    def replay(self, eng, handle, sems):
        for (waits, fn, inc_key, inc) in self.ops[eng]:
            for (k, v) in waits:
                handle.wait_ge(sems[k], v)
            if fn is not None:
                ins = fn(handle)
                ins.then_inc(sems[inc_key], inc)


class Arena:
    def __init__(self, t, nwords):
        self.t = t
        self.n = nwords
        self.off = 0

    def reset(self):
        self.off = 0

    def f32(self, n):
        a = self.off
        self.off += n
        assert self.off <= self.n, "SBUF arena overflow %d > %d" % (self.off, self.n)
        return self.t[:, a:a + n]

    def bf16(self, n):
        w = (n + 1) // 2
        a = self.off
        self.off += w
        assert self.off <= self.n, "SBUF arena overflow %d > %d" % (self.off, self.n)
        return self.t[:, a:a + w].bitcast(BF16)


class Cfg:
    def __init__(self, seq=16384, depth=4):
        self.SEQ = seq
        self.DEPTH = depth
        self.TOK = seq * 2 // 8
        self.NT = self.TOK // 512
        self.ROWS = self.TOK // 64


MIXER_OF = lambda l: l % 3


class Prog:
    def __init__(self, cfg, dbg=None):
        self.cfg = cfg
        self.dbg = dbg or []
        self.S = Sched()
        self.pb = [Buf("bank%d" % i, excl=True) for i in range(8)]
        self.nc = bass.Bass("TRN2", target_bir_lowering=False)
        self.dram = {}

    def dt(self, name, shape, dtype, kind="Internal"):
        if name in self.dbg and kind == "Internal":
            kind = "ExternalOutput"
        t = self.nc.dram_tensor(name, list(shape), dtype, kind=kind).ap()
        self.dram[name] = t
        return t

    def declare(self):
        c = self.cfg
        T = c.TOK
        L = c.DEPTH
        n_a = len(range(0, L, 3)); n_b = len(range(1, L, 3)); n_c = len(range(2, L, 3))
        self.x_in = self.dt("x", [T, D], F32, "ExternalInput")
        self.p_all = self.dt("p", [L, T, PLE], F32, "ExternalInput")
        self.norm_g_all = self.dt("norm_g", [L, 8, D], F32, "ExternalInput")
        self.rope_c = self.dt("rope_c", [128, T], F32, "ExternalInput")
        self.rope_s = self.dt("rope_s", [128, T], F32, "ExternalInput")
        self.flags = self.dt("flags", [128, 6], F32, "ExternalInput")
        self.masks_all = {0: self.dt("masks_a", [9, 128, 128], F32, "ExternalInput")}
        wspecs = [("ffn_wi", [L, 2, D, 2 * DFF]), ("ffn_wo", [L, 2, DFF, D]), ("ple_proj", [L, PLE, D]), ("ple_gate", [L, D, D]),
                  ("a_wqkv", [n_a, D, 4608]), ("a_wo", [n_a, 512, D])]
        if n_b:
            wspecs += [("b_wqkv", [n_b, D, 1536]), ("b_wo", [n_b, D, D])]
            self.b_sink_all = self.dt("b_sink", [n_b, 16], F32, "ExternalInput")
            self.masks_all[1] = self.dt("masks_b", [2, 128, 128], F32, "ExternalInput")
        if n_c:
            wspecs += [("c_wqkv", [n_c, D, 3072]), ("c_wo", [n_c, D, D])]
            self.c_biasg_all = self.dt("c_biasg", [n_c, 16, 7, 128, 128], F32, "ExternalInput")
            self.masks_all[2] = self.dt("masks_c", [35, 128, 128], F32, "ExternalInput")
        self.w32 = {}
        self.w16_all = {}
        for name, shp in wspecs:
            self.w32[name] = self.dt(name, shp, F32, "ExternalInput")
            if name != "ffn_wi":
                self.w16_all[name] = self.dt(name + "_bf", shp, BF16)
        self.wi_slot = self.dt("ffn_wi_slot", [L, 2, NJ // 2, 128, 4096], BF16)
        self.out_final = self.dt("out", [T, D], F32, "ExternalOutput")
        self.hbuf3 = [self.dt("h_scr%d" % i, [T, D], F32) for i in range(3)]
        self.srcs_all = {}
        for mx in range(3):
            if (mx == 1 and not n_b) or (mx == 2 and not n_c):
                continue
            lst = []
            for (nm, F, H, VW) in self.src_specs(mx):
                q = self.dt("q_" + nm, [F, T], BF16)
                k = self.dt("k_" + nm, [F if mx != 1 else 512, T], BF16)
                v = self.dt("v_" + nm, [T, VW], BF16)
                lst.append((nm, q, k, v, H, VW))
            self.srcs_all[mx] = lst
        mxr = 0
        for mx, lst in self.srcs_all.items():
            u, cl = self.plan_exchange(lst)
            mxr = max(mxr, cl[-1][0] + cl[-1][1])
        self.send_rows = mxr
        self.send = self.dt("x_send", [mxr, 512], BF16)
        self.recv = self.dt("x_recv", [4 * mxr, 512], BF16)

    def set_layer(self, l):
        L = self.cfg.DEPTH
        mx = l % 3; j = l // 3
        self.layer = l
        self.mx = mx
        self.norm_g = self.norm_g_all[l]
        self.p = self.p_all[l]
        self.masks = self.masks_all[mx]
        pre = "abc"[mx]
        self.w16 = {"wqkv": self.w16_all[pre + "_wqkv"][j], "wo": self.w16_all[pre + "_wo"][j],
                    "ple_proj": self.w16_all["ple_proj"][l], "ple_gate": self.w16_all["ple_gate"][l]}
        self.wi_l = self.wi_slot[l]
        self.wo_l = self.w16_all["ffn_wo"][l]
        if mx == 1:
            self.b_sink = self.b_sink_all[j:j + 1, :]
        if mx == 2:
            self.c_biasg = self.c_biasg_all[j]
        self.srcs = self.srcs_all[mx]
        self.h_pre_in = self.x_in if l == 0 else self.hbuf3[0]
        self.h_pre_out = self.hbuf3[1]
        self.h_att_out = self.hbuf3[2]
        self.h_mid_out = self.out_final if l == L - 1 else self.hbuf3[0]
        self.unit, self.colls = self.plan_exchange(self.srcs)

    @staticmethod
    def plan_exchange(srcs, maxrows=1000):
        units = []
        for si, (nm, q, k, v, H, VW) in enumerate(srcs):
            nkc = k.shape[0] // 128
            for side in "LR":
                for c in range(nkc):
                    units.append(((si, "K", side, c), 128 * H))
                for j in range(H // 128):
                    units.append(((si, "V", side, j), 128 * VW))
        unit = {}
        colls = []
        start = 0; cur = 0
        for key, n in units:
            assert n % 512 == 0
            r = n // 512
            if cur + r > maxrows:
                colls.append((start, cur)); start += cur; cur = 0
            unit[key] = (len(colls), cur * 512, n)
            cur += r
        colls.append((start, cur))
        return unit, colls

    def src_specs(self, mx):
        if mx == 0:
            return [("g0", 512, 128, 528), ("g1", 512, 256, 528), ("g2", 512, 1024, 528)]
        if mx == 1:
            return [("b", 1024, 128, 264)]
        return [("c", 1024, 256, 1056)]

    def nmask(self):
        return (9, 2, 35)[self.mx]

    def build(self):
        nc = self.nc
        c = self.cfg
        self.declare()
        NW = 48 * 1024 - 64
        with (
            nc.sbuf_tensor("arena", [128, NW], F32) as arena_t,
            nc.sbuf_tensor("consts", [128, 2048], F32) as consts_t,
            nc.psum_tensor("psum", [128, 4096], F32) as psum_t,
        ):
            self.A = Arena(arena_t, NW)
            self.CA = Arena(consts_t, 2048)
            self.ps = psum_t
            self.emit_all()
            keys = self.S.keys()
            print("semaphores:", len(keys))
            assert len(keys) < 140, "too many semaphores: %d" % len(keys)
            sem_cms = [nc.semaphore("s_" + k) for k in keys]
            handles = [cm.__enter__() for cm in sem_cms]
            sems = dict(zip(keys, handles))
            try:
                with nc.Block() as block:
                    @block.tensor
                    def _(e):
                        self.S.replay("pe", e, sems)

                    @block.scalar
                    def _(e):
                        self.S.replay("act", e, sems)

                    @block.vector
                    def _(e):
                        self.S.replay("dve", e, sems)

                    @block.gpsimd
                    def _(e):
                        self.S.replay("pool", e, sems)

                    @block.sync
                    def _(e):
                        self.S.replay("sp", e, sems)
            finally:
                for cm in reversed(sem_cms):
                    cm.__exit__(None, None, None)
        return nc

    def bank(self, i, n=512, dtype=F32):
        a = self.ps[:, i * 512:i * 512 + n]
        return a

    def emit_consts(self):
        S = self.S
        CA = self.CA
        self.ident = CA.bf16(128)
        self.rmat = CA.bf16(128)
        rtmp = CA.bf16(128)
        self.maskLG = CA.bf16(384)
        self.eps = CA.f32(1)
        self.ones65 = None
        self.zt = CA.bf16(1024)
        b = Buf("consts")
        self.cb = b
        ident, rmat, mk = self.ident, self.rmat, self.maskLG
        S.op("pool", lambda e: e.memset(ident, 0.0), w=[b])
        S.op("pool", lambda e: e.affine_select(out=ident, in_=ident, pattern=[[-1, 128]], compare_op=ALU.not_equal,
                                               fill=1.0, base=0, channel_multiplier=1), r=[b], w=[b])
        S.op("pool", lambda e: e.memset(rmat, 0.0), w=[b])
        S.op("pool", lambda e: e.affine_select(out=rmat, in_=rmat, pattern=[[-1, 128]], compare_op=ALU.not_equal,
                                               fill=1.0, base=-8, channel_multiplier=1), r=[b], w=[b])
        for (a0, a1) in ((8, 64), (72, 128)):
            S.op("pool", lambda e, a0=a0, a1=a1: e.memset(rmat[:, a0:a1], 0.0), r=[b], w=[b])
        S.op("pool", lambda e: e.memset(rtmp, 0.0), w=[b])
        S.op("pool", lambda e: e.affine_select(out=rtmp, in_=rtmp, pattern=[[-1, 128]], compare_op=ALU.not_equal,
                                               fill=1.0, base=8, channel_multiplier=1), r=[b], w=[b])
        for (a0, a1) in ((0, 8), (16, 72), (80, 128)):
            S.op("pool", lambda e, a0=a0, a1=a1: e.memset(rtmp[:, a0:a1], 0.0), r=[b], w=[b])
        S.op("pool", lambda e: e.tensor_tensor(out=rmat, in0=rmat, in1=rtmp, op=ALU.add), r=[b], w=[b])
        S.op("pool", lambda e: e.memset(mk, 0.0), w=[b])
        S.op("pool", lambda e: e.affine_select(out=mk[:, 0:128], in_=mk[:, 0:128], pattern=[[1, 128]], compare_op=ALU.is_ge,
                                               fill=NEG, base=0, channel_multiplier=-1), r=[b], w=[b])
        S.op("pool", lambda e: e.affine_select(out=mk[:, 256:384], in_=mk[:, 256:384], pattern=[[-1, 128]], compare_op=ALU.is_ge,
                                               fill=NEG, base=0, channel_multiplier=1), r=[b], w=[b])
        S.op("pool", lambda e: e.memset(self.eps, 1e-6), w=[b])
        S.op("pool", lambda e: e.memset(self.zt, 0.0), w=[b])

    def conv_jobs(self, l):
        mx = l % 3; j = l // 3
        jobs = []
        for f in range(2):
            src = self.w32["ffn_wi"][l, f].rearrange("(k p) (two n) -> p k two n", p=128, two=2)
            dst = self.wi_slot[l, f]
            for jp in range(NJ // 2):
                d3 = dst[jp].rearrange("p (k two c) -> p k two c", k=8, two=2)
                for two in range(2):
                    jobs.append((d3[:, :, two, :], src[:, :, two, jp * 256:(jp + 1) * 256]))
        def plain(dst, src, step=256):
            K = src.shape[0]
            for k0 in range(0, K, step):
                k1 = min(K, k0 + step)
                jobs.append((dst[k0:k1, :], src[k0:k1, :]))
        for f in range(2):
            plain(self.w16_all["ffn_wo"][l, f], self.w32["ffn_wo"][l, f])
        pre = "abc"[mx]
        plain(self.w16_all[pre + "_wqkv"][j], self.w32[pre + "_wqkv"][j])
        plain(self.w16_all[pre + "_wo"][j], self.w32[pre + "_wo"][j])
        plain(self.w16_all["ple_proj"][l], self.w32["ple_proj"][l])
        plain(self.w16_all["ple_gate"][l], self.w32["ple_gate"][l])
        return jobs

    def emit_conv(self, jobs):
        S = self.S
        for (dst, src) in jobs:
            S.dma("pool", dst, src, w=[self.wbuf], key=S.dram_key("w16", 4))

    def load_gvec(self, l, idx, dst, buf):
        src = self.norm_g[idx:idx + 1, :].partition_broadcast(128)
        self.S.dma("sp", dst, src, w=[buf])

    def pre_norm_T(self, src_ap, src_buf, gt, gbuf, xT, xTbuf, col0, R):
        S = self.S
        ss = R["ss"]; ssb = R["ssb"]
        i = R["ni"] = R.get("ni", 0) + 1
        xn = R["xn"][i % 2]; xnb = R["xnb"][i % 2]
        junk = R["junk"]; jb = R["junkb"]
        tb = R["tbank"][i % 2]; tbb = R["tbankb"][i % 2]
        eps = self.eps
        S.op("act", lambda e: e.activation(out=junk, in_=src_ap, func=AF.Square, accum_out=ss[:, 0:1]), r=[src_buf], w=[jb, ssb])
        S.op("act", lambda e: e.activation(out=ss[:, 1:2], in_=ss[:, 0:1], func=AF.Sqrt, scale=1.0 / D, bias=eps), r=[ssb], w=[ssb])
        S.op("dve", lambda e: e.reciprocal(out=ss[:, 2:3], in_=ss[:, 1:2]), r=[ssb], w=[ssb])
        S.op("dve", lambda e: e.scalar_tensor_tensor(out=xn, in0=src_ap, scalar=ss[:, 2:3], in1=gt, op0=ALU.mult, op1=ALU.mult),
             r=[src_buf, ssb, gbuf], w=[xnb])
        tv = tb.bitcast(BF16)
        for k in range(8):
            S.op("pe", lambda e, k=k: e.transpose(out=tv[:, k * 128:(k + 1) * 128], in_=xn[:, k * 128:(k + 1) * 128], identity=self.ident),
                 r=[xnb, self.cb], w=[tbb])
        S.op("act", lambda e: e.copy(out=xT[:, :, col0:col0 + 128], in_=tv.rearrange("p (k t) -> p k t", k=8)), r=[tbb], w=[xTbuf])

    def post_norm_res(self, y_ap, ybuf, gt, gbuf, coef, h_ap, hbuf, R):
        S = self.S
        ybl = ybuf if isinstance(ybuf, list) else [ybuf]
        ss = R["ss2"]; ssb = R["ss2b"]
        junk = R["junk"]; jb = R["junkb"]
        tmp = R["tmp"]; tmpb = R["tmpb"]
        eps = self.eps
        S.op("act", lambda e: e.activation(out=junk, in_=y_ap, func=AF.Square, accum_out=ss[:, 0:1]), r=ybl, w=[jb, ssb])
        S.op("act", lambda e: e.activation(out=ss[:, 1:2], in_=ss[:, 0:1], func=AF.Sqrt, scale=1.0 / D, bias=eps), r=[ssb], w=[ssb])
        S.op("dve", lambda e: e.reciprocal(out=ss[:, 2:3], in_=ss[:, 1:2]), r=[ssb], w=[ssb])
        S.op("dve", lambda e: e.scalar_tensor_tensor(out=tmp, in0=y_ap, scalar=ss[:, 2:3], in1=gt, op0=ALU.mult, op1=ALU.mult),
             r=ybl + [ssb, gbuf], w=[tmpb])
        S.op("pool", lambda e: e.tensor_tensor(out=h_ap, in0=tmp, in1=h_ap, op=ALU.add), r=[tmpb, hbuf], w=[hbuf])

    def alloc_common(self):
        A = self.A
        R = {}
        R["ss"] = A.f32(4); R["ssb"] = Buf("ss")
        R["ss2"] = A.f32(4); R["ss2b"] = Buf("ss2")
        R["xn"] = [A.bf16(1024), A.bf16(1024)]; R["xnb"] = [Buf("xn0"), Buf("xn1")]
        R["junk"] = A.bf16(1024); R["junkb"] = Buf("junk")
        R["tmp"] = A.f32(1024); R["tmpb"] = Buf("tmp")
        R["tbank"] = [self.bank(6), self.bank(7)]; R["tbankb"] = [self.pb[6], self.pb[7]]
        return R

    def alloc_ffn(self):
        A = self.A
        F = {}
        F["xT"] = A.bf16(8 * 512).rearrange("p (k t) -> p k t", k=8); F["xTb"] = Buf("xT")
        F["wi"] = [A.bf16(8 * 512).rearrange("p (k two c) -> p k two c", k=8, two=2) for _ in range(3)]
        F["wib"] = [Buf("wi%d" % i) for i in range(3)]
        F["act"] = A.bf16(NJ * 512).rearrange("p (j t) -> p j t", j=NJ); F["actb"] = Buf("act")
        F["wo"] = A.bf16(NJ * 1024).rearrange("p (j n) -> p j n", j=NJ); F["wob"] = [Buf("wo%d" % j) for j in range(NJ)]
        F["sg"] = [A.f32(512), A.f32(512)]; F["sgb"] = [Buf("sg0"), Buf("sg1")]
        F["gpre"] = A.f32(1024); F["gpreb"] = Buf("gpre")
        F["gpost"] = A.f32(1024); F["gpostb"] = Buf("gpost")
        F["wi_i"] = 0
        F["gu_i"] = 0
        return F

    def ffn_load_resident(self, F, l, f, gi_pre, gi_post):
        S = self.S
        wo16 = self.wo_l[f]
        for j in range(NJ):
            S.dma("sp", F["wo"][:, j, :], wo16[j * 128:(j + 1) * 128, :], r=[self.wbuf], w=[F["wob"][j]], key="d_wores%d" % (j % 4))
        self.load_gvec(l, gi_pre, F["gpre"], F["gpreb"])
        self.load_gvec(l, gi_post, F["gpost"], F["gpostb"])
        self.scale_gvec(F["gpost"], F["gpostb"], 0.5)

    def ffn_tile(self, F, R, l, f, ht, hbuf):
        S = self.S
        xT = F["xT"]; xTb = F["xTb"]
        for s in range(4):
            self.pre_norm_T(ht[:, s, :], hbuf, F["gpre"], F["gpreb"], xT, xTb, s * 128, R)
        act = F["act"]; actb = F["actb"]
        for jp in range(NJ // 2):
            wi_i = F["wi_i"]; F["wi_i"] += 1
            ws = F["wi"][wi_i % 3]; wsb = F["wib"][wi_i % 3]
            S.dma("sp", ws.rearrange("p k two c -> p (k two c)"), self.wi_l[f][jp], r=[self.wbuf], w=[wsb])
            for jj in range(2):
                j = jp * 2 + jj
                gi = F["gu_i"]; F["gu_i"] += 1
                pg = self.bank((gi % 2) * 2); pu = self.bank((gi % 2) * 2 + 1)
                pgb = self.pb[(gi % 2) * 2]
                pub = self.pb[(gi % 2) * 2 + 1]
                sg = F["sg"][gi % 2]; sgb = F["sgb"][gi % 2]
                for k in range(8):
                    S.op("pe", lambda e, k=k, ws=ws, jj=jj, pg=pg: e.matmul(pg, lhsT=ws[:, k, 0, jj * 128:(jj + 1) * 128], rhs=xT[:, k, :],
                                                                      start=(k == 0), stop=(k == 7)), r=[wsb, xTb], w=[pgb])
                for k in range(8):
                    S.op("pe", lambda e, k=k, ws=ws, jj=jj, pu=pu: e.matmul(pu, lhsT=ws[:, k, 1, jj * 128:(jj + 1) * 128], rhs=xT[:, k, :],
                                                                      start=(k == 0), stop=(k == 7)), r=[wsb, xTb], w=[pub])
                S.op("act", lambda e, sg=sg, pg=pg: e.activation(out=sg, in_=pg, func=AF.Silu), r=[pgb], w=[sgb])
                S.op("dve", lambda e, sg=sg, pu=pu, j=j: e.tensor_tensor(out=act[:, j, :], in0=sg, in1=pu, op=ALU.mult), r=[sgb, pub], w=[actb])
        for s in range(4):
            b0 = (4, 0, 2)[s % 3]
            y = self.ps[:, b0 * 512:(b0 + 2) * 512]
            for dh in range(2):
                for j in range(NJ):
                    S.op("pe", lambda e, j=j, s=s, dh=dh, b0=b0: e.matmul(self.ps[:, (b0 + dh) * 512:(b0 + dh + 1) * 512], lhsT=act[:, j, s * 128:(s + 1) * 128],
                                                                        rhs=F["wo"][:, j, dh * 512:(dh + 1) * 512], start=(j == 0), stop=(j == NJ - 1)),
                         r=[actb, F["wob"][j]], w=[self.pb[b0 + dh]])
            self.post_norm_res(y, [self.pb[b0], self.pb[b0 + 1]], F["gpost"], F["gpostb"], 0.5, ht[:, s, :], hbuf, R)

    def qkv_jobs(self):
        mx = self.mx
        W = self.w16["wqkv"].rearrange("(k p) n -> p k n", p=128)
        jobs = []
        vjobs = []
        if mx == 0:
            for g in range(3):
                nm, q, k, v, H, VW = self.srcs[g]
                jobs.append(([(0, g * 512, 512)], [(c * 128, q[c * 128:(c + 1) * 128, :]) for c in range(4)], True))
                jobs.append(([(0, 1536 + g * 512, 512)], [(c * 128, k[c * 128:(c + 1) * 128, :]) for c in range(4)], True))
                vjobs.append((3072 + g * 512, 512, v, 0, 8))
        elif mx == 1:
            nm, q, k, v, H, VW = self.srcs[0]
            for hf in range(2):
                jobs.append(([(0, hf * 512, 512)], [(c * 128, q[(hf * 4 + c) * 128:(hf * 4 + c + 1) * 128, :]) for c in range(4)], True))
            pieces = []
            for kv in range(4):
                pieces += [(kv * 128, 1024 + kv * 64, 64), (kv * 128 + 64, 1024 + kv * 64, 64)]
            jobs.append((pieces, [(kv * 128, k[kv * 128:(kv + 1) * 128, :]) for kv in range(4)], True))
            vjobs.append((1280, 256, v, 0, 4))
        else:
            nm, q, k, v, H, VW = self.srcs[0]
            for hf in range(2):
                jobs.append(([(0, hf * 512, 512)], [(c * 128, q[(hf * 4 + c) * 128:(hf * 4 + c + 1) * 128, :]) for c in range(4)], False))
            for hf in range(2):
                jobs.append(([(0, 1024 + hf * 512, 512)], [(c * 128, k[(hf * 4 + c) * 128:(hf * 4 + c + 1) * 128, :]) for c in range(4)], False))
            for vb in range(2):
                vjobs.append((2048 + vb * 512, 512, v, vb * 528, 8))
        return W, jobs, vjobs

    def phase_pre(self):
        S = self.S; A = self.A
        A.reset()
        R = self.alloc_common()
        F = self.alloc_ffn()
        hts = [A.f32(4 * 1024).rearrange("p (s d) -> p s d", s=4) for _ in range(2)]
        hbs = [Buf("ht0"), Buf("ht1")]
        g2 = A.f32(1024); g2b = Buf("g2")
        wq = [A.bf16(8 * 512).rearrange("p (k n) -> p k n", k=8) for _ in range(2)]; wqb = [Buf("wq0"), Buf("wq1")]
        ct = A.f32(512); st = A.f32(512); ctb = Buf("ct"); stb = Buf("st")
        qb = [A.bf16(512), A.bf16(512)]; qbb = [Buf("qb0"), Buf("qb1")]
        t1 = A.f32(512); t2 = A.f32(512); t1b = Buf("t1"); t2b = Buf("t2")
        qr = [A.bf16(512), A.bf16(512)]; qrb = [Buf("qr0"), Buf("qr1")]
        vsf = [A.bf16(8 * 66) for _ in range(2)]
        vs = [v_.rearrange("p (h d) -> p h d", h=8) for v_ in vsf]; vsb = [Buf("vs0"), Buf("vs1")]
        for i in range(2):
            S.op("dve", lambda e, i=i: e.memset(vsf[i], 1.0), w=[vsb[i]])
        self.ffn_load_resident(F, 0, 0, 0, 1)
        self.load_gvec(0, 2, g2, g2b)
        W, jobs, vjobs = self.qkv_jobs()
        od = Buf("out_dram")
        cnt = {"w": 0, "c": 0, "v": 0}
        for t in range(self.cfg.NT):
            ht = hts[t % 2]; hb = hbs[t % 2]
            tok = slice(t * 512, (t + 1) * 512)
            S.dma("sp", ht, self.h_pre_in[tok, :].rearrange("(s p) d -> p s d", p=128), w=[hb])
            self.ffn_tile(F, R, 0, 0, ht, hb)
            S.dma("pool", self.h_pre_out[tok, :].rearrange("(s p) d -> p s d", p=128), ht, r=[hb], w=[od], key=S.dram_key("h"))
            xT = F["xT"]; xTb = F["xTb"]
            for s4 in range(4):
                self.pre_norm_T(ht[:, s4, :], hb, g2, g2b, xT, xTb, s4 * 128, R)
            if self.mx != 2:
                S.dma("sp", ct, self.rope_c[:, tok], w=[ctb])
                S.dma("sp", st, self.rope_s[:, tok], w=[stb])
            import os
            for (pieces, chunks, rope) in (jobs if not os.environ.get('SKIP_QK') else []):
                wi_ = cnt["w"]; cnt["w"] += 1
                ws = wq[wi_ % 2]; wsb = wqb[wi_ % 2]
                for (sc, src, n) in pieces:
                    S.dma("sp", ws[:, :, sc:sc + n], W[:, :, src:src + n], r=[self.wbuf], w=[wsb])
                for (sc, dest) in chunks:
                    ci = cnt["c"]; cnt["c"] += 1
                    pq = self.bank(ci % 2); pqb = self.pb[ci % 2]
                    for k in range(8):
                        S.op("pe", lambda e, k=k, ws=ws, sc=sc, pq=pq: e.matmul(pq, lhsT=ws[:, k, sc:sc + 128], rhs=xT[:, k, :], start=(k == 0), stop=(k == 7)),
                             r=[wsb, xTb], w=[pqb])
                    q1 = qr[ci % 2]; q1b = qrb[ci % 2]
                    if rope and not os.environ.get('SKIP_ROPE'):
                        b1 = qb[ci % 2]; b1b = qbb[ci % 2]
                        pr = self.bank(2 + ci % 2); prb = self.pb[2 + ci % 2]
                        S.op("act", lambda e, b1=b1, pq=pq: e.copy(out=b1, in_=pq), r=[pqb], w=[b1b])
                        if not os.environ.get('SKIP_RM'):
                            S.op("pe", lambda e, b1=b1, pr=pr: e.matmul(pr, lhsT=self.rmat, rhs=b1, start=True, stop=True), r=[b1b, self.cb], w=[prb])
                        S.op("dve", lambda e, pq=pq: e.tensor_tensor(out=t1, in0=ct, in1=pq, op=ALU.mult), r=[pqb, ctb], w=[t1b])
                        S.op("dve", lambda e, pr=pr: e.tensor_tensor(out=t2, in0=st, in1=pr, op=ALU.mult), r=[prb, stb], w=[t2b])
                        S.op("dve", lambda e, q1=q1: e.tensor_tensor(out=q1, in0=t1, in1=t2, op=ALU.add), r=[t1b, t2b], w=[q1b])
                    else:
                        S.op("act", lambda e, q1=q1, pq=pq: e.copy(out=q1, in_=pq), r=[pqb], w=[q1b])
                    S.dma("pool", dest[:, tok], q1, r=[q1b], w=[od], key=S.dram_key("qk", 3))
            for (src, ncols, vdst, dc0, nh) in (vjobs if not os.environ.get('SKIP_V') else []):
                wi_ = cnt["w"]; cnt["w"] += 1
                ws = wq[wi_ % 2]; wsb = wqb[wi_ % 2]
                S.dma("sp", ws[:, :, 0:ncols], W[:, :, src:src + ncols], r=[self.wbuf], w=[wsb])
                for s4 in range(4):
                    vi = cnt["v"]; cnt["v"] += 1
                    pv = self.bank(4 + vi % 2)[:, 0:ncols]; pvb = self.pb[4 + vi % 2]
                    for k in range(8):
                        S.op("pe", lambda e, k=k, ws=ws, pv=pv, s4=s4, ncols=ncols: e.matmul(pv, lhsT=xT[:, k, s4 * 128:(s4 + 1) * 128], rhs=ws[:, k, 0:ncols],
                                                                                    start=(k == 0), stop=(k == 7)), r=[wsb, xTb], w=[pvb])
                    v1 = vs[vi % 2]; v1b = vsb[vi % 2]
                    S.op("act", lambda e, v1=v1, pv=pv, nh=nh: e.copy(out=v1[:, 0:nh, 0:64], in_=pv.rearrange("p (h d) -> p h d", h=nh)), r=[pvb], w=[v1b])
                    r0 = t * 512 + s4 * 128
                    S.dma("pool", vdst[r0:r0 + 128, dc0:dc0 + nh * 66], v1[:, 0:nh, :].rearrange("p h d -> p (h d)"), r=[v1b], w=[od], key=S.dram_key("v"))
            if self.pending_conv:
                nper = -(-len(self.pending_conv) // max(1, self.cfg.NT - t))
                self.emit_conv(self.pending_conv[:nper])
                self.pending_conv = self.pending_conv[nper:]
        S.barrier()

    def head_plan(self, i, NQ):
        mx = self.mx
        M = self.msk
        groups = []
        if mx == 0:
            dl = [[(-1, 0), (0, 1), (1, 2)], [(-2, 3), (-1, 4), (0, 4), (1, 4), (2, 5)], [(-8, 6)] + [(d, 7) for d in range(-7, 8)] + [(8, 8)]]
            for hg in range(2):
                heads = []
                for h in range(hg * 4, hg * 4 + 4):
                    tl = []
                    for g in range(3):
                        for (d, m) in dl[g]:
                            tl.append((g, h // 2, h % 2, h // 2, h * 66, d, [M[:, m, :]]))
                    heads.append((h, tl))
                groups.append(heads)
        elif mx == 1:
            for hg in range(4):
                heads = []
                for h in range(hg * 4, hg * 4 + 4):
                    tl = [(0, h // 2, h % 2, h // 4, (h // 4) * 66, -1, [M[:, 0, :]]), (0, h // 2, h % 2, h // 4, (h // 4) * 66, 0, []),
                          (0, h // 2, h % 2, h // 4, (h // 4) * 66, 1, [M[:, 1, :]])]
                    heads.append((h, tl))
                groups.append(heads)
        else:
            cls = 0
            if i == 0: cls = 1
            elif i == 1: cls = 2
            elif i == NQ - 2: cls = 3
            elif i == NQ - 1: cls = 4
            ds = list(range(-2, 3))
            if i == 0: ds.append(3)
            if i == NQ - 1: ds = [-3] + ds
            for hg in range(4):
                heads = []
                for h in range(hg * 4, hg * 4 + 4):
                    tl = [(0, h // 2, h % 2, h // 2, h * 66, d, [self.cbias[:, h * 7 + d + 3, :], M[:, cls * 7 + d + 3, :]]) for d in ds]
                    heads.append((h, tl))
                groups.append(heads)
        return groups

    def phase_att(self):
        S = self.S; A = self.A
        A.reset()
        R = self.alloc_common()
        T = self.cfg.TOK
        NQ = T // 128
        mx = self.mx
        NM = self.nmask()
        self.msk = A.bf16(NM * 128).rearrange("p (m q) -> p m q", m=NM); mb = Buf("msk")
        for m0 in range(0, NM, 8):
            m1 = min(NM, m0 + 8)
            S.dma("pool", self.msk[:, m0:m1, :], self.masks[m0:m1].rearrange("m p q -> p m q"), w=[mb])
        cbb = Buf("cbias")
        if mx == 2:
            self.cbias = A.bf16(112 * 128).rearrange("p (m q) -> p m q", m=112)
            cbg = self.c_biasg.rearrange("h d p q -> p (h d) q")
            for m0 in range(0, 112, 8):
                S.dma("pool", self.cbias[:, m0:m0 + 8, :], cbg[:, m0:m0 + 8, :], w=[cbb])
            S.op("dve", lambda e, cbt=self.cbias: e.tensor_scalar(out=cbt, in0=cbt, scalar1=8.0, scalar2=None, op0=ALU.mult), r=[cbb], w=[cbb])
        Fo = (512, 1024, 1024)[mx]
        nco = Fo // 128
        wo = A.bf16(nco * 1024).rearrange("p (c n) -> p c n", c=nco); wob = Buf("wo_mix")
        S.dma("sp", wo, self.w16["wo"].rearrange("(c p) n -> p c n", p=128), r=[self.wbuf], w=[wob])
        g3 = A.f32(1024); g3b = Buf("g3")
        self.load_gvec(0, 3, g3, g3b)
        es = None
        if mx == 1:
            es = A.f32(16); esb = Buf("esink")
            S.dma("sp", es, self.b_sink.partition_broadcast(128), w=[esb])
            S.op("act", lambda e: e.activation(out=es, in_=es, func=AF.Exp), r=[esb], w=[esb])
        fl = A.f32(6); flb = Buf("flags")
        S.dma("sp", fl, self.flags, w=[flb])
        ckmax = max(k.shape[0] for (nm, q, k, v, H, VW) in self.srcs)
        vwmax = max(VW for (nm, q, k, v, H, VW) in self.srcs)
        candk = [A.bf16(ckmax) for _ in range(3)]; candv = [A.bf16(vwmax) for _ in range(3)]
        candb = [Buf("cand%d" % j) for j in range(3)]
        recv_flat = self.recv.rearrange("r c -> (r c)")
        rings = []
        for si_, (nm, q, k, v, H, VW) in enumerate(self.srcs):
            nkc = k.shape[0] // 128
            nqc = q.shape[0] // 128
            Hb = H // 128
            dmax = Hb if mx != 2 else 2
            rs = 2 * dmax + 1 + 2
            kslots = [A.bf16(nkc * 128).rearrange("p (c t) -> p c t", c=nkc) for _ in range(rs)]
            vslots = [A.bf16(VW) for _ in range(rs)]
            kb = [Buf("kv_%s_%d" % (nm, j)) for j in range(rs)]
            vb = kb
            qs = [A.bf16(nqc * 128).rearrange("p (c t) -> p c t", c=nqc) for _ in range(2)]
            qbf = [Buf("q_%d_%d" % (si_, j)) for j in range(2)]
            rings.append(dict(q=q.rearrange("(c p) t -> p c t", p=128), k=k.rearrange("(c p) t -> p c t", p=128), v=v, Hb=Hb, dmax=dmax, rs=rs,
                              ks=kslots, vs=vslots, kb=kb, vb=vb, qs=qs, qb=qbf, loaded=-1, NKT=(T + 2 * H) // 128,
                              nm=nm, si=si_, H=H, VW=VW, Fk=k.shape[0], nkc=nkc))

        def load_kv(rg, kt, sl):
            Hb_ = rg["Hb"]; H_ = rg["H"]; VW_ = rg["VW"]; Fk_ = rg["Fk"]; nkc_ = rg["nkc"]
            key = "d_ring_%d_%d" % (rg["si"], sl % 5)
            kdst = rg["ks"][sl]; vdst = rg["vs"][sl]; sb_ = rg["kb"][sl]
            if Hb_ <= kt < Hb_ + NQ:
                j = kt - Hb_
                S.dma("sp", kdst, rg["k"][:, :, j * 128:(j + 1) * 128], w=[sb_], key=key)
                S.dma("sp", vdst, rg["v"][j * 128:(j + 1) * 128, :], w=[sb_], key=key)
                return
            left = kt < Hb_
            j = kt if left else kt - Hb_ - NQ
            side = "R" if left else "L"
            blocks = (0, 1, 2) if left else (1, 2, 3)
            f0 = 0 if left else 3
            for ci, bj in enumerate(blocks):
                ck = candk[ci][:, 0:nkc_ * 128].rearrange("p (c t) -> p c t", c=nkc_)
                cv = candv[ci][:, 0:VW_]
                for c in range(nkc_):
                    g, og, n = self.unit[(rg["si"], "K", side, c)]
                    r0, nr = self.colls[g]
                    o = (4 * r0 + bj * nr) * 512 + og
                    S.dma("sp", ck[:, c, :], recv_flat[o:o + n].rearrange("(p t) -> p t", t=H_)[:, j * 128:(j + 1) * 128], w=[candb[ci]])
                g, og, n = self.unit[(rg["si"], "V", side, j)]
                r0, nr = self.colls[g]
                o = (4 * r0 + bj * nr) * 512 + og
                S.dma("sp", cv, recv_flat[o:o + n].rearrange("(t w) -> t w", w=VW_), w=[candb[ci]])
                fcol = fl[:, f0 + ci:f0 + ci + 1]
                if ci == 0:
                    S.op("dve", lambda e, o_=kdst, i=ck, f=fcol: e.tensor_scalar(out=o_, in0=i, scalar1=f, scalar2=None, op0=ALU.mult), r=[candb[ci], flb], w=[sb_])
                    S.op("dve", lambda e, o_=vdst, i=cv, f=fcol: e.tensor_scalar(out=o_, in0=i, scalar1=f, scalar2=None, op0=ALU.mult), r=[candb[ci], flb], w=[sb_])
                else:
                    S.op("dve", lambda e, o_=kdst, i=ck, f=fcol: e.scalar_tensor_tensor(out=o_, in0=i, scalar=f, in1=o_, op0=ALU.mult, op1=ALU.add), r=[candb[ci], flb], w=[sb_])
                    S.op("dve", lambda e, o_=vdst, i=cv, f=fcol: e.scalar_tensor_tensor(out=o_, in0=i, scalar=f, in1=o_, op0=ALU.mult, op1=ALU.add), r=[candb[ci], flb], w=[sb_])

        pts = [A.bf16(512) for _ in range(3)]; ptb = [Buf("pt%d" % j) for j in range(3)]
        ob = [A.bf16(Fo) for _ in range(2)]; obb = [Buf("o%d" % j) for j in range(2)]
        oT = A.bf16(nco * 128).rearrange("p (c t) -> p c t", c=nco); oTb = Buf("oT")
        rd = A.f32(8); rdb = Buf("rd")
        hts = [A.f32(1024) for _ in range(3)]; hbs = [Buf("hq0"), Buf("hq1"), Buf("hq2")]
        hd = Buf("hscr")
        sc_i = 0; pt_i = 0; og_i = 0
        pending_pv = []
        pending_fin = []
        for i in range(NQ):
            ht = hts[i % 3]; hb = hbs[i % 3]
            for rg in rings:
                upto = min(rg["NKT"] - 1, i + rg["Hb"] + rg["dmax"] + 1)
                while rg["loaded"] < upto:
                    kt = rg["loaded"] + 1
                    sl = kt % rg["rs"]
                    load_kv(rg, kt, sl)
                    rg["loaded"] = kt
                S.dma("sp", rg["qs"][i % 2], rg["q"][:, :, i * 128:(i + 1) * 128], w=[rg["qb"][i % 2]])
            S.dma("sp", ht, self.h_pre_out[i * 128:(i + 1) * 128, :], w=[hb])
            o1 = ob[i % 2]; o1b = obb[i % 2]
            for hg_idx, heads in enumerate(self.head_plan(i, NQ)):
                og = og_i; og_i += 1
                orb = self.pb[3 + og % 2]
                oreg = self.bank(3 + og % 2)
                for hh, (h, tl) in enumerate(heads):
                    ocol = hh * 65
                    nt = len(tl)
                    for c0 in range(0, nt, 4):
                        grp = tl[c0:c0 + 4]
                        sb_ = sc_i % 3; sc_i += 1
                        sbank = self.bank(sb_); sbb = self.pb[sb_]
                        reads_kv = []
                        for j, (si, qc, half, kc, vcol, d, mts) in enumerate(grp):
                            rg = rings[si]
                            kt = i + rg["Hb"] + d
                            sl = kt % rg["rs"]
                            hp = slice(half * 64, half * 64 + 64)
                            kap = rg["ks"][sl][hp, kc, :]
                            qap = rg["qs"][i % 2][hp, qc, :]
                            dst = sbank[:, j * 128:(j + 1) * 128]
                            S.op("pe", lambda e, dst=dst, kap=kap, qap=qap, last=(len(mts) == 0): e.matmul(dst, lhsT=kap, rhs=qap, start=True, stop=last),
                                 r=[rg["kb"][sl], rg["qb"][i % 2]], w=[sbb])
                            for mi, mt in enumerate(mts):
                                S.op("pe", lambda e, dst=dst, mt=mt, last=(mi == len(mts) - 1): e.matmul(dst, lhsT=self.ident, rhs=mt, start=False, stop=last),
                                     r=[mb, cbb, self.cb], w=[sbb])
                        n = len(grp)
                        pi = pt_i % 3; pt_i += 1
                        pt = pts[pi]; ptbuf = ptb[pi]
                        S.op("act", lambda e, pt=pt, sbank=sbank, n=n: e.activation(out=pt[:, 0:n * 128], in_=sbank[:, 0:n * 128], func=AF.Exp, scale=0.125),
                             r=[sbb], w=[ptbuf])
                        def emit_pv(grp=grp, pt=pt, ptbuf=ptbuf, c0=c0, nt=nt, oreg=oreg, ocol=ocol, orb=orb):
                            for j, (si, qc, half, kc, vcol, d, mts) in enumerate(grp):
                                rg = rings[si]
                                kt = i + rg["Hb"] + d
                                sl = kt % rg["rs"]
                                first = (c0 + j == 0); last = (c0 + j == nt - 1)
                                S.op("pe", lambda e, pt=pt, j=j, vap=rg["vs"][sl][:, vcol:vcol + 65], oreg=oreg, ocol=ocol, first=first, last=last:
                                     e.matmul(oreg[:, ocol:ocol + 65], lhsT=pt[:, j * 128:(j + 1) * 128], rhs=vap, start=first, stop=last),
                                     r=[ptbuf, rg["vb"][sl]], w=[orb])
                        if pending_pv:
                            pending_pv.pop()()
                        pending_pv.append(emit_pv)
                if pending_pv:
                    pending_pv.pop()()
                nh = len(heads)
                den = oreg[:, 0:nh * 65].rearrange("p (h d) -> p h d", h=nh)[:, :, 64]
                if es is not None:
                    h0 = heads[0][0]
                    S.op("dve", lambda e, den=den, h0=h0, nh=nh: e.tensor_tensor(out=rd[:, 0:nh], in0=den, in1=es[:, h0:h0 + nh], op=ALU.add), r=[orb, esb], w=[rdb])
                    S.op("dve", lambda e, nh=nh: e.reciprocal(out=rd[:, 0:nh], in_=rd[:, 0:nh]), r=[rdb], w=[rdb])
                else:
                    S.op("dve", lambda e, den=den, nh=nh: e.reciprocal(out=rd[:, 0:nh], in_=den), r=[orb], w=[rdb])
                for hh, (h, tl) in enumerate(heads):
                    S.op("dve", lambda e, hh=hh, h=h, oreg=oreg, o1=o1: e.tensor_scalar(out=o1[:, h * 64:(h + 1) * 64], in0=oreg[:, hh * 65:hh * 65 + 64],
                                                                                 scalar1=rd[:, hh:hh + 1], scalar2=None, op0=ALU.mult), r=[orb, rdb], w=[o1b])
                if pending_fin and hg_idx < 2:
                    pending_fin.pop(0)()
            def fin_a(o1=o1, o1b=o1b):
                tv = self.bank(5).bitcast(BF16)
                for c in range(nco):
                    S.op("pe", lambda e, c=c, o1=o1, tv=tv: e.transpose(out=tv[:, c * 128:(c + 1) * 128], in_=o1[:, c * 128:(c + 1) * 128], identity=self.ident),
                         r=[o1b, self.cb], w=[self.pb[5]])
                S.op("act", lambda e, tv=tv: e.copy(out=oT, in_=tv[:, 0:nco * 128].rearrange("p (c t) -> p c t", c=nco)), r=[self.pb[5]], w=[oTb])

            def fin_b(i=i, ht=ht, hb=hb):
                for dh in range(2):
                    for c in range(nco):
                        S.op("pe", lambda e, c=c, dh=dh: e.matmul(self.bank(6 + dh), lhsT=oT[:, c, :], rhs=wo[:, c, dh * 512:(dh + 1) * 512], start=(c == 0), stop=(c == nco - 1)),
                             r=[oTb, wob], w=[self.pb[6 + dh]])
                self.post_norm_res(self.ps[:, 6 * 512:8 * 512], [self.pb[6], self.pb[7]], g3, g3b, 1.0, ht, hb, R)
                S.dma("pool", self.h_att_out[i * 128:(i + 1) * 128, :], ht, r=[hb], w=[hd], key=S.dram_key("hs"))
            pending_fin.extend([fin_a, fin_b])
        while pending_fin:
            pending_fin.pop(0)()
        S.barrier()

    def phase_mid(self):
        S = self.S; A = self.A
        A.reset()
        R = self.alloc_common()
        F = self.alloc_ffn()
        hts = [A.f32(4 * 1024).rearrange("p (s d) -> p s d", s=4) for _ in range(2)]
        hbs = [Buf("ht0"), Buf("ht1")]
        wg = A.bf16(8 * 1024).rearrange("p (k n) -> p k n", k=8); wgb = Buf("wgate")
        wp = A.bf16(2 * 1024).rearrange("p (k n) -> p k n", k=2); wpb = Buf("wproj")
        g6 = A.f32(1024); g6b = Buf("g6"); g7 = A.f32(1024); g7b = Buf("g7")
        sgt = A.f32(1024); sgtb = Buf("sgt")
        pt_ = A.f32(256); ptb_ = Buf("ptile")
        pbf = A.bf16(256); pbfb = Buf("pbf")
        pT = A.bf16(256).rearrange("p (k t) -> p k t", k=2); pTb = Buf("pT")
        S.dma("sp", wg, self.w16["ple_gate"].rearrange("(k p) n -> p k n", p=128), r=[self.wbuf], w=[wgb])
        S.dma("sp", wp, self.w16["ple_proj"].rearrange("(k p) n -> p k n", p=128), r=[self.wbuf], w=[wpb])
        self.ffn_load_resident(F, 0, 1, 4, 5)
        self.load_gvec(0, 6, g6, g6b)
        self.load_gvec(0, 7, g7, g7b)
        od = Buf("out_dram")
        xT = F["xT"]; xTb = F["xTb"]
        eg = R["tmp"]
        for t in range(self.cfg.NT):
            ht = hts[t % 2]; hb = hbs[t % 2]
            tok = slice(t * 512, (t + 1) * 512)
            S.dma("sp", ht, self.h_att_out[tok, :].rearrange("(s p) d -> p s d", p=128), w=[hb])
            self.ffn_tile(F, R, 0, 1, ht, hb)
            for s4 in range(4):
                r0 = t * 512 + s4 * 128
                self.pre_norm_T(ht[:, s4, :], hb, g6, g6b, xT, xTb, s4 * 128, R)
                for dh in range(2):
                    for k in range(8):
                        S.op("pe", lambda e, k=k, dh=dh, s4=s4: e.matmul(self.bank(4 + dh), lhsT=xT[:, k, s4 * 128:(s4 + 1) * 128], rhs=wg[:, k, dh * 512:(dh + 1) * 512],
                                                                      start=(k == 0), stop=(k == 7)), r=[xTb, wgb], w=[self.pb[4 + dh]])
                S.op("act", lambda e: e.activation(out=sgt, in_=self.ps[:, 4 * 512:6 * 512], func=AF.Sigmoid), r=[self.pb[4], self.pb[5]], w=[sgtb])
                S.dma("sp", pt_, self.p[r0:r0 + 128, :], w=[ptb_])
                S.op("dve", lambda e: e.tensor_copy(out=pbf, in_=pt_), r=[ptb_], w=[pbfb])
                tv = self.bank(6).bitcast(BF16)
                for k in range(2):
                    S.op("pe", lambda e, k=k, tv=tv: e.transpose(out=tv[:, k * 128:(k + 1) * 128], in_=pbf[:, k * 128:(k + 1) * 128], identity=self.ident),
                         r=[pbfb, self.cb], w=[self.pb[6]])
                S.op("act", lambda e, tv=tv: e.copy(out=pT, in_=tv[:, 0:256].rearrange("p (k t) -> p k t", k=2)), r=[self.pb[6]], w=[pTb])
                for dh in range(2):
                    for k in range(2):
                        S.op("pe", lambda e, k=k, dh=dh: e.matmul(self.bank(dh), lhsT=pT[:, k, :], rhs=wp[:, k, dh * 512:(dh + 1) * 512], start=(k == 0), stop=(k == 1)),
                             r=[pTb, wpb], w=[self.pb[dh]])
                S.op("dve", lambda e: e.tensor_tensor(out=sgt, in0=self.ps[:, 0:1024], in1=sgt, op=ALU.mult), r=[self.pb[0], self.pb[1], sgtb], w=[sgtb])
                self.post_norm_res(sgt, sgtb, g7, g7b, 1.0, ht[:, s4, :], hb, R)
            S.dma("pool", self.h_mid_out[tok, :].rearrange("(s p) d -> p s d", p=128), ht, r=[hb], w=[od], key=S.dram_key("h"))
        S.barrier()

    def scale_gvec(self, gt, gb, coef):
        self.S.op("dve", lambda e: e.tensor_scalar(out=gt, in0=gt, scalar1=float(coef), scalar2=None, op0=ALU.mult), r=[gb], w=[gb])

    def phase_exchange(self):
        S = self.S
        T = self.cfg.TOK
        send_flat = self.send.rearrange("r c -> (r c)")
        sb = Buf("sendbuf"); rb = Buf("recvbuf")
        for (si, kind, side, idx), (g, og, n) in self.unit.items():
            nm, q, k, v, H, VW = self.srcs[si]
            o = self.colls[g][0] * 512 + og
            t0 = 0 if side == "L" else T - H
            if kind == "K":
                src = k[idx * 128:(idx + 1) * 128, t0:t0 + H]
                dst = send_flat[o:o + n].rearrange("(p t) -> p t", t=H)
            else:
                src = v[t0 + idx * 128:t0 + (idx + 1) * 128, :]
                dst = send_flat[o:o + n].rearrange("(t w) -> t w", w=VW)
            S.dma("sp", dst, src, w=[sb], key=S.dram_key("xs", 2))
        for (r0, nr) in self.colls:
            S.custom("pool", lambda e, r0=r0, nr=nr: e.collective_compute("AllGather", ALU.bypass, replica_groups=[[0, 1, 2, 3], [4, 5, 6, 7]],
                                                                     ins=[self.send[r0:r0 + nr, :]], outs=[self.recv[4 * r0:4 * r0 + 4 * nr, :]]),
                     r=[sb], w=[rb], key="cc", inc=1)
        S.barrier()

    def emit_all(self):
        self.emit_consts()
        self.wbuf = Buf("w16")
        self.emit_conv(self.conv_jobs(0))
        self.S.barrier()
        import os
        nstop = int(os.environ.get("FUSE_STOP", "999"))
        n = 0
        for l in range(self.cfg.DEPTH):
            self.set_layer(l)
            self.pending_conv = self.conv_jobs(l + 1) if l + 1 < self.cfg.DEPTH else []
            for ph in (self.phase_pre, self.phase_exchange, self.phase_att, self.phase_mid):
                if n < nstop:
                    ph()
                n += 1
            self.emit_conv(self.pending_conv)
            self.pending_conv = []
        self.S.barrier()


_PROG_CACHE = {}


def _get_prog(seq, depth):
    key = (seq, depth)
    if key not in _PROG_CACHE:
        P = Prog(Cfg(seq=seq, depth=depth))
        nc = P.build()
        _PROG_CACHE[key] = (P, nc)
    return _PROG_CACHE[key]


def _rope_tables(T, pos0):
    pos = (pos0 + np.arange(T)).astype(np.float32)
    inv = (np.float32(500000.0) ** (-np.arange(0, 16, 2, dtype=np.float32) / np.float32(16))).astype(np.float32)
    ang = pos[None, :] * inv[:, None]
    c = np.cos(ang).astype(np.float32); s = np.sin(ang).astype(np.float32)
    C = np.ones((128, T), np.float32); Sg = np.zeros((128, T), np.float32)
    for p in range(128):
        f = p % 64
        if f < 8:
            C[p] = c[f]; Sg[p] = -s[f]
        elif f < 16:
            C[p] = c[f - 8]; Sg[p] = s[f - 8]
    return C, Sg


def _masks(mx, rank, T, seq):
    kk = np.arange(128)[:, None]; qq = np.arange(128)[None, :]
    def tile(valid):
        return np.where(valid, 0.0, NEG).astype(np.float32)
    out = []
    if mx == 0:
        for d in (-1, 0, 1):
            rel = 128 * d + kk - qq
            out.append(tile(np.abs(rel) <= 64))
        for d in (-2, 0, 2):
            rel = 128 * d + kk - qq
            out.append(tile((rel % 4 == 0) & (np.abs(rel) <= 256)))
        for d in (-8, 0, 8):
            rel = 128 * d + kk - qq
            out.append(tile((rel % 16 == 0) & (np.abs(rel) <= 1024)))
    elif mx == 1:
        for d in (-1, 1):
            rel = 128 * d + kk - qq
            out.append(tile(np.abs(rel) <= 128))
    else:
        NQ = T // 128
        rows_total = seq // 64
        row0 = (rank % 4) * (T // 64)
        kr = kk // 64; kc = kk % 64; qr = qq // 64; qc = qq % 64
        cstart = np.clip(qc - 8, 0, 64 - 16)
        colv = (kc >= cstart) & (kc < cstart + 16)
        for i in (2, 0, 1, NQ - 2, NQ - 1):
            for d in range(-3, 4):
                Rq = row0 + 2 * i + qr
                Rk = row0 + 2 * (i + d) + kr
                rs = np.clip(Rq - 4, 0, rows_total - 8)
                rowv = (Rk >= rs) & (Rk < rs + 8)
                out.append(tile(rowv & colv))
    return np.stack(out, 0)


def _c_bias_gather(rpb):
    kk = np.arange(128)[:, None]; qq = np.arange(128)[None, :]
    kr = kk // 64; kc = kk % 64; qr = qq // 64; qc = qq % 64
    out = np.zeros((16, 7, 128, 128), np.float32)
    for d in range(-3, 4):
        dr = 2 * d + kr - qr
        dc = kc - qc
        ok = (np.abs(dr) <= 7) & (np.abs(dc) <= 15)
        g = rpb[:, np.clip(dr + 7, 0, 14), np.clip(dc + 15, 0, 30)]
        out[:, d + 3] = np.where(ok[None], g, 0.0)
    return out


def kernel(x, p, norm_g, ffn_wi, ffn_wo, ple_proj, ple_gate, a_wqkv, a_wo, b_wqkv, b_wo, b_sink, c_wqkv, c_wo, c_rpb):
    x = np.asarray(x, np.float32)
    B, SEQ, _ = x.shape
    depth = np.asarray(norm_g).shape[0]
    T = SEQ * B // 8
    NC = 8
    f = lambda a: np.ascontiguousarray(np.asarray(a, np.float32))
    p = f(p)
    n_b = len(range(1, depth, 3)); n_c = len(range(2, depth, 3))
    shared = {"norm_g": f(norm_g), "ffn_wi": f(ffn_wi), "ffn_wo": f(ffn_wo), "ple_proj": f(ple_proj), "ple_gate": f(ple_gate),
              "a_wqkv": f(a_wqkv), "a_wo": f(a_wo)}
    if n_b:
        shared.update({"b_wqkv": f(b_wqkv)[:n_b], "b_wo": f(b_wo)[:n_b], "b_sink": f(b_sink)[:n_b]})
    if n_c:
        shared.update({"c_wqkv": f(c_wqkv)[:n_c], "c_wo": f(c_wo)[:n_c],
                       "c_biasg": np.stack([_c_bias_gather(f(c_rpb)[j]) for j in range(n_c)], 0)})
    xf = x.reshape(B * SEQ, D)
    pf = p.reshape(depth, B * SEQ, PLE)
    P, nc = _get_prog(SEQ, depth)
    in_maps = []
    for r in range(NC):
        q = r % 4
        m = dict(shared)
        m["x"] = np.ascontiguousarray(xf[r * T:(r + 1) * T])
        m["p"] = np.ascontiguousarray(pf[:, r * T:(r + 1) * T])
        C_, S_ = _rope_tables(T, q * T)
        m["rope_c"] = C_; m["rope_s"] = S_
        fl = np.zeros((128, 6), np.float32)
        if q >= 1:
            fl[:, q - 1] = 1.0
        if q <= 2:
            fl[:, 3 + q] = 1.0
        m["flags"] = fl
        m["masks_a"] = _masks(0, r, T, SEQ)
        if n_b:
            m["masks_b"] = _masks(1, r, T, SEQ)
        if n_c:
            m["masks_c"] = _masks(2, r, T, SEQ)
        in_maps.append(m)
    res = run_bass_kernel_spmd(nc, in_maps, core_ids=list(range(NC))).results
    out = np.concatenate([np.asarray(res[r]["out"]) for r in range(NC)], 0).reshape(B, SEQ, D).astype(np.float32)
    return out
```

```python
import math
import numpy as np
import concourse.bass as bass
import concourse.mybir as mybir
from concourse.bass_utils import run_bass_kernel_spmd

F32 = mybir.dt.float32
BF16 = mybir.dt.bfloat16
ALU = mybir.AluOpType
AF = mybir.ActivationFunctionType

D = 1024
DFF = 2816
NJ = DFF // 128
PLE = 256
NEG = -30000.0
ENGS = ["pe", "act", "dve", "pool", "sp"]


class Buf:
    __slots__ = ("name", "excl")

    def __init__(self, name, excl=False):
        self.name = name
        self.excl = excl


class Sched:
    NEPOCH = 5

    def __init__(self):
        self.ops = {e: [] for e in ENGS}
        self.semval = {}
        self.waited = {e: {} for e in ENGS}
        self.lastw = {}
        self.readers = {}
        self.dram_rr = {}
        self.epoch = 0

    def ekey(self, eng):
        return "%s@%d" % (eng, (self.epoch % self.NEPOCH) if eng == "pe" else 0)

    def _deps(self, eng, reads, writes):
        toks = []
        for b in reads:
            t = self.lastw.get(b)
            if t is not None:
                toks.append(t)
            if b.excl:
                toks.extend(tk for tk in self.readers.get(b, ()) if tk[0].split("@")[0] != eng)
        for b in writes:
            t = self.lastw.get(b)
            if t is not None:
                toks.append(t)
            toks.extend(self.readers.get(b, ()))
        need = {}
        for (k, v) in toks:
            if eng == "pe" and k.startswith("pe@"):
                continue
            if need.get(k, 0) < v:
                need[k] = v
        out = []
        for k, v in need.items():
            if self.waited[eng].get(k, 0) < v:
                self.waited[eng][k] = v
                out.append((k, v))
        return out

    def _commit(self, tok, reads, writes):
        for b in writes:
            self.lastw[b] = tok
            self.readers[b] = []
        for b in reads:
            self.readers.setdefault(b, []).append(tok)

    def op(self, eng, fn, r=(), w=()):
        waits = self._deps(eng, r, w)
        k = self.ekey(eng)
        v = self.semval.get(k, 0) + 1
        self.semval[k] = v
        tok = (k, v)
        self.ops[eng].append((waits, fn, k, 1))
        self._commit(tok, r, w)
        return tok

    def custom(self, eng, fn, r=(), w=(), key=None, inc=1):
        waits = self._deps(eng, r, w)
        prev = self.semval.get(key, 0)
        if prev and self.waited[eng].get(key, 0) < prev:
            self.waited[eng][key] = prev
            waits.append((key, prev))
        self.semval[key] = prev + inc
        tok = (key, prev + inc)
        self.ops[eng].append((waits, fn, key, inc))
        self._commit(tok, r, w)
        return tok

    def dma(self, eng, out_ap, in_ap, r=(), w=(), key=None):
        if key is None:
            key = "d_" + w[0].name
        return self.custom(eng, (lambda e, o=out_ap, i=in_ap: e.dma_start(out=o, in_=i)), r=r, w=w, key=key, inc=16)

    def dram_key(self, name, n=2):
        i = self.dram_rr.get(name, 0)
        self.dram_rr[name] = i + 1
        return "d_%s_%d" % (name, i % n)

    def barrier(self):
        allv = dict(self.semval)
        for e in ENGS:
            waits = []
            for k, v in allv.items():
                if v and self.waited[e].get(k, 0) < v:
                    self.waited[e][k] = v
                    waits.append((k, v))
            if waits:
                self.ops[e].append((waits, None, None, 0))
        self.lastw.clear()
        self.readers.clear()
        self.epoch += 1

    def keys(self):
        return sorted(self.semval.keys())

    def replay(self, eng, handle, sems):
        for (waits, fn, inc_key, inc) in self.ops[eng]:
            for (k, v) in waits:
                handle.wait_ge(sems[k], v)
            if fn is not None:
                ins = fn(handle)
                ins.then_inc(sems[inc_key], inc)


class Arena:
    def __init__(self, t, nwords):
        self.t = t
        self.n = nwords
        self.off = 0

    def reset(self):
        self.off = 0

    def f32(self, n):
        a = self.off
        self.off += n
        assert self.off <= self.n, "SBUF arena overflow %d > %d" % (self.off, self.n)
        return self.t[:, a:a + n]

    def bf16(self, n):
        w = (n + 1) // 2
        a = self.off
        self.off += w
        assert self.off <= self.n, "SBUF arena overflow %d > %d" % (self.off, self.n)
        return self.t[:, a:a + w].bitcast(BF16)


class Cfg:
    def __init__(self, seq=16384, depth=4):
        self.SEQ = seq
        self.DEPTH = depth
        self.TOK = seq * 2 // 8
        self.NT = self.TOK // 512
        self.ROWS = self.TOK // 64


MIXER_OF = lambda l: l % 3


class Prog:
    def __init__(self, cfg, dbg=None):
        self.cfg = cfg
        self.dbg = dbg or []
        self.S = Sched()
        self.pb = [Buf("bank%d" % i, excl=True) for i in range(8)]
        self.nc = bass.Bass("TRN2", target_bir_lowering=False)
        self.dram = {}

    def dt(self, name, shape, dtype, kind="Internal"):
        if name in self.dbg and kind == "Internal":
            kind = "ExternalOutput"
        t = self.nc.dram_tensor(name, list(shape), dtype, kind=kind).ap()
        self.dram[name] = t
        return t

    def declare(self):
        c = self.cfg
        T = c.TOK
        L = c.DEPTH
        n_a = len(range(0, L, 3)); n_b = len(range(1, L, 3)); n_c = len(range(2, L, 3))
        self.x_in = self.dt("x", [T, D], F32, "ExternalInput")
        self.p_all = self.dt("p", [L, T, PLE], F32, "ExternalInput")
        self.norm_g_all = self.dt("norm_g", [L, 8, D], F32, "ExternalInput")
        self.rope_c = self.dt("rope_c", [128, T], F32, "ExternalInput")
        self.rope_s = self.dt("rope_s", [128, T], F32, "ExternalInput")
        self.flags = self.dt("flags", [128, 6], F32, "ExternalInput")
        self.masks_all = {0: self.dt("masks_a", [25, 128, 128], F32, "ExternalInput")}
        wspecs = [("ffn_wi", [L, 2, D, 2 * DFF]), ("ffn_wo", [L, 2, DFF, D]), ("ple_proj", [L, PLE, D]), ("ple_gate", [L, D, D]),
                  ("a_wqkv", [n_a, D, 4608]), ("a_wo", [n_a, 512, D])]
        if n_b:
            wspecs += [("b_wqkv", [n_b, D, 1536]), ("b_wo", [n_b, D, D])]
            self.b_sink_all = self.dt("b_sink", [n_b, 16], F32, "ExternalInput")
            self.masks_all[1] = self.dt("masks_b", [3, 128, 128], F32, "ExternalInput")
        if n_c:
            wspecs += [("c_wqkv", [n_c, D, 3072]), ("c_wo", [n_c, D, D])]
            self.c_biasg_all = self.dt("c_biasg", [n_c, 16, 7, 128, 128], F32, "ExternalInput")
            self.masks_all[2] = self.dt("masks_c", [35, 128, 128], F32, "ExternalInput")
        self.w32 = {}
        self.w16_all = {}
        for name, shp in wspecs:
            self.w32[name] = self.dt(name, shp, F32, "ExternalInput")
            if name != "ffn_wi":
                self.w16_all[name] = self.dt(name + "_bf", shp, BF16)
        self.wi_slot = self.dt("ffn_wi_slot", [L, 2, NJ // 2, 128, 4096], BF16)
        self.out_final = self.dt("out", [T, D], F32, "ExternalOutput")
        self.hbuf3 = [self.dt("h_scr%d" % i, [T, D], F32) for i in range(3)]
        self.srcs_all = {}
        for mx in range(3):
            if (mx == 1 and not n_b) or (mx == 2 and not n_c):
                continue
            lst = []
            for (nm, F, H, VW) in self.src_specs(mx):
                q = self.dt("q_" + nm, [F, T], BF16)
                k = self.dt("k_" + nm, [F if mx != 1 else 512, T], BF16)
                v = self.dt("v_" + nm, [T, VW], BF16)
                lst.append((nm, q, k, v, H, VW))
            self.srcs_all[mx] = lst
        mxr = 0
        for mx, lst in self.srcs_all.items():
            u, cl = self.plan_exchange(lst)
            mxr = max(mxr, cl[-1][0] + cl[-1][1])
        self.send_rows = mxr
        self.send = self.dt("x_send", [mxr, 512], BF16)
        self.recv = self.dt("x_recv", [4 * mxr, 512], BF16)

    def set_layer(self, l):
        L = self.cfg.DEPTH
        mx = l % 3; j = l // 3
        self.layer = l
        self.mx = mx
        self.norm_g = self.norm_g_all[l]
        self.p = self.p_all[l]
        self.masks = self.masks_all[mx]
        pre = "abc"[mx]
        self.w16 = {"wqkv": self.w16_all[pre + "_wqkv"][j], "wo": self.w16_all[pre + "_wo"][j],
                    "ple_proj": self.w16_all["ple_proj"][l], "ple_gate": self.w16_all["ple_gate"][l]}
        self.wi_l = self.wi_slot[l]
        self.wo_l = self.w16_all["ffn_wo"][l]
        if mx == 1:
            self.b_sink = self.b_sink_all[j:j + 1, :]
        if mx == 2:
            self.c_biasg = self.c_biasg_all[j]
        self.srcs = self.srcs_all[mx]
        self.h_pre_in = self.x_in if l == 0 else self.hbuf3[0]
        self.h_pre_out = self.hbuf3[1]
        self.h_att_out = self.hbuf3[2]
        self.h_mid_out = self.out_final if l == L - 1 else self.hbuf3[0]
        self.unit, self.colls = self.plan_exchange(self.srcs)

    @staticmethod
    def plan_exchange(srcs, maxrows=1000):
        units = []
        for si, (nm, q, k, v, H, VW) in enumerate(srcs):
            nkc = k.shape[0] // 128
            for side in "LR":
                for c in range(nkc):
                    units.append(((si, "K", side, c), 128 * H))
                for j in range(H // 128):
                    units.append(((si, "V", side, j), 128 * VW))
        unit = {}
        colls = []
        start = 0; cur = 0
        for key, n in units:
            assert n % 512 == 0
            r = n // 512
            if cur + r > maxrows:
                colls.append((start, cur)); start += cur; cur = 0
            unit[key] = (len(colls), cur * 512, n)
            cur += r
        colls.append((start, cur))
        return unit, colls

    def src_specs(self, mx):
        if mx == 0:
            return [("g0", 512, 128, 528), ("g1", 512, 256, 528), ("g2", 512, 1024, 528)]
        if mx == 1:
            return [("b", 1024, 128, 264)]
        return [("c", 1024, 256, 1056)]

    def nmask(self):
        return (25, 3, 35)[self.mx]

    def build(self):
        nc = self.nc
        c = self.cfg
        self.declare()
        NW = 48 * 1024 - 64
        with (
            nc.sbuf_tensor("arena", [128, NW], F32) as arena_t,
            nc.sbuf_tensor("consts", [128, 2048], F32) as consts_t,
            nc.psum_tensor("psum", [128, 4096], F32) as psum_t,
        ):
            self.A = Arena(arena_t, NW)
            self.CA = Arena(consts_t, 2048)
            self.ps = psum_t
            self.emit_all()
            keys = self.S.keys()
            print("semaphores:", len(keys))
            assert len(keys) < 140, "too many semaphores: %d" % len(keys)
            sem_cms = [nc.semaphore("s_" + k) for k in keys]
            handles = [cm.__enter__() for cm in sem_cms]
            sems = dict(zip(keys, handles))
            try:
                with nc.Block() as block:
                    @block.tensor
                    def _(e):
                        self.S.replay("pe", e, sems)

                    @block.scalar
                    def _(e):
                        self.S.replay("act", e, sems)

                    @block.vector
                    def _(e):
                        self.S.replay("dve", e, sems)

                    @block.gpsimd
                    def _(e):
                        self.S.replay("pool", e, sems)

                    @block.sync
                    def _(e):
                        self.S.replay("sp", e, sems)
            finally:
                for cm in reversed(sem_cms):
                    cm.__exit__(None, None, None)
        return nc

    def bank(self, i, n=512, dtype=F32):
        a = self.ps[:, i * 512:i * 512 + n]
        return a

    def emit_consts(self):
        S = self.S
        CA = self.CA
        self.ident = CA.bf16(128)
        self.rmat = CA.bf16(128)
        rtmp = CA.bf16(128)
        self.maskLG = CA.bf16(384)
        self.eps = CA.f32(1)
        self.ones65 = None
        self.zt = CA.bf16(1024)
        b = Buf("consts")
        self.cb = b
        ident, rmat, mk = self.ident, self.rmat, self.maskLG
        S.op("pool", lambda e: e.memset(ident, 0.0), w=[b])
        S.op("pool", lambda e: e.affine_select(out=ident, in_=ident, pattern=[[-1, 128]], compare_op=ALU.not_equal,
                                               fill=1.0, base=0, channel_multiplier=1), r=[b], w=[b])
        S.op("pool", lambda e: e.memset(rmat, 0.0), w=[b])
        S.op("pool", lambda e: e.affine_select(out=rmat, in_=rmat, pattern=[[-1, 128]], compare_op=ALU.not_equal,
                                               fill=1.0, base=-8, channel_multiplier=1), r=[b], w=[b])
        for (a0, a1) in ((8, 64), (72, 128)):
            S.op("pool", lambda e, a0=a0, a1=a1: e.memset(rmat[:, a0:a1], 0.0), r=[b], w=[b])
        S.op("pool", lambda e: e.memset(rtmp, 0.0), w=[b])
        S.op("pool", lambda e: e.affine_select(out=rtmp, in_=rtmp, pattern=[[-1, 128]], compare_op=ALU.not_equal,
                                               fill=1.0, base=8, channel_multiplier=1), r=[b], w=[b])
        for (a0, a1) in ((0, 8), (16, 72), (80, 128)):
            S.op("pool", lambda e, a0=a0, a1=a1: e.memset(rtmp[:, a0:a1], 0.0), r=[b], w=[b])
        S.op("pool", lambda e: e.tensor_tensor(out=rmat, in0=rmat, in1=rtmp, op=ALU.add), r=[b], w=[b])
        S.op("pool", lambda e: e.memset(mk, 0.0), w=[b])
        S.op("pool", lambda e: e.affine_select(out=mk[:, 0:128], in_=mk[:, 0:128], pattern=[[1, 128]], compare_op=ALU.is_ge,
                                               fill=NEG, base=0, channel_multiplier=-1), r=[b], w=[b])
        S.op("pool", lambda e: e.affine_select(out=mk[:, 256:384], in_=mk[:, 256:384], pattern=[[-1, 128]], compare_op=ALU.is_ge,
                                               fill=NEG, base=0, channel_multiplier=1), r=[b], w=[b])
        S.op("pool", lambda e: e.memset(self.eps, 1e-6), w=[b])
        S.op("pool", lambda e: e.memset(self.zt, 0.0), w=[b])

    def conv_jobs(self, l):
        mx = l % 3; j = l // 3
        jobs = []
        for f in range(2):
            src = self.w32["ffn_wi"][l, f].rearrange("(k p) (two n) -> p k two n", p=128, two=2)
            dst = self.wi_slot[l, f]
            for jp in range(NJ // 2):
                d3 = dst[jp].rearrange("p (k two c) -> p k two c", k=8, two=2)
                for two in range(2):
                    jobs.append((d3[:, :, two, :], src[:, :, two, jp * 256:(jp + 1) * 256]))
        def plain(dst, src, step=256):
            K = src.shape[0]
            for k0 in range(0, K, step):
                k1 = min(K, k0 + step)
                jobs.append((dst[k0:k1, :], src[k0:k1, :]))
        for f in range(2):
            plain(self.w16_all["ffn_wo"][l, f], self.w32["ffn_wo"][l, f])
        pre = "abc"[mx]
        plain(self.w16_all[pre + "_wqkv"][j], self.w32[pre + "_wqkv"][j])
        plain(self.w16_all[pre + "_wo"][j], self.w32[pre + "_wo"][j])
        plain(self.w16_all["ple_proj"][l], self.w32["ple_proj"][l])
        plain(self.w16_all["ple_gate"][l], self.w32["ple_gate"][l])
        return jobs

    def emit_conv(self, jobs):
        S = self.S
        for (dst, src) in jobs:
            S.dma("pool", dst, src, w=[self.wbuf], key=S.dram_key("w16", 4))

    def load_gvec(self, l, idx, dst, buf):
        src = self.norm_g[idx:idx + 1, :].partition_broadcast(128)
        self.S.dma("sp", dst, src, w=[buf])

    def pre_norm_T(self, src_ap, src_buf, gt, gbuf, xT, xTbuf, col0, R):
        S = self.S
        ss = R["ss"]; ssb = R["ssb"]
        i = R["ni"] = R.get("ni", 0) + 1
        xn = R["xn"][i % 2]; xnb = R["xnb"][i % 2]
        junk = R["junk"]; jb = R["junkb"]
        tb = R["tbank"][i % 2]; tbb = R["tbankb"][i % 2]
        eps = self.eps
        S.op("act", lambda e: e.activation(out=junk, in_=src_ap, func=AF.Square, accum_out=ss[:, 0:1]), r=[src_buf], w=[jb, ssb])
        S.op("act", lambda e: e.activation(out=ss[:, 1:2], in_=ss[:, 0:1], func=AF.Sqrt, scale=1.0 / D, bias=eps), r=[ssb], w=[ssb])
        S.op("dve", lambda e: e.reciprocal(out=ss[:, 2:3], in_=ss[:, 1:2]), r=[ssb], w=[ssb])
        S.op("dve", lambda e: e.scalar_tensor_tensor(out=xn, in0=src_ap, scalar=ss[:, 2:3], in1=gt, op0=ALU.mult, op1=ALU.mult),
             r=[src_buf, ssb, gbuf], w=[xnb])
        tv = tb.bitcast(BF16)
        for k in range(8):
            S.op("pe", lambda e, k=k: e.transpose(out=tv[:, k * 128:(k + 1) * 128], in_=xn[:, k * 128:(k + 1) * 128], identity=self.ident),
                 r=[xnb, self.cb], w=[tbb])
        S.op("act", lambda e: e.copy(out=xT[:, :, col0:col0 + 128], in_=tv.rearrange("p (k t) -> p k t", k=8)), r=[tbb], w=[xTbuf])

    def post_norm_res(self, y_ap, ybuf, gt, gbuf, coef, h_ap, hbuf, R):
        S = self.S
        ybl = ybuf if isinstance(ybuf, list) else [ybuf]
        ss = R["ss2"]; ssb = R["ss2b"]
        junk = R["junk"]; jb = R["junkb"]
        tmp = R["tmp"]; tmpb = R["tmpb"]
        eps = self.eps
        S.op("act", lambda e: e.activation(out=junk, in_=y_ap, func=AF.Square, accum_out=ss[:, 0:1]), r=ybl, w=[jb, ssb])
        S.op("act", lambda e: e.activation(out=ss[:, 1:2], in_=ss[:, 0:1], func=AF.Sqrt, scale=1.0 / D, bias=eps), r=[ssb], w=[ssb])
        S.op("dve", lambda e: e.reciprocal(out=ss[:, 2:3], in_=ss[:, 1:2]), r=[ssb], w=[ssb])
        S.op("dve", lambda e: e.scalar_tensor_tensor(out=tmp, in0=y_ap, scalar=ss[:, 2:3], in1=gt, op0=ALU.mult, op1=ALU.mult),
             r=ybl + [ssb, gbuf], w=[tmpb])
        S.op("pool", lambda e: e.tensor_tensor(out=h_ap, in0=tmp, in1=h_ap, op=ALU.add), r=[tmpb, hbuf], w=[hbuf])

    def alloc_common(self):
        A = self.A
        R = {}
        R["ss"] = A.f32(4); R["ssb"] = Buf("ss")
        R["ss2"] = A.f32(4); R["ss2b"] = Buf("ss2")
        R["xn"] = [A.bf16(1024), A.bf16(1024)]; R["xnb"] = [Buf("xn0"), Buf("xn1")]
        R["junk"] = A.bf16(1024); R["junkb"] = Buf("junk")
        R["tmp"] = A.f32(1024); R["tmpb"] = Buf("tmp")
        R["tbank"] = [self.bank(6), self.bank(7)]; R["tbankb"] = [self.pb[6], self.pb[7]]
        return R

    def alloc_ffn(self):
        A = self.A
        F = {}
        F["xT"] = A.bf16(8 * 512).rearrange("p (k t) -> p k t", k=8); F["xTb"] = Buf("xT")
        F["wi"] = [A.bf16(8 * 512).rearrange("p (k two c) -> p k two c", k=8, two=2) for _ in range(3)]
        F["wib"] = [Buf("wi%d" % i) for i in range(3)]
        F["act"] = A.bf16(NJ * 512).rearrange("p (j t) -> p j t", j=NJ); F["actb"] = Buf("act")
        F["wo"] = A.bf16(NJ * 1024).rearrange("p (j n) -> p j n", j=NJ); F["wob"] = [Buf("wo%d" % j) for j in range(NJ)]
        F["sg"] = [A.f32(512), A.f32(512)]; F["sgb"] = [Buf("sg0"), Buf("sg1")]
        F["gpre"] = A.f32(1024); F["gpreb"] = Buf("gpre")
        F["gpost"] = A.f32(1024); F["gpostb"] = Buf("gpost")
        F["wi_i"] = 0
        F["gu_i"] = 0
        return F

    def ffn_load_resident(self, F, l, f, gi_pre, gi_post):
        S = self.S
        wo16 = self.wo_l[f]
        for j in range(NJ):
            S.dma("sp", F["wo"][:, j, :], wo16[j * 128:(j + 1) * 128, :], r=[self.wbuf], w=[F["wob"][j]], key="d_wores%d" % (j % 4))
        self.load_gvec(l, gi_pre, F["gpre"], F["gpreb"])
        self.load_gvec(l, gi_post, F["gpost"], F["gpostb"])
        self.scale_gvec(F["gpost"], F["gpostb"], 0.5)

    def ffn_tile(self, F, R, l, f, ht, hbuf):
        S = self.S
        xT = F["xT"]; xTb = F["xTb"]
        for s in range(4):
            self.pre_norm_T(ht[:, s, :], hbuf, F["gpre"], F["gpreb"], xT, xTb, s * 128, R)
        act = F["act"]; actb = F["actb"]
        for jp in range(NJ // 2):
            wi_i = F["wi_i"]; F["wi_i"] += 1
            ws = F["wi"][wi_i % 3]; wsb = F["wib"][wi_i % 3]
            S.dma("sp", ws.rearrange("p k two c -> p (k two c)"), self.wi_l[f][jp], r=[self.wbuf], w=[wsb])
            for jj in range(2):
                j = jp * 2 + jj
                gi = F["gu_i"]; F["gu_i"] += 1
                pg = self.bank((gi % 2) * 2); pu = self.bank((gi % 2) * 2 + 1)
                pgb = self.pb[(gi % 2) * 2]
                pub = self.pb[(gi % 2) * 2 + 1]
                sg = F["sg"][gi % 2]; sgb = F["sgb"][gi % 2]
                for k in range(8):
                    S.op("pe", lambda e, k=k, ws=ws, jj=jj, pg=pg: e.matmul(pg, lhsT=ws[:, k, 0, jj * 128:(jj + 1) * 128], rhs=xT[:, k, :],
                                                                      start=(k == 0), stop=(k == 7)), r=[wsb, xTb], w=[pgb])
                for k in range(8):
                    S.op("pe", lambda e, k=k, ws=ws, jj=jj, pu=pu: e.matmul(pu, lhsT=ws[:, k, 1, jj * 128:(jj + 1) * 128], rhs=xT[:, k, :],
                                                                      start=(k == 0), stop=(k == 7)), r=[wsb, xTb], w=[pub])
                S.op("act", lambda e, sg=sg, pg=pg: e.activation(out=sg, in_=pg, func=AF.Silu), r=[pgb], w=[sgb])
                S.op("dve", lambda e, sg=sg, pu=pu, j=j: e.tensor_tensor(out=act[:, j, :], in0=sg, in1=pu, op=ALU.mult), r=[sgb, pub], w=[actb])
        for s in range(4):
            b0 = (4, 0, 2)[s % 3]
            y = self.ps[:, b0 * 512:(b0 + 2) * 512]
            for dh in range(2):
                for j in range(NJ):
                    S.op("pe", lambda e, j=j, s=s, dh=dh, b0=b0: e.matmul(self.ps[:, (b0 + dh) * 512:(b0 + dh + 1) * 512], lhsT=act[:, j, s * 128:(s + 1) * 128],
                                                                        rhs=F["wo"][:, j, dh * 512:(dh + 1) * 512], start=(j == 0), stop=(j == NJ - 1)),
                         r=[actb, F["wob"][j]], w=[self.pb[b0 + dh]])
            self.post_norm_res(y, [self.pb[b0], self.pb[b0 + 1]], F["gpost"], F["gpostb"], 0.5, ht[:, s, :], hbuf, R)

    def qkv_jobs(self):
        mx = self.mx
        W = self.w16["wqkv"].rearrange("(k p) n -> p k n", p=128)
        jobs = []
        vjobs = []
        if mx == 0:
            for g in range(3):
                nm, q, k, v, H, VW = self.srcs[g]
                jobs.append(([(0, g * 512, 512)], [(c * 128, q[c * 128:(c + 1) * 128, :]) for c in range(4)], True))
                jobs.append(([(0, 1536 + g * 512, 512)], [(c * 128, k[c * 128:(c + 1) * 128, :]) for c in range(4)], True))
                vjobs.append((3072 + g * 512, 512, v, 0, 8))
        elif mx == 1:
            nm, q, k, v, H, VW = self.srcs[0]
            for hf in range(2):
                jobs.append(([(0, hf * 512, 512)], [(c * 128, q[(hf * 4 + c) * 128:(hf * 4 + c + 1) * 128, :]) for c in range(4)], True))
            pieces = []
            for kv in range(4):
                pieces += [(kv * 128, 1024 + kv * 64, 64), (kv * 128 + 64, 1024 + kv * 64, 64)]
            jobs.append((pieces, [(kv * 128, k[kv * 128:(kv + 1) * 128, :]) for kv in range(4)], True))
            vjobs.append((1280, 256, v, 0, 4))
        else:
            nm, q, k, v, H, VW = self.srcs[0]
            for hf in range(2):
                jobs.append(([(0, hf * 512, 512)], [(c * 128, q[(hf * 4 + c) * 128:(hf * 4 + c + 1) * 128, :]) for c in range(4)], False))
            for hf in range(2):
                jobs.append(([(0, 1024 + hf * 512, 512)], [(c * 128, k[(hf * 4 + c) * 128:(hf * 4 + c + 1) * 128, :]) for c in range(4)], False))
            for vb in range(2):
                vjobs.append((2048 + vb * 512, 512, v, vb * 528, 8))
        return W, jobs, vjobs

    def phase_pre(self):
        S = self.S; A = self.A
        A.reset()
        R = self.alloc_common()
        F = self.alloc_ffn()
        hts = [A.f32(4 * 1024).rearrange("p (s d) -> p s d", s=4) for _ in range(2)]
        hbs = [Buf("ht0"), Buf("ht1")]
        g2 = A.f32(1024); g2b = Buf("g2")
        wq = [A.bf16(8 * 512).rearrange("p (k n) -> p k n", k=8) for _ in range(2)]; wqb = [Buf("wq0"), Buf("wq1")]
        ct = A.f32(512); st = A.f32(512); ctb = Buf("ct"); stb = Buf("st")
        qb = [A.bf16(512), A.bf16(512)]; qbb = [Buf("qb0"), Buf("qb1")]
        t1 = A.f32(512); t2 = A.f32(512); t1b = Buf("t1"); t2b = Buf("t2")
        qr = [A.bf16(512), A.bf16(512)]; qrb = [Buf("qr0"), Buf("qr1")]
        vsf = [A.bf16(8 * 66) for _ in range(2)]
        vs = [v_.rearrange("p (h d) -> p h d", h=8) for v_ in vsf]; vsb = [Buf("vs0"), Buf("vs1")]
        for i in range(2):
            S.op("dve", lambda e, i=i: e.memset(vsf[i], 1.0), w=[vsb[i]])
        self.ffn_load_resident(F, 0, 0, 0, 1)
        self.load_gvec(0, 2, g2, g2b)
        W, jobs, vjobs = self.qkv_jobs()
        od = Buf("out_dram")
        cnt = {"w": 0, "c": 0, "v": 0}
        for t in range(self.cfg.NT):
            ht = hts[t % 2]; hb = hbs[t % 2]
            tok = slice(t * 512, (t + 1) * 512)
            S.dma("sp", ht, self.h_pre_in[tok, :].rearrange("(s p) d -> p s d", p=128), w=[hb])
            self.ffn_tile(F, R, 0, 0, ht, hb)
            S.dma("pool", self.h_pre_out[tok, :].rearrange("(s p) d -> p s d", p=128), ht, r=[hb], w=[od], key=S.dram_key("h"))
            xT = F["xT"]; xTb = F["xTb"]
            for s4 in range(4):
                self.pre_norm_T(ht[:, s4, :], hb, g2, g2b, xT, xTb, s4 * 128, R)
            if self.mx != 2:
                S.dma("sp", ct, self.rope_c[:, tok], w=[ctb])
                S.dma("sp", st, self.rope_s[:, tok], w=[stb])
            import os
            for (pieces, chunks, rope) in (jobs if not os.environ.get('SKIP_QK') else []):
                wi_ = cnt["w"]; cnt["w"] += 1
                ws = wq[wi_ % 2]; wsb = wqb[wi_ % 2]
                for (sc, src, n) in pieces:
                    S.dma("sp", ws[:, :, sc:sc + n], W[:, :, src:src + n], r=[self.wbuf], w=[wsb])
                for (sc, dest) in chunks:
                    ci = cnt["c"]; cnt["c"] += 1
                    pq = self.bank(ci % 2); pqb = self.pb[ci % 2]
                    for k in range(8):
                        S.op("pe", lambda e, k=k, ws=ws, sc=sc, pq=pq: e.matmul(pq, lhsT=ws[:, k, sc:sc + 128], rhs=xT[:, k, :], start=(k == 0), stop=(k == 7)),
                             r=[wsb, xTb], w=[pqb])
                    q1 = qr[ci % 2]; q1b = qrb[ci % 2]
                    if rope and not os.environ.get('SKIP_ROPE'):
                        b1 = qb[ci % 2]; b1b = qbb[ci % 2]
                        pr = self.bank(2 + ci % 2); prb = self.pb[2 + ci % 2]
                        S.op("act", lambda e, b1=b1, pq=pq: e.copy(out=b1, in_=pq), r=[pqb], w=[b1b])
                        if not os.environ.get('SKIP_RM'):
                            S.op("pe", lambda e, b1=b1, pr=pr: e.matmul(pr, lhsT=self.rmat, rhs=b1, start=True, stop=True), r=[b1b, self.cb], w=[prb])
                        S.op("dve", lambda e, pq=pq: e.tensor_tensor(out=t1, in0=ct, in1=pq, op=ALU.mult), r=[pqb, ctb], w=[t1b])
                        S.op("dve", lambda e, pr=pr: e.tensor_tensor(out=t2, in0=st, in1=pr, op=ALU.mult), r=[prb, stb], w=[t2b])
                        S.op("dve", lambda e, q1=q1: e.tensor_tensor(out=q1, in0=t1, in1=t2, op=ALU.add), r=[t1b, t2b], w=[q1b])
                    else:
                        S.op("act", lambda e, q1=q1, pq=pq: e.copy(out=q1, in_=pq), r=[pqb], w=[q1b])
                    S.dma("pool", dest[:, tok], q1, r=[q1b], w=[od], key=S.dram_key("qk", 3))
            for (src, ncols, vdst, dc0, nh) in (vjobs if not os.environ.get('SKIP_V') else []):
                wi_ = cnt["w"]; cnt["w"] += 1
                ws = wq[wi_ % 2]; wsb = wqb[wi_ % 2]
                S.dma("sp", ws[:, :, 0:ncols], W[:, :, src:src + ncols], r=[self.wbuf], w=[wsb])
                for s4 in range(4):
                    vi = cnt["v"]; cnt["v"] += 1
                    pv = self.bank(4 + vi % 2)[:, 0:ncols]; pvb = self.pb[4 + vi % 2]
                    for k in range(8):
                        S.op("pe", lambda e, k=k, ws=ws, pv=pv, s4=s4, ncols=ncols: e.matmul(pv, lhsT=xT[:, k, s4 * 128:(s4 + 1) * 128], rhs=ws[:, k, 0:ncols],
                                                                                    start=(k == 0), stop=(k == 7)), r=[wsb, xTb], w=[pvb])
                    v1 = vs[vi % 2]; v1b = vsb[vi % 2]
                    S.op("act", lambda e, v1=v1, pv=pv, nh=nh: e.copy(out=v1[:, 0:nh, 0:64], in_=pv.rearrange("p (h d) -> p h d", h=nh)), r=[pvb], w=[v1b])
                    r0 = t * 512 + s4 * 128
                    S.dma("pool", vdst[r0:r0 + 128, dc0:dc0 + nh * 66], v1[:, 0:nh, :].rearrange("p h d -> p (h d)"), r=[v1b], w=[od], key=S.dram_key("v"))
            if self.pending_conv:
                nper = -(-len(self.pending_conv) // max(1, self.cfg.NT - t))
                self.emit_conv(self.pending_conv[:nper])
                self.pending_conv = self.pending_conv[nper:]
        S.barrier()

    def head_plan(self, i, NQ):
        mx = self.mx
        groups = []
        if mx == 0:
            dl = [[-1, 0, 1], [-2, -1, 0, 1, 2], list(range(-8, 9))]
            for hg in range(2):
                heads = []
                for h in range(hg * 4, hg * 4 + 4):
                    tl = []
                    for g in range(3):
                        for d in dl[g]:
                            tl.append((g, h // 2, h % 2, h // 2, h * 66, d, [("m", len(tl))]))
                    heads.append((h, tl))
                groups.append(heads)
        elif mx == 1:
            for hg in range(4):
                heads = []
                for h in range(hg * 4, hg * 4 + 4):
                    tl = [(0, h // 2, h % 2, h // 4, (h // 4) * 66, d, [("m", d + 1)]) for d in (-1, 0, 1)]
                    heads.append((h, tl))
                groups.append(heads)
        else:
            cls = 0
            if i == 0: cls = 1
            elif i == 1: cls = 2
            elif i == NQ - 2: cls = 3
            elif i == NQ - 1: cls = 4
            ds = list(range(-2, 3))
            if i == 0: ds.append(3)
            if i == NQ - 1: ds = [-3] + ds
            for hg in range(4):
                heads = []
                for h in range(hg * 4, hg * 4 + 4):
                    tl = [(0, h // 2, h % 2, h // 2, h * 66, d, [("e", h * 7 + d + 3), ("m", cls * 7 + d + 3)]) for d in ds]
                    heads.append((h, tl))
                groups.append(heads)
        return groups

    def phase_att(self):
        S = self.S; A = self.A
        A.reset()
        R = self.alloc_common()
        T = self.cfg.TOK
        NQ = T // 128
        mx = self.mx
        NM = self.nmask()
        self.msk = A.bf16(NM * 128).rearrange("p (m q) -> p m q", m=NM); mb = Buf("msk")
        for m0 in range(0, NM, 8):
            m1 = min(NM, m0 + 8)
            S.dma("pool", self.msk[:, m0:m1, :], self.masks[m0:m1].rearrange("m p q -> p m q"), w=[mb])
        cbb = Buf("cbias")
        if mx == 2:
            self.cbias = A.bf16(112 * 128).rearrange("p (m q) -> p m q", m=112)
            cbg = self.c_biasg.rearrange("h d p q -> p (h d) q")
            for m0 in range(0, 112, 8):
                S.dma("pool", self.cbias[:, m0:m0 + 8, :], cbg[:, m0:m0 + 8, :], w=[cbb])
            cflat = self.cbias.rearrange("p m q -> p (m q)")
            for c0_ in range(0, 112 * 128, 512):
                S.op("act", lambda e, c0_=c0_, cflat=cflat: e.activation(out=cflat[:, c0_:c0_ + 512], in_=cflat[:, c0_:c0_ + 512], func=AF.Exp), r=[cbb], w=[cbb])
        Fo = (512, 1024, 1024)[mx]
        nco = Fo // 128
        wo = A.bf16(nco * 1024).rearrange("p (c n) -> p c n", c=nco); wob = Buf("wo_mix")
        S.dma("sp", wo, self.w16["wo"].rearrange("(c p) n -> p c n", p=128), r=[self.wbuf], w=[wob])
        g3 = A.f32(1024); g3b = Buf("g3")
        self.load_gvec(0, 3, g3, g3b)
        es = None
        if mx == 1:
            es = A.f32(16); esb = Buf("esink")
            S.dma("sp", es, self.b_sink.partition_broadcast(128), w=[esb])
            S.op("act", lambda e: e.activation(out=es, in_=es, func=AF.Exp), r=[esb], w=[esb])
        fl = A.f32(6); flb = Buf("flags")
        S.dma("sp", fl, self.flags, w=[flb])
        ckmax = max(k.shape[0] for (nm, q, k, v, H, VW) in self.srcs)
        vwmax = max(VW for (nm, q, k, v, H, VW) in self.srcs)
        candk = [A.bf16(ckmax) for _ in range(3)]; candv = [A.bf16(vwmax) for _ in range(3)]
        candb = [Buf("cand%d" % j) for j in range(3)]
        recv_flat = self.recv.rearrange("r c -> (r c)")
        rings = []
        for si_, (nm, q, k, v, H, VW) in enumerate(self.srcs):
            nkc = k.shape[0] // 128
            nqc = q.shape[0] // 128
            Hb = H // 128
            dmax = Hb if mx != 2 else 2
            rs = 2 * dmax + 1 + 2
            kslots = [A.bf16(nkc * 128).rearrange("p (c t) -> p c t", c=nkc) for _ in range(rs)]
            vslots = [A.bf16(VW) for _ in range(rs)]
            kb = [Buf("kv_%s_%d" % (nm, j)) for j in range(rs)]
            vb = kb
            qs = [A.bf16(nqc * 128).rearrange("p (c t) -> p c t", c=nqc) for _ in range(2)]
            qbf = [Buf("q_%d_%d" % (si_, j)) for j in range(2)]
            rings.append(dict(q=q.rearrange("(c p) t -> p c t", p=128), k=k.rearrange("(c p) t -> p c t", p=128), v=v, Hb=Hb, dmax=dmax, rs=rs,
                              ks=kslots, vs=vslots, kb=kb, vb=vb, qs=qs, qb=qbf, loaded=-1, NKT=(T + 2 * H) // 128,
                              nm=nm, si=si_, H=H, VW=VW, Fk=k.shape[0], nkc=nkc))

        def load_kv(rg, kt, sl):
            Hb_ = rg["Hb"]; H_ = rg["H"]; VW_ = rg["VW"]; Fk_ = rg["Fk"]; nkc_ = rg["nkc"]
            key = "d_ring_%d_%d" % (rg["si"], sl % 5)
            kdst = rg["ks"][sl]; vdst = rg["vs"][sl]; sb_ = rg["kb"][sl]
            if Hb_ <= kt < Hb_ + NQ:
                j = kt - Hb_
                S.dma("sp", kdst, rg["k"][:, :, j * 128:(j + 1) * 128], w=[sb_], key=key)
                S.dma("sp", vdst, rg["v"][j * 128:(j + 1) * 128, :], w=[sb_], key=key)
                return
            left = kt < Hb_
            j = kt if left else kt - Hb_ - NQ
            side = "R" if left else "L"
            blocks = (0, 1, 2) if left else (1, 2, 3)
            f0 = 0 if left else 3
            for ci, bj in enumerate(blocks):
                ck = candk[ci][:, 0:nkc_ * 128].rearrange("p (c t) -> p c t", c=nkc_)
                cv = candv[ci][:, 0:VW_]
                for c in range(nkc_):
                    g, og, n = self.unit[(rg["si"], "K", side, c)]
                    r0, nr = self.colls[g]
                    o = (4 * r0 + bj * nr) * 512 + og
                    S.dma("sp", ck[:, c, :], recv_flat[o:o + n].rearrange("(p t) -> p t", t=H_)[:, j * 128:(j + 1) * 128], w=[candb[ci]])
                g, og, n = self.unit[(rg["si"], "V", side, j)]
                r0, nr = self.colls[g]
                o = (4 * r0 + bj * nr) * 512 + og
                S.dma("sp", cv, recv_flat[o:o + n].rearrange("(t w) -> t w", w=VW_), w=[candb[ci]])
                fcol = fl[:, f0 + ci:f0 + ci + 1]
                if ci == 0:
                    S.op("dve", lambda e, o_=kdst, i=ck, f=fcol: e.tensor_scalar(out=o_, in0=i, scalar1=f, scalar2=None, op0=ALU.mult), r=[candb[ci], flb], w=[sb_])
                    S.op("dve", lambda e, o_=vdst, i=cv, f=fcol: e.tensor_scalar(out=o_, in0=i, scalar1=f, scalar2=None, op0=ALU.mult), r=[candb[ci], flb], w=[sb_])
                else:
                    S.op("dve", lambda e, o_=kdst, i=ck, f=fcol: e.scalar_tensor_tensor(out=o_, in0=i, scalar=f, in1=o_, op0=ALU.mult, op1=ALU.add), r=[candb[ci], flb], w=[sb_])
                    S.op("dve", lambda e, o_=vdst, i=cv, f=fcol: e.scalar_tensor_tensor(out=o_, in0=i, scalar=f, in1=o_, op0=ALU.mult, op1=ALU.add), r=[candb[ci], flb], w=[sb_])

        pts = [A.bf16(512) for _ in range(4)]; ptb = [Buf("pt%d" % j) for j in range(4)]
        ob = [A.bf16(Fo) for _ in range(2)]; obb = [Buf("o%d" % j) for j in range(2)]
        oT = A.bf16(nco * 128).rearrange("p (c t) -> p c t", c=nco); oTb = Buf("oT")
        rd = A.f32(8); rdb = Buf("rd")
        hts = [A.f32(1024) for _ in range(3)]; hbs = [Buf("hq0"), Buf("hq1"), Buf("hq2")]
        hd = Buf("hscr")
        mflat = self.msk.rearrange("p m q -> p (m q)")
        eflat = self.cbias.rearrange("p m q -> p (m q)") if mx == 2 else None
        sc_i = 0; pt_i = 0; og_i = 0
        pending_pv = []
        pending_fin = []
        for i in range(NQ):
            ht = hts[i % 3]; hb = hbs[i % 3]
            for rg in rings:
                upto = min(rg["NKT"] - 1, i + rg["Hb"] + rg["dmax"] + 1)
                while rg["loaded"] < upto:
                    kt = rg["loaded"] + 1
                    sl = kt % rg["rs"]
                    load_kv(rg, kt, sl)
                    rg["loaded"] = kt
                S.dma("sp", rg["qs"][i % 2], rg["q"][:, :, i * 128:(i + 1) * 128], w=[rg["qb"][i % 2]])
            S.dma("sp", ht, self.h_pre_out[i * 128:(i + 1) * 128, :], w=[hb])
            o1 = ob[i % 2]; o1b = obb[i % 2]
            for hg_idx, heads in enumerate(self.head_plan(i, NQ)):
                og = og_i; og_i += 1
                orb = self.pb[3 + og % 2]
                oreg = self.bank(3 + og % 2)
                for hh, (h, tl) in enumerate(heads):
                    ocol = hh * 65
                    nt = len(tl)
                    for c0 in range(0, nt, 4):
                        grp = tl[c0:c0 + 4]
                        sb_ = sc_i % 3; sc_i += 1
                        sbank = self.bank(sb_); sbb = self.pb[sb_]
                        reads_kv = []
                        for j, (si, qc, half, kc, vcol, d, mts) in enumerate(grp):
                            rg = rings[si]
                            kt = i + rg["Hb"] + d
                            sl = kt % rg["rs"]
                            hp = slice(half * 64, half * 64 + 64)
                            kap = rg["ks"][sl][hp, kc, :]
                            qap = rg["qs"][i % 2][hp, qc, :]
                            dst = sbank[:, j * 128:(j + 1) * 128]
                            S.op("pe", lambda e, dst=dst, kap=kap, qap=qap: e.matmul(dst, lhsT=kap, rhs=qap, start=True, stop=True),
                                 r=[rg["kb"][sl], rg["qb"][i % 2]], w=[sbb])
                        n = len(grp)
                        pi = pt_i % 4; pt_i += 1
                        pt = pts[pi]; ptbuf = ptb[pi]
                        S.op("act", lambda e, pt=pt, sbank=sbank, n=n: e.activation(out=pt[:, 0:n * 128], in_=sbank[:, 0:n * 128], func=AF.Exp, scale=0.125),
                             r=[sbb], w=[ptbuf])
                        for fi in range(len(grp[0][6])):
                            nm_, i0 = grp[0][6][fi]
                            for jj_, tt_ in enumerate(grp):
                                assert tt_[6][fi] == (nm_, i0 + jj_)
                            strip = mflat if nm_ == "m" else eflat
                            S.op("dve", lambda e, pt=pt, n=n, strip=strip, i0=i0: e.tensor_tensor(out=pt[:, 0:n * 128], in0=pt[:, 0:n * 128],
                                                                                              in1=strip[:, i0 * 128:(i0 + n) * 128], op=ALU.mult),
                                 r=[ptbuf, mb, cbb], w=[ptbuf])
                        def emit_pv(grp=grp, pt=pt, ptbuf=ptbuf, c0=c0, nt=nt, oreg=oreg, ocol=ocol, orb=orb):
                            for j, (si, qc, half, kc, vcol, d, mts) in enumerate(grp):
                                rg = rings[si]
                                kt = i + rg["Hb"] + d
                                sl = kt % rg["rs"]
                                first = (c0 + j == 0); last = (c0 + j == nt - 1)
                                S.op("pe", lambda e, pt=pt, j=j, vap=rg["vs"][sl][:, vcol:vcol + 65], oreg=oreg, ocol=ocol, first=first, last=last:
                                     e.matmul(oreg[:, ocol:ocol + 65], lhsT=pt[:, j * 128:(j + 1) * 128], rhs=vap, start=first, stop=last),
                                     r=[ptbuf, rg["vb"][sl]], w=[orb])
                        if len(pending_pv) >= 2:
                            pending_pv.pop(0)()
                        pending_pv.append(emit_pv)
                while pending_pv:
                    pending_pv.pop(0)()
                nh = len(heads)
                den = oreg[:, 0:nh * 65].rearrange("p (h d) -> p h d", h=nh)[:, :, 64]
                if es is not None:
                    h0 = heads[0][0]
                    S.op("dve", lambda e, den=den, h0=h0, nh=nh: e.tensor_tensor(out=rd[:, 0:nh], in0=den, in1=es[:, h0:h0 + nh], op=ALU.add), r=[orb, esb], w=[rdb])
                    S.op("dve", lambda e, nh=nh: e.reciprocal(out=rd[:, 0:nh], in_=rd[:, 0:nh]), r=[rdb], w=[rdb])
                else:
                    S.op("dve", lambda e, den=den, nh=nh: e.reciprocal(out=rd[:, 0:nh], in_=den), r=[orb], w=[rdb])
                for hh, (h, tl) in enumerate(heads):
                    S.op("dve", lambda e, hh=hh, h=h, oreg=oreg, o1=o1: e.tensor_scalar(out=o1[:, h * 64:(h + 1) * 64], in0=oreg[:, hh * 65:hh * 65 + 64],
                                                                                 scalar1=rd[:, hh:hh + 1], scalar2=None, op0=ALU.mult), r=[orb, rdb], w=[o1b])
                if pending_fin and hg_idx < 2:
                    pending_fin.pop(0)()
            def fin_a(o1=o1, o1b=o1b):
                tv = self.bank(5).bitcast(BF16)
                for c in range(nco):
                    S.op("pe", lambda e, c=c, o1=o1, tv=tv: e.transpose(out=tv[:, c * 128:(c + 1) * 128], in_=o1[:, c * 128:(c + 1) * 128], identity=self.ident),
                         r=[o1b, self.cb], w=[self.pb[5]])
                S.op("act", lambda e, tv=tv: e.copy(out=oT, in_=tv[:, 0:nco * 128].rearrange("p (c t) -> p c t", c=nco)), r=[self.pb[5]], w=[oTb])

            def fin_b(i=i, ht=ht, hb=hb):
                for dh in range(2):
                    for c in range(nco):
                        S.op("pe", lambda e, c=c, dh=dh: e.matmul(self.bank(6 + dh), lhsT=oT[:, c, :], rhs=wo[:, c, dh * 512:(dh + 1) * 512], start=(c == 0), stop=(c == nco - 1)),
                             r=[oTb, wob], w=[self.pb[6 + dh]])
                self.post_norm_res(self.ps[:, 6 * 512:8 * 512], [self.pb[6], self.pb[7]], g3, g3b, 1.0, ht, hb, R)
                S.dma("pool", self.h_att_out[i * 128:(i + 1) * 128, :], ht, r=[hb], w=[hd], key=S.dram_key("hs"))
            pending_fin.extend([fin_a, fin_b])
        while pending_fin:
            pending_fin.pop(0)()
        S.barrier()

    def phase_mid(self):
        S = self.S; A = self.A
        A.reset()
        R = self.alloc_common()
        F = self.alloc_ffn()
        hts = [A.f32(4 * 1024).rearrange("p (s d) -> p s d", s=4) for _ in range(2)]
        hbs = [Buf("ht0"), Buf("ht1")]
        wg = A.bf16(8 * 1024).rearrange("p (k n) -> p k n", k=8); wgb = Buf("wgate")
        wp = A.bf16(2 * 1024).rearrange("p (k n) -> p k n", k=2); wpb = Buf("wproj")
        g6 = A.f32(1024); g6b = Buf("g6"); g7 = A.f32(1024); g7b = Buf("g7")
        sgt = A.f32(1024); sgtb = Buf("sgt")
        pt_ = A.f32(256); ptb_ = Buf("ptile")
        pbf = A.bf16(256); pbfb = Buf("pbf")
        pT = A.bf16(256).rearrange("p (k t) -> p k t", k=2); pTb = Buf("pT")
        S.dma("sp", wg, self.w16["ple_gate"].rearrange("(k p) n -> p k n", p=128), r=[self.wbuf], w=[wgb])
        S.dma("sp", wp, self.w16["ple_proj"].rearrange("(k p) n -> p k n", p=128), r=[self.wbuf], w=[wpb])
        self.ffn_load_resident(F, 0, 1, 4, 5)
        self.load_gvec(0, 6, g6, g6b)
        self.load_gvec(0, 7, g7, g7b)
        od = Buf("out_dram")
        xT = F["xT"]; xTb = F["xTb"]
        eg = R["tmp"]
        for t in range(self.cfg.NT):
            ht = hts[t % 2]; hb = hbs[t % 2]
            tok = slice(t * 512, (t + 1) * 512)
            S.dma("sp", ht, self.h_att_out[tok, :].rearrange("(s p) d -> p s d", p=128), w=[hb])
            self.ffn_tile(F, R, 0, 1, ht, hb)
            for s4 in range(4):
                r0 = t * 512 + s4 * 128
                self.pre_norm_T(ht[:, s4, :], hb, g6, g6b, xT, xTb, s4 * 128, R)
                for dh in range(2):
                    for k in range(8):
                        S.op("pe", lambda e, k=k, dh=dh, s4=s4: e.matmul(self.bank(4 + dh), lhsT=xT[:, k, s4 * 128:(s4 + 1) * 128], rhs=wg[:, k, dh * 512:(dh + 1) * 512],
                                                                      start=(k == 0), stop=(k == 7)), r=[xTb, wgb], w=[self.pb[4 + dh]])
                S.op("act", lambda e: e.activation(out=sgt, in_=self.ps[:, 4 * 512:6 * 512], func=AF.Sigmoid), r=[self.pb[4], self.pb[5]], w=[sgtb])
                S.dma("sp", pt_, self.p[r0:r0 + 128, :], w=[ptb_])
                S.op("dve", lambda e: e.tensor_copy(out=pbf, in_=pt_), r=[ptb_], w=[pbfb])
                tv = self.bank(6).bitcast(BF16)
                for k in range(2):
                    S.op("pe", lambda e, k=k, tv=tv: e.transpose(out=tv[:, k * 128:(k + 1) * 128], in_=pbf[:, k * 128:(k + 1) * 128], identity=self.ident),
                         r=[pbfb, self.cb], w=[self.pb[6]])
                S.op("act", lambda e, tv=tv: e.copy(out=pT, in_=tv[:, 0:256].rearrange("p (k t) -> p k t", k=2)), r=[self.pb[6]], w=[pTb])
                for dh in range(2):
                    for k in range(2):
                        S.op("pe", lambda e, k=k, dh=dh: e.matmul(self.bank(dh), lhsT=pT[:, k, :], rhs=wp[:, k, dh * 512:(dh + 1) * 512], start=(k == 0), stop=(k == 1)),
                             r=[pTb, wpb], w=[self.pb[dh]])
                S.op("dve", lambda e: e.tensor_tensor(out=sgt, in0=self.ps[:, 0:1024], in1=sgt, op=ALU.mult), r=[self.pb[0], self.pb[1], sgtb], w=[sgtb])
                self.post_norm_res(sgt, sgtb, g7, g7b, 1.0, ht[:, s4, :], hb, R)
            S.dma("pool", self.h_mid_out[tok, :].rearrange("(s p) d -> p s d", p=128), ht, r=[hb], w=[od], key=S.dram_key("h"))
        S.barrier()

    def scale_gvec(self, gt, gb, coef):
        self.S.op("dve", lambda e: e.tensor_scalar(out=gt, in0=gt, scalar1=float(coef), scalar2=None, op0=ALU.mult), r=[gb], w=[gb])

    def phase_exchange(self):
        S = self.S
        T = self.cfg.TOK
        send_flat = self.send.rearrange("r c -> (r c)")
        sb = Buf("sendbuf"); rb = Buf("recvbuf")
        for (si, kind, side, idx), (g, og, n) in self.unit.items():
            nm, q, k, v, H, VW = self.srcs[si]
            o = self.colls[g][0] * 512 + og
            t0 = 0 if side == "L" else T - H
            if kind == "K":
                src = k[idx * 128:(idx + 1) * 128, t0:t0 + H]
                dst = send_flat[o:o + n].rearrange("(p t) -> p t", t=H)
            else:
                src = v[t0 + idx * 128:t0 + (idx + 1) * 128, :]
                dst = send_flat[o:o + n].rearrange("(t w) -> t w", w=VW)
            S.dma("sp", dst, src, w=[sb], key=S.dram_key("xs", 2))
        for (r0, nr) in self.colls:
            S.custom("pool", lambda e, r0=r0, nr=nr: e.collective_compute("AllGather", ALU.bypass, replica_groups=[[0, 1, 2, 3], [4, 5, 6, 7]],
                                                                     ins=[self.send[r0:r0 + nr, :]], outs=[self.recv[4 * r0:4 * r0 + 4 * nr, :]]),
                     r=[sb], w=[rb], key="cc", inc=1)
        S.barrier()

    def emit_all(self):
        self.emit_consts()
        self.wbuf = Buf("w16")
        self.emit_conv(self.conv_jobs(0))
        self.S.barrier()
        import os
        nstop = int(os.environ.get("FUSE_STOP", "999"))
        n = 0
        for l in range(self.cfg.DEPTH):
            self.set_layer(l)
            self.pending_conv = self.conv_jobs(l + 1) if l + 1 < self.cfg.DEPTH else []
            for ph in (self.phase_pre, self.phase_exchange, self.phase_att, self.phase_mid):
                if n < nstop:
                    ph()
                n += 1
            self.emit_conv(self.pending_conv)
            self.pending_conv = []
        self.S.barrier()


_PROG_CACHE = {}


def _get_prog(seq, depth):
    key = (seq, depth)
    if key not in _PROG_CACHE:
        P = Prog(Cfg(seq=seq, depth=depth))
        nc = P.build()
        _PROG_CACHE[key] = (P, nc)
    return _PROG_CACHE[key]


def _rope_tables(T, pos0):
    pos = (pos0 + np.arange(T)).astype(np.float32)
    inv = (np.float32(500000.0) ** (-np.arange(0, 16, 2, dtype=np.float32) / np.float32(16))).astype(np.float32)
    ang = pos[None, :] * inv[:, None]
    c = np.cos(ang).astype(np.float32); s = np.sin(ang).astype(np.float32)
    C = np.ones((128, T), np.float32); Sg = np.zeros((128, T), np.float32)
    for p in range(128):
        f = p % 64
        if f < 8:
            C[p] = c[f]; Sg[p] = -s[f]
        elif f < 16:
            C[p] = c[f - 8]; Sg[p] = s[f - 8]
    return C, Sg


def _masks(mx, rank, T, seq):
    kk = np.arange(128)[:, None]; qq = np.arange(128)[None, :]
    def tile(valid):
        return np.where(valid, 1.0, 0.0).astype(np.float32)
    out = []
    if mx == 0:
        for d in (-1, 0, 1):
            rel = 128 * d + kk - qq
            out.append(tile(np.abs(rel) <= 64))
        for d in range(-2, 3):
            rel = 128 * d + kk - qq
            out.append(tile((rel % 4 == 0) & (np.abs(rel) <= 256)))
        for d in range(-8, 9):
            rel = 128 * d + kk - qq
            out.append(tile((rel % 16 == 0) & (np.abs(rel) <= 1024)))
    elif mx == 1:
        for d in (-1, 0, 1):
            rel = 128 * d + kk - qq
            out.append(tile(np.abs(rel) <= 128))
    else:
        NQ = T // 128
        rows_total = seq // 64
        row0 = (rank % 4) * (T // 64)
        kr = kk // 64; kc = kk % 64; qr = qq // 64; qc = qq % 64
        cstart = np.clip(qc - 8, 0, 64 - 16)
        colv = (kc >= cstart) & (kc < cstart + 16)
        for i in (2, 0, 1, NQ - 2, NQ - 1):
            for d in range(-3, 4):
                Rq = row0 + 2 * i + qr
                Rk = row0 + 2 * (i + d) + kr
                rs = np.clip(Rq - 4, 0, rows_total - 8)
                rowv = (Rk >= rs) & (Rk < rs + 8)
                out.append(tile(rowv & colv))
    return np.stack(out, 0)


def _c_bias_gather(rpb):
    kk = np.arange(128)[:, None]; qq = np.arange(128)[None, :]
    kr = kk // 64; kc = kk % 64; qr = qq // 64; qc = qq % 64
    out = np.zeros((16, 7, 128, 128), np.float32)
    for d in range(-3, 4):
        dr = 2 * d + kr - qr
        dc = kc - qc
        ok = (np.abs(dr) <= 7) & (np.abs(dc) <= 15)
        g = rpb[:, np.clip(dr + 7, 0, 14), np.clip(dc + 15, 0, 30)]
        out[:, d + 3] = np.where(ok[None], g, 0.0)
    return out


def kernel(x, p, norm_g, ffn_wi, ffn_wo, ple_proj, ple_gate, a_wqkv, a_wo, b_wqkv, b_wo, b_sink, c_wqkv, c_wo, c_rpb):
    x = np.asarray(x, np.float32)
    B, SEQ, _ = x.shape
    depth = np.asarray(norm_g).shape[0]
    T = SEQ * B // 8
    NC = 8
    f = lambda a: np.ascontiguousarray(np.asarray(a, np.float32))
    p = f(p)
    n_b = len(range(1, depth, 3)); n_c = len(range(2, depth, 3))
    shared = {"norm_g": f(norm_g), "ffn_wi": f(ffn_wi), "ffn_wo": f(ffn_wo), "ple_proj": f(ple_proj), "ple_gate": f(ple_gate),
              "a_wqkv": f(a_wqkv), "a_wo": f(a_wo)}
    if n_b:
        shared.update({"b_wqkv": f(b_wqkv)[:n_b], "b_wo": f(b_wo)[:n_b], "b_sink": f(b_sink)[:n_b]})
    if n_c:
        shared.update({"c_wqkv": f(c_wqkv)[:n_c], "c_wo": f(c_wo)[:n_c],
                       "c_biasg": np.stack([_c_bias_gather(f(c_rpb)[j]) for j in range(n_c)], 0)})
    xf = x.reshape(B * SEQ, D)
    pf = p.reshape(depth, B * SEQ, PLE)
    P, nc = _get_prog(SEQ, depth)
    in_maps = []
    for r in range(NC):
        q = r % 4
        m = dict(shared)
        m["x"] = np.ascontiguousarray(xf[r * T:(r + 1) * T])
        m["p"] = np.ascontiguousarray(pf[:, r * T:(r + 1) * T])
        C_, S_ = _rope_tables(T, q * T)
        m["rope_c"] = C_; m["rope_s"] = S_
        fl = np.zeros((128, 6), np.float32)
        if q >= 1:
            fl[:, q - 1] = 1.0
        if q <= 2:
            fl[:, 3 + q] = 1.0
        m["flags"] = fl
        m["masks_a"] = _masks(0, r, T, SEQ)
        if n_b:
            m["masks_b"] = _masks(1, r, T, SEQ)
        if n_c:
            m["masks_c"] = _masks(2, r, T, SEQ)
        in_maps.append(m)
    res = run_bass_kernel_spmd(nc, in_maps, core_ids=list(range(NC))).results
    out = np.concatenate([np.asarray(res[r]["out"]) for r in range(NC)], 0).reshape(B, SEQ, D).astype(np.float32)
    return out
```

```python
import math
import numpy as np
import concourse.bass as bass
import concourse.mybir as mybir
from concourse.bass_utils import run_bass_kernel_spmd

F32 = mybir.dt.float32
BF16 = mybir.dt.bfloat16
ALU = mybir.AluOpType
AF = mybir.ActivationFunctionType

D = 1024
DFF = 2816
NJ = DFF // 128
PLE = 256
NEG = -30000.0
ENGS = ["pe", "act", "dve", "pool", "sp"]


class Buf:
    __slots__ = ("name", "excl")

    def __init__(self, name, excl=False):
        self.name = name
        self.excl = excl


class Sched:
    NEPOCH = 5

    def __init__(self):
        self.ops = {e: [] for e in ENGS}
        self.semval = {}
        self.waited = {e: {} for e in ENGS}
        self.lastw = {}
        self.readers = {}
        self.dram_rr = {}
        self.epoch = 0

    def ekey(self, eng):
        return "%s@%d" % (eng, (self.epoch % self.NEPOCH) if eng == "pe" else 0)

    def _deps(self, eng, reads, writes):
        toks = []
        for b in reads:
            t = self.lastw.get(b)
            if t is not None:
                toks.append(t)
            if b.excl:
                toks.extend(tk for tk in self.readers.get(b, ()) if tk[0].split("@")[0] != eng)
        for b in writes:
            t = self.lastw.get(b)
            if t is not None:
                toks.append(t)
            toks.extend(self.readers.get(b, ()))
        need = {}
        for (k, v) in toks:
            if eng == "pe" and k.startswith("pe@"):
                continue
            if need.get(k, 0) < v:
                need[k] = v
        out = []
        for k, v in need.items():
            if self.waited[eng].get(k, 0) < v:
                self.waited[eng][k] = v
                out.append((k, v))
        return out

    def _commit(self, tok, reads, writes):
        for b in writes:
            self.lastw[b] = tok
            self.readers[b] = []
        for b in reads:
            self.readers.setdefault(b, []).append(tok)

    def op(self, eng, fn, r=(), w=()):
        waits = self._deps(eng, r, w)
        k = self.ekey(eng)
        v = self.semval.get(k, 0) + 1
        self.semval[k] = v
        tok = (k, v)
        self.ops[eng].append((waits, fn, k, 1))
        self._commit(tok, r, w)
        return tok

    def custom(self, eng, fn, r=(), w=(), key=None, inc=1):
        waits = self._deps(eng, r, w)
        prev = self.semval.get(key, 0)
        if prev and self.waited[eng].get(key, 0) < prev:
            self.waited[eng][key] = prev
            waits.append((key, prev))
        self.semval[key] = prev + inc
        tok = (key, prev + inc)
        self.ops[eng].append((waits, fn, key, inc))
        self._commit(tok, r, w)
        return tok

    def dma(self, eng, out_ap, in_ap, r=(), w=(), key=None):
        if key is None:
            key = "d_" + w[0].name
        return self.custom(eng, (lambda e, o=out_ap, i=in_ap: e.dma_start(out=o, in_=i)), r=r, w=w, key=key, inc=16)

    def dram_key(self, name, n=2):
        i = self.dram_rr.get(name, 0)
        self.dram_rr[name] = i + 1
        return "d_%s_%d" % (name, i % n)

    def barrier(self):
        allv = dict(self.semval)
        for e in ENGS:
            waits = []
            for k, v in allv.items():
                if v and self.waited[e].get(k, 0) < v:
                    self.waited[e][k] = v
                    waits.append((k, v))
            if waits:
                self.ops[e].append((waits, None, None, 0))
        self.lastw.clear()
        self.readers.clear()
        self.epoch += 1

    def keys(self):
        return sorted(self.semval.keys())

    def replay(self, eng, handle, sems):
        for (waits, fn, inc_key, inc) in self.ops[eng]:
            for (k, v) in waits:
                handle.wait_ge(sems[k], v)
            if fn is not None:
                ins = fn(handle)
                ins.then_inc(sems[inc_key], inc)


class Arena:
    def __init__(self, t, nwords):
        self.t = t
        self.n = nwords
        self.off = 0

    def reset(self):
        self.off = 0

    def f32(self, n):
        a = self.off
        self.off += n
        assert self.off <= self.n, "SBUF arena overflow %d > %d" % (self.off, self.n)
        return self.t[:, a:a + n]

    def bf16(self, n):
        w = (n + 1) // 2
        a = self.off
        self.off += w
        assert self.off <= self.n, "SBUF arena overflow %d > %d" % (self.off, self.n)
        return self.t[:, a:a + w].bitcast(BF16)


class Cfg:
    def __init__(self, seq=16384, depth=4):
        self.SEQ = seq
        self.DEPTH = depth
        self.TOK = seq * 2 // 8
        self.NT = self.TOK // 512
        self.ROWS = self.TOK // 64


MIXER_OF = lambda l: l % 3


class Prog:
    def __init__(self, cfg, dbg=None):
        self.cfg = cfg
        self.dbg = dbg or []
        self.S = Sched()
        self.pb = [Buf("bank%d" % i, excl=True) for i in range(8)]
        self.nc = bass.Bass("TRN2", target_bir_lowering=False)
        self.dram = {}

    def dt(self, name, shape, dtype, kind="Internal"):
        if name in self.dbg and kind == "Internal":
            kind = "ExternalOutput"
        t = self.nc.dram_tensor(name, list(shape), dtype, kind=kind).ap()
        self.dram[name] = t
        return t

    def declare(self):
        c = self.cfg
        T = c.TOK
        L = c.DEPTH
        n_a = len(range(0, L, 3)); n_b = len(range(1, L, 3)); n_c = len(range(2, L, 3))
        self.x_in = self.dt("x", [T, D], F32, "ExternalInput")
        self.p_all = self.dt("p", [L, T, PLE], F32, "ExternalInput")
        self.norm_g_all = self.dt("norm_g", [L, 8, D], F32, "ExternalInput")
        self.rope_c = self.dt("rope_c", [128, T], F32, "ExternalInput")
        self.rope_s = self.dt("rope_s", [128, T], F32, "ExternalInput")
        self.flags = self.dt("flags", [128, 6], F32, "ExternalInput")
        self.masks_all = {0: self.dt("masks_a", [25, 128, 128], F32, "ExternalInput")}
        wspecs = [("ffn_wi", [L, 2, D, 2 * DFF]), ("ffn_wo", [L, 2, DFF, D]), ("ple_proj", [L, PLE, D]), ("ple_gate", [L, D, D]),
                  ("a_wqkv", [n_a, D, 4608]), ("a_wo", [n_a, 512, D])]
        if n_b:
            wspecs += [("b_wqkv", [n_b, D, 1536]), ("b_wo", [n_b, D, D])]
            self.b_sink_all = self.dt("b_sink", [n_b, 16], F32, "ExternalInput")
            self.masks_all[1] = self.dt("masks_b", [3, 128, 128], F32, "ExternalInput")
        if n_c:
            wspecs += [("c_wqkv", [n_c, D, 3072]), ("c_wo", [n_c, D, D])]
            self.c_biasg_all = self.dt("c_biasg", [n_c, 16, 7, 128, 128], F32, "ExternalInput")
            self.masks_all[2] = self.dt("masks_c", [35, 128, 128], F32, "ExternalInput")
        self.w32 = {}
        self.w16_all = {}
        for name, shp in wspecs:
            self.w32[name] = self.dt(name, shp, F32, "ExternalInput")
            if name != "ffn_wi":
                self.w16_all[name] = self.dt(name + "_bf", shp, BF16)
        self.wi_slot = self.dt("ffn_wi_slot", [L, 2, NJ // 2, 128, 4096], BF16)
        self.out_final = self.dt("out", [T, D], F32, "ExternalOutput")
        self.hbuf3 = [self.dt("h_scr%d" % i, [T, D], F32) for i in range(3)]
        self.srcs_all = {}
        for mx in range(3):
            if (mx == 1 and not n_b) or (mx == 2 and not n_c):
                continue
            lst = []
            for (nm, F, H, VW) in self.src_specs(mx):
                q = self.dt("q_" + nm, [F, T], BF16)
                k = self.dt("k_" + nm, [F if mx != 1 else 512, T], BF16)
                v = self.dt("v_" + nm, [T, VW], BF16)
                lst.append((nm, q, k, v, H, VW))
            self.srcs_all[mx] = lst
        mxr = 0
        for mx, lst in self.srcs_all.items():
            u, cl = self.plan_exchange(lst)
            mxr = max(mxr, cl[-1][0] + cl[-1][1])
        self.send_rows = mxr
        self.send = self.dt("x_send", [mxr, 512], BF16)
        self.recv = self.dt("x_recv", [4 * mxr, 512], BF16)

    def set_layer(self, l):
        L = self.cfg.DEPTH
        mx = l % 3; j = l // 3
        self.layer = l
        self.mx = mx
        self.norm_g = self.norm_g_all[l]
        self.p = self.p_all[l]
        self.masks = self.masks_all[mx]
        pre = "abc"[mx]
        self.w16 = {"wqkv": self.w16_all[pre + "_wqkv"][j], "wo": self.w16_all[pre + "_wo"][j],
                    "ple_proj": self.w16_all["ple_proj"][l], "ple_gate": self.w16_all["ple_gate"][l]}
        self.wi_l = self.wi_slot[l]
        self.wo_l = self.w16_all["ffn_wo"][l]
        if mx == 1:
            self.b_sink = self.b_sink_all[j:j + 1, :]
        if mx == 2:
            self.c_biasg = self.c_biasg_all[j]
        self.srcs = self.srcs_all[mx]
        self.h_pre_in = self.x_in if l == 0 else self.hbuf3[0]
        self.h_pre_out = self.hbuf3[1]
        self.h_att_out = self.hbuf3[2]
        self.h_mid_out = self.out_final if l == L - 1 else self.hbuf3[0]
        self.unit, self.colls = self.plan_exchange(self.srcs)

    @staticmethod
    def plan_exchange(srcs, maxrows=1000):
        units = []
        for si, (nm, q, k, v, H, VW) in enumerate(srcs):
            nkc = k.shape[0] // 128
            for side in "LR":
                for c in range(nkc):
                    units.append(((si, "K", side, c), 128 * H))
                for j in range(H // 128):
                    units.append(((si, "V", side, j), 128 * VW))
        unit = {}
        colls = []
        start = 0; cur = 0
        for key, n in units:
            assert n % 512 == 0
            r = n // 512
            if cur + r > maxrows:
                colls.append((start, cur)); start += cur; cur = 0
            unit[key] = (len(colls), cur * 512, n)
            cur += r
        colls.append((start, cur))
        return unit, colls

    def src_specs(self, mx):
        if mx == 0:
            return [("g0", 512, 128, 528), ("g1", 512, 256, 528), ("g2", 512, 1024, 528)]
        if mx == 1:
            return [("b", 1024, 128, 264)]
        return [("c", 1024, 256, 1056)]

    def nmask(self):
        return (25, 3, 35)[self.mx]

    def build(self):
        nc = self.nc
        c = self.cfg
        self.declare()
        NW = 48 * 1024 - 64
        with (
            nc.sbuf_tensor("arena", [128, NW], F32) as arena_t,
            nc.sbuf_tensor("consts", [128, 2048], F32) as consts_t,
            nc.psum_tensor("psum", [128, 4096], F32) as psum_t,
        ):
            self.A = Arena(arena_t, NW)
            self.CA = Arena(consts_t, 2048)
            self.ps = psum_t
            self.emit_all()
            keys = self.S.keys()
            print("semaphores:", len(keys))
            assert len(keys) < 140, "too many semaphores: %d" % len(keys)
            sem_cms = [nc.semaphore("s_" + k) for k in keys]
            handles = [cm.__enter__() for cm in sem_cms]
            sems = dict(zip(keys, handles))
            try:
                with nc.Block() as block:
                    @block.tensor
                    def _(e):
                        self.S.replay("pe", e, sems)

                    @block.scalar
                    def _(e):
                        self.S.replay("act", e, sems)

                    @block.vector
                    def _(e):
                        self.S.replay("dve", e, sems)

                    @block.gpsimd
                    def _(e):
                        self.S.replay("pool", e, sems)

                    @block.sync
                    def _(e):
                        self.S.replay("sp", e, sems)
            finally:
                for cm in reversed(sem_cms):
                    cm.__exit__(None, None, None)
        return nc

    def bank(self, i, n=512, dtype=F32):
        a = self.ps[:, i * 512:i * 512 + n]
        return a

    def emit_consts(self):
        S = self.S
        CA = self.CA
        self.ident = CA.bf16(128)
        self.rmat = CA.bf16(128)
        rtmp = CA.bf16(128)
        self.maskLG = CA.bf16(384)
        self.eps = CA.f32(1)
        self.ones65 = None
        self.zt = CA.bf16(1024)
        b = Buf("consts")
        self.cb = b
        ident, rmat, mk = self.ident, self.rmat, self.maskLG
        S.op("pool", lambda e: e.memset(ident, 0.0), w=[b])
        S.op("pool", lambda e: e.affine_select(out=ident, in_=ident, pattern=[[-1, 128]], compare_op=ALU.not_equal,
                                               fill=1.0, base=0, channel_multiplier=1), r=[b], w=[b])
        S.op("pool", lambda e: e.memset(rmat, 0.0), w=[b])
        S.op("pool", lambda e: e.affine_select(out=rmat, in_=rmat, pattern=[[-1, 128]], compare_op=ALU.not_equal,
                                               fill=1.0, base=-8, channel_multiplier=1), r=[b], w=[b])
        for (a0, a1) in ((8, 64), (72, 128)):
            S.op("pool", lambda e, a0=a0, a1=a1: e.memset(rmat[:, a0:a1], 0.0), r=[b], w=[b])
        S.op("pool", lambda e: e.memset(rtmp, 0.0), w=[b])
        S.op("pool", lambda e: e.affine_select(out=rtmp, in_=rtmp, pattern=[[-1, 128]], compare_op=ALU.not_equal,
                                               fill=1.0, base=8, channel_multiplier=1), r=[b], w=[b])
        for (a0, a1) in ((0, 8), (16, 72), (80, 128)):
            S.op("pool", lambda e, a0=a0, a1=a1: e.memset(rtmp[:, a0:a1], 0.0), r=[b], w=[b])
        S.op("pool", lambda e: e.tensor_tensor(out=rmat, in0=rmat, in1=rtmp, op=ALU.add), r=[b], w=[b])
        S.op("pool", lambda e: e.memset(mk, 0.0), w=[b])
        S.op("pool", lambda e: e.affine_select(out=mk[:, 0:128], in_=mk[:, 0:128], pattern=[[1, 128]], compare_op=ALU.is_ge,
                                               fill=NEG, base=0, channel_multiplier=-1), r=[b], w=[b])
        S.op("pool", lambda e: e.affine_select(out=mk[:, 256:384], in_=mk[:, 256:384], pattern=[[-1, 128]], compare_op=ALU.is_ge,
                                               fill=NEG, base=0, channel_multiplier=1), r=[b], w=[b])
        S.op("pool", lambda e: e.memset(self.eps, 1e-6), w=[b])
        S.op("pool", lambda e: e.memset(self.zt, 0.0), w=[b])

    def conv_jobs(self, l):
        mx = l % 3; j = l // 3
        jobs = []
        for f in range(2):
            src = self.w32["ffn_wi"][l, f].rearrange("(k p) (two n) -> p k two n", p=128, two=2)
            dst = self.wi_slot[l, f]
            for jp in range(NJ // 2):
                d3 = dst[jp].rearrange("p (k two c) -> p k two c", k=8, two=2)
                for two in range(2):
                    jobs.append((d3[:, :, two, :], src[:, :, two, jp * 256:(jp + 1) * 256]))
        def plain(dst, src, step=256):
            K = src.shape[0]
            for k0 in range(0, K, step):
                k1 = min(K, k0 + step)
                jobs.append((dst[k0:k1, :], src[k0:k1, :]))
        wi1 = jobs[NJ:]
        jobs = jobs[:NJ]
        pre = "abc"[mx]
        plain(self.w16_all["ffn_wo"][l, 0], self.w32["ffn_wo"][l, 0])
        plain(self.w16_all[pre + "_wqkv"][j], self.w32[pre + "_wqkv"][j])
        self.n_conv_first = len(jobs)
        plain(self.w16_all[pre + "_wo"][j], self.w32[pre + "_wo"][j])
        jobs += wi1
        plain(self.w16_all["ffn_wo"][l, 1], self.w32["ffn_wo"][l, 1])
        plain(self.w16_all["ple_proj"][l], self.w32["ple_proj"][l])
        plain(self.w16_all["ple_gate"][l], self.w32["ple_gate"][l])
        return jobs

    def emit_conv(self, jobs):
        S = self.S
        for (dst, src) in jobs:
            S.dma("pool", dst, src, w=[self.wbuf], key=S.dram_key("w16", 4))

    def load_gvec(self, l, idx, dst, buf):
        src = self.norm_g[idx:idx + 1, :].partition_broadcast(128)
        self.S.dma("sp", dst, src, w=[buf])

    def pre_norm_T(self, src_ap, src_buf, gt, gbuf, xT, xTbuf, col0, R):
        S = self.S
        ss = R["ss"]; ssb = R["ssb"]
        i = R["ni"] = R.get("ni", 0) + 1
        xn = R["xn"][i % 2]; xnb = R["xnb"][i % 2]
        junk = R["junk"]; jb = R["junkb"]
        tb = R["tbank"][i % 2]; tbb = R["tbankb"][i % 2]
        eps = self.eps
        S.op("act", lambda e: e.activation(out=junk, in_=src_ap, func=AF.Square, accum_out=ss[:, 0:1]), r=[src_buf], w=[jb, ssb])
        S.op("act", lambda e: e.activation(out=ss[:, 1:2], in_=ss[:, 0:1], func=AF.Sqrt, scale=1.0 / D, bias=eps), r=[ssb], w=[ssb])
        S.op("dve", lambda e: e.reciprocal(out=ss[:, 2:3], in_=ss[:, 1:2]), r=[ssb], w=[ssb])
        S.op("dve", lambda e: e.scalar_tensor_tensor(out=xn, in0=src_ap, scalar=ss[:, 2:3], in1=gt, op0=ALU.mult, op1=ALU.mult),
             r=[src_buf, ssb, gbuf], w=[xnb])
        tv = tb.bitcast(BF16)
        for k in range(8):
            S.op("pe", lambda e, k=k: e.transpose(out=tv[:, k * 128:(k + 1) * 128], in_=xn[:, k * 128:(k + 1) * 128], identity=self.ident),
                 r=[xnb, self.cb], w=[tbb])
        S.op("act", lambda e: e.copy(out=xT[:, :, col0:col0 + 128], in_=tv.rearrange("p (k t) -> p k t", k=8)), r=[tbb], w=[xTbuf])

    def post_norm_res(self, y_ap, ybuf, gt, gbuf, coef, h_ap, hbuf, R):
        S = self.S
        ybl = ybuf if isinstance(ybuf, list) else [ybuf]
        ss = R["ss2"]; ssb = R["ss2b"]
        junk = R["junk"]; jb = R["junkb"]
        tmp = R["tmp"]; tmpb = R["tmpb"]
        eps = self.eps
        S.op("act", lambda e: e.activation(out=junk, in_=y_ap, func=AF.Square, accum_out=ss[:, 0:1]), r=ybl, w=[jb, ssb])
        S.op("act", lambda e: e.activation(out=ss[:, 1:2], in_=ss[:, 0:1], func=AF.Sqrt, scale=1.0 / D, bias=eps), r=[ssb], w=[ssb])
        S.op("dve", lambda e: e.reciprocal(out=ss[:, 2:3], in_=ss[:, 1:2]), r=[ssb], w=[ssb])
        S.op("dve", lambda e: e.scalar_tensor_tensor(out=tmp, in0=y_ap, scalar=ss[:, 2:3], in1=gt, op0=ALU.mult, op1=ALU.mult),
             r=ybl + [ssb, gbuf], w=[tmpb])
        S.op("pool", lambda e: e.tensor_tensor(out=h_ap, in0=tmp, in1=h_ap, op=ALU.add), r=[tmpb, hbuf], w=[hbuf])

    def alloc_common(self):
        A = self.A
        R = {}
        R["ss"] = A.f32(4); R["ssb"] = Buf("ss")
        R["ss2"] = A.f32(4); R["ss2b"] = Buf("ss2")
        R["xn"] = [A.bf16(1024), A.bf16(1024)]; R["xnb"] = [Buf("xn0"), Buf("xn1")]
        R["junk"] = A.bf16(1024); R["junkb"] = Buf("junk")
        R["tmp"] = A.f32(1024); R["tmpb"] = Buf("tmp")
        R["tbank"] = [self.bank(6), self.bank(7)]; R["tbankb"] = [self.pb[6], self.pb[7]]
        return R

    def alloc_ffn(self):
        A = self.A
        F = {}
        F["xT"] = A.bf16(8 * 512).rearrange("p (k t) -> p k t", k=8); F["xTb"] = Buf("xT")
        F["wi"] = [A.bf16(8 * 512).rearrange("p (k two c) -> p k two c", k=8, two=2) for _ in range(3)]
        F["wib"] = [Buf("wi%d" % i) for i in range(3)]
        F["act"] = A.bf16(NJ * 512).rearrange("p (j t) -> p j t", j=NJ); F["actb"] = Buf("act")
        F["wo"] = A.bf16(NJ * 1024).rearrange("p (j n) -> p j n", j=NJ); F["wob"] = [Buf("wo%d" % j) for j in range(NJ)]
        F["sg"] = [A.f32(512), A.f32(512)]; F["sgb"] = [Buf("sg0"), Buf("sg1")]
        F["gpre"] = A.f32(1024); F["gpreb"] = Buf("gpre")
        F["gpost"] = A.f32(1024); F["gpostb"] = Buf("gpost")
        F["wi_i"] = 0
        F["gu_i"] = 0
        return F

    def ffn_load_resident(self, F, l, f, gi_pre, gi_post):
        S = self.S
        wo16 = self.wo_l[f]
        for j in range(NJ):
            S.dma("sp", F["wo"][:, j, :], wo16[j * 128:(j + 1) * 128, :], r=[self.wbuf], w=[F["wob"][j]], key="d_wores%d" % (j % 4))
        self.load_gvec(l, gi_pre, F["gpre"], F["gpreb"])
        self.load_gvec(l, gi_post, F["gpost"], F["gpostb"])
        self.scale_gvec(F["gpost"], F["gpostb"], 0.5)

    def ffn_tile(self, F, R, l, f, ht, hbuf):
        S = self.S
        xT = F["xT"]; xTb = F["xTb"]
        for s in range(4):
            self.pre_norm_T(ht[:, s, :], hbuf, F["gpre"], F["gpreb"], xT, xTb, s * 128, R)
        act = F["act"]; actb = F["actb"]
        for jp in range(NJ // 2):
            wi_i = F["wi_i"]; F["wi_i"] += 1
            ws = F["wi"][wi_i % 3]; wsb = F["wib"][wi_i % 3]
            S.dma("sp", ws.rearrange("p k two c -> p (k two c)"), self.wi_l[f][jp], r=[self.wbuf], w=[wsb])
            for jj in range(2):
                j = jp * 2 + jj
                gi = F["gu_i"]; F["gu_i"] += 1
                pg = self.bank((gi % 2) * 2); pu = self.bank((gi % 2) * 2 + 1)
                pgb = self.pb[(gi % 2) * 2]
                pub = self.pb[(gi % 2) * 2 + 1]
                sg = F["sg"][gi % 2]; sgb = F["sgb"][gi % 2]
                for k in range(8):
                    S.op("pe", lambda e, k=k, ws=ws, jj=jj, pg=pg: e.matmul(pg, lhsT=ws[:, k, 0, jj * 128:(jj + 1) * 128], rhs=xT[:, k, :],
                                                                      start=(k == 0), stop=(k == 7)), r=[wsb, xTb], w=[pgb])
                for k in range(8):
                    S.op("pe", lambda e, k=k, ws=ws, jj=jj, pu=pu: e.matmul(pu, lhsT=ws[:, k, 1, jj * 128:(jj + 1) * 128], rhs=xT[:, k, :],
                                                                      start=(k == 0), stop=(k == 7)), r=[wsb, xTb], w=[pub])
                S.op("act", lambda e, sg=sg, pg=pg: e.activation(out=sg, in_=pg, func=AF.Silu), r=[pgb], w=[sgb])
                S.op("dve", lambda e, sg=sg, pu=pu, j=j: e.tensor_tensor(out=act[:, j, :], in0=sg, in1=pu, op=ALU.mult), r=[sgb, pub], w=[actb])
        for s in range(4):
            b0 = (4, 0, 2)[s % 3]
            y = self.ps[:, b0 * 512:(b0 + 2) * 512]
            for dh in range(2):
                for j in range(NJ):
                    S.op("pe", lambda e, j=j, s=s, dh=dh, b0=b0: e.matmul(self.ps[:, (b0 + dh) * 512:(b0 + dh + 1) * 512], lhsT=act[:, j, s * 128:(s + 1) * 128],
                                                                        rhs=F["wo"][:, j, dh * 512:(dh + 1) * 512], start=(j == 0), stop=(j == NJ - 1)),
                         r=[actb, F["wob"][j]], w=[self.pb[b0 + dh]])
            self.post_norm_res(y, [self.pb[b0], self.pb[b0 + 1]], F["gpost"], F["gpostb"], 0.5, ht[:, s, :], hbuf, R)

    def qkv_jobs(self):
        mx = self.mx
        W = self.w16["wqkv"].rearrange("(k p) n -> p k n", p=128)
        jobs = []
        vjobs = []
        if mx == 0:
            for g in range(3):
                nm, q, k, v, H, VW = self.srcs[g]
                jobs.append(([(0, g * 512, 512)], [(c * 128, q[c * 128:(c + 1) * 128, :]) for c in range(4)], True))
                jobs.append(([(0, 1536 + g * 512, 512)], [(c * 128, k[c * 128:(c + 1) * 128, :]) for c in range(4)], True))
                vjobs.append((3072 + g * 512, 512, v, 0, 8))
        elif mx == 1:
            nm, q, k, v, H, VW = self.srcs[0]
            for hf in range(2):
                jobs.append(([(0, hf * 512, 512)], [(c * 128, q[(hf * 4 + c) * 128:(hf * 4 + c + 1) * 128, :]) for c in range(4)], True))
            pieces = []
            for kv in range(4):
                pieces += [(kv * 128, 1024 + kv * 64, 64), (kv * 128 + 64, 1024 + kv * 64, 64)]
            jobs.append((pieces, [(kv * 128, k[kv * 128:(kv + 1) * 128, :]) for kv in range(4)], True))
            vjobs.append((1280, 256, v, 0, 4))
        else:
            nm, q, k, v, H, VW = self.srcs[0]
            for hf in range(2):
                jobs.append(([(0, hf * 512, 512)], [(c * 128, q[(hf * 4 + c) * 128:(hf * 4 + c + 1) * 128, :]) for c in range(4)], False))
            for hf in range(2):
                jobs.append(([(0, 1024 + hf * 512, 512)], [(c * 128, k[(hf * 4 + c) * 128:(hf * 4 + c + 1) * 128, :]) for c in range(4)], False))
            for vb in range(2):
                vjobs.append((2048 + vb * 512, 512, v, vb * 528, 8))
        return W, jobs, vjobs

    def phase_pre(self):
        S = self.S; A = self.A
        A.reset()
        R = self.alloc_common()
        F = self.alloc_ffn()
        hts = [A.f32(4 * 1024).rearrange("p (s d) -> p s d", s=4) for _ in range(2)]
        hbs = [Buf("ht0"), Buf("ht1")]
        g2 = A.f32(1024); g2b = Buf("g2")
        wq = [A.bf16(8 * 512).rearrange("p (k n) -> p k n", k=8) for _ in range(2)]; wqb = [Buf("wq0"), Buf("wq1")]
        ct = A.f32(512); st = A.f32(512); ctb = Buf("ct"); stb = Buf("st")
        qb = [A.bf16(512), A.bf16(512)]; qbb = [Buf("qb0"), Buf("qb1")]
        t1 = A.f32(512); t2 = A.f32(512); t1b = Buf("t1"); t2b = Buf("t2")
        qr = [A.bf16(512), A.bf16(512)]; qrb = [Buf("qr0"), Buf("qr1")]
        vsf = [A.bf16(8 * 66) for _ in range(2)]
        vs = [v_.rearrange("p (h d) -> p h d", h=8) for v_ in vsf]; vsb = [Buf("vs0"), Buf("vs1")]
        for i in range(2):
            S.op("dve", lambda e, i=i: e.memset(vsf[i], 1.0), w=[vsb[i]])
        self.ffn_load_resident(F, 0, 0, 0, 1)
        self.load_gvec(0, 2, g2, g2b)
        W, jobs, vjobs = self.qkv_jobs()
        od = Buf("out_dram")
        cnt = {"w": 0, "c": 0, "v": 0}
        for t in range(self.cfg.NT):
            ht = hts[t % 2]; hb = hbs[t % 2]
            tok = slice(t * 512, (t + 1) * 512)
            S.dma("sp", ht, self.h_pre_in[tok, :].rearrange("(s p) d -> p s d", p=128), w=[hb])
            self.ffn_tile(F, R, 0, 0, ht, hb)
            S.dma("pool", self.h_pre_out[tok, :].rearrange("(s p) d -> p s d", p=128), ht, r=[hb], w=[od], key=S.dram_key("h"))
            xT = F["xT"]; xTb = F["xTb"]
            for s4 in range(4):
                self.pre_norm_T(ht[:, s4, :], hb, g2, g2b, xT, xTb, s4 * 128, R)
            if self.mx != 2:
                S.dma("sp", ct, self.rope_c[:, tok], w=[ctb])
                S.dma("sp", st, self.rope_s[:, tok], w=[stb])
            import os
            for (pieces, chunks, rope) in (jobs if not os.environ.get('SKIP_QK') else []):
                wi_ = cnt["w"]; cnt["w"] += 1
                ws = wq[wi_ % 2]; wsb = wqb[wi_ % 2]
                for (sc, src, n) in pieces:
                    S.dma("sp", ws[:, :, sc:sc + n], W[:, :, src:src + n], r=[self.wbuf], w=[wsb])
                for (sc, dest) in chunks:
                    ci = cnt["c"]; cnt["c"] += 1
                    pq = self.bank(ci % 2); pqb = self.pb[ci % 2]
                    for k in range(8):
                        S.op("pe", lambda e, k=k, ws=ws, sc=sc, pq=pq: e.matmul(pq, lhsT=ws[:, k, sc:sc + 128], rhs=xT[:, k, :], start=(k == 0), stop=(k == 7)),
                             r=[wsb, xTb], w=[pqb])
                    q1 = qr[ci % 2]; q1b = qrb[ci % 2]
                    if rope and not os.environ.get('SKIP_ROPE'):
                        b1 = qb[ci % 2]; b1b = qbb[ci % 2]
                        pr = self.bank(2 + ci % 2); prb = self.pb[2 + ci % 2]
                        S.op("act", lambda e, b1=b1, pq=pq: e.copy(out=b1, in_=pq), r=[pqb], w=[b1b])
                        if not os.environ.get('SKIP_RM'):
                            S.op("pe", lambda e, b1=b1, pr=pr: e.matmul(pr, lhsT=self.rmat, rhs=b1, start=True, stop=True), r=[b1b, self.cb], w=[prb])
                        S.op("dve", lambda e, pq=pq: e.tensor_tensor(out=t1, in0=ct, in1=pq, op=ALU.mult), r=[pqb, ctb], w=[t1b])
                        S.op("dve", lambda e, pr=pr: e.tensor_tensor(out=t2, in0=st, in1=pr, op=ALU.mult), r=[prb, stb], w=[t2b])
                        S.op("pool", lambda e, q1=q1: e.tensor_tensor(out=q1, in0=t1, in1=t2, op=ALU.add), r=[t1b, t2b], w=[q1b])
                    else:
                        S.op("act", lambda e, q1=q1, pq=pq: e.copy(out=q1, in_=pq), r=[pqb], w=[q1b])
                    S.dma("pool", dest[:, tok], q1, r=[q1b], w=[od], key=S.dram_key("qk", 3))
            for (src, ncols, vdst, dc0, nh) in (vjobs if not os.environ.get('SKIP_V') else []):
                wi_ = cnt["w"]; cnt["w"] += 1
                ws = wq[wi_ % 2]; wsb = wqb[wi_ % 2]
                S.dma("sp", ws[:, :, 0:ncols], W[:, :, src:src + ncols], r=[self.wbuf], w=[wsb])
                for s4 in range(4):
                    vi = cnt["v"]; cnt["v"] += 1
                    pv = self.bank(4 + vi % 2)[:, 0:ncols]; pvb = self.pb[4 + vi % 2]
                    for k in range(8):
                        S.op("pe", lambda e, k=k, ws=ws, pv=pv, s4=s4, ncols=ncols: e.matmul(pv, lhsT=xT[:, k, s4 * 128:(s4 + 1) * 128], rhs=ws[:, k, 0:ncols],
                                                                                    start=(k == 0), stop=(k == 7)), r=[wsb, xTb], w=[pvb])
                    v1 = vs[vi % 2]; v1b = vsb[vi % 2]
                    S.op("act", lambda e, v1=v1, pv=pv, nh=nh: e.copy(out=v1[:, 0:nh, 0:64], in_=pv.rearrange("p (h d) -> p h d", h=nh)), r=[pvb], w=[v1b])
                    r0 = t * 512 + s4 * 128
                    S.dma("pool", vdst[r0:r0 + 128, dc0:dc0 + nh * 66], v1[:, 0:nh, :].rearrange("p h d -> p (h d)"), r=[v1b], w=[od], key=S.dram_key("v"))
            if self.pending_conv:
                nper = -(-len(self.pending_conv) // max(1, self.cfg.NT - t))
                self.emit_conv(self.pending_conv[:nper])
                self.pending_conv = self.pending_conv[nper:]
        S.barrier()

    def head_plan(self, i, NQ):
        mx = self.mx
        groups = []
        if mx == 0:
            dl = [[-1, 0, 1], [-2, -1, 0, 1, 2], list(range(-8, 9))]
            for hg in range(2):
                heads = []
                for h in range(hg * 4, hg * 4 + 4):
                    tl = []
                    for g in range(3):
                        for d in dl[g]:
                            tl.append((g, h // 2, h % 2, h // 2, h * 66, d, [("m", len(tl))]))
                    heads.append((h, tl))
                groups.append(heads)
        elif mx == 1:
            for hg in range(4):
                heads = []
                for h in range(hg * 4, hg * 4 + 4):
                    tl = [(0, h // 2, h % 2, h // 4, (h // 4) * 66, d, [("m", d + 1)]) for d in (-1, 0, 1)]
                    heads.append((h, tl))
                groups.append(heads)
        else:
            cls = 0
            if i == 0: cls = 1
            elif i == 1: cls = 2
            elif i == NQ - 2: cls = 3
            elif i == NQ - 1: cls = 4
            ds = list(range(-2, 3))
            if i == 0: ds.append(3)
            if i == NQ - 1: ds = [-3] + ds
            for hg in range(4):
                heads = []
                for h in range(hg * 4, hg * 4 + 4):
                    tl = [(0, h // 2, h % 2, h // 2, h * 66, d, [("e", h * 7 + d + 3), ("m", cls * 7 + d + 3)]) for d in ds]
                    heads.append((h, tl))
                groups.append(heads)
        return groups

    def phase_att(self):
        S = self.S; A = self.A
        A.reset()
        R = self.alloc_common()
        T = self.cfg.TOK
        NQ = T // 128
        mx = self.mx
        NM = self.nmask()
        self.msk = A.bf16(NM * 128).rearrange("p (m q) -> p m q", m=NM); mb = Buf("msk")
        for m0 in range(0, NM, 8):
            m1 = min(NM, m0 + 8)
            S.dma("pool", self.msk[:, m0:m1, :], self.masks[m0:m1].rearrange("m p q -> p m q"), w=[mb])
        cbb = Buf("cbias")
        if mx == 2:
            self.cbias = A.bf16(112 * 128).rearrange("p (m q) -> p m q", m=112)
            cbg = self.c_biasg.rearrange("h d p q -> p (h d) q")
            for m0 in range(0, 112, 8):
                S.dma("pool", self.cbias[:, m0:m0 + 8, :], cbg[:, m0:m0 + 8, :], w=[cbb])
            cflat = self.cbias.rearrange("p m q -> p (m q)")
            for c0_ in range(0, 112 * 128, 512):
                S.op("act", lambda e, c0_=c0_, cflat=cflat: e.activation(out=cflat[:, c0_:c0_ + 512], in_=cflat[:, c0_:c0_ + 512], func=AF.Exp), r=[cbb], w=[cbb])
        Fo = (512, 1024, 1024)[mx]
        nco = Fo // 128
        wo = A.bf16(nco * 1024).rearrange("p (c n) -> p c n", c=nco); wob = Buf("wo_mix")
        S.dma("sp", wo, self.w16["wo"].rearrange("(c p) n -> p c n", p=128), r=[self.wbuf], w=[wob])
        g3 = A.f32(1024); g3b = Buf("g3")
        self.load_gvec(0, 3, g3, g3b)
        es = None
        if mx == 1:
            es = A.f32(16); esb = Buf("esink")
            S.dma("sp", es, self.b_sink.partition_broadcast(128), w=[esb])
            S.op("act", lambda e: e.activation(out=es, in_=es, func=AF.Exp), r=[esb], w=[esb])
        fl = A.f32(6); flb = Buf("flags")
        S.dma("sp", fl, self.flags, w=[flb])
        ckmax = max(k.shape[0] for (nm, q, k, v, H, VW) in self.srcs)
        vwmax = max(VW for (nm, q, k, v, H, VW) in self.srcs)
        candk = [A.bf16(ckmax) for _ in range(3)]; candv = [A.bf16(vwmax) for _ in range(3)]
        candb = [Buf("cand%d" % j) for j in range(3)]
        recv_flat = self.recv.rearrange("r c -> (r c)")
        rings = []
        for si_, (nm, q, k, v, H, VW) in enumerate(self.srcs):
            nkc = k.shape[0] // 128
            nqc = q.shape[0] // 128
            Hb = H // 128
            dmax = Hb if mx != 2 else 2
            rs = 2 * dmax + 1 + 2
            kslots = [A.bf16(nkc * 128).rearrange("p (c t) -> p c t", c=nkc) for _ in range(rs)]
            vslots = [A.bf16(VW) for _ in range(rs)]
            kb = [Buf("kv_%s_%d" % (nm, j)) for j in range(rs)]
            vb = kb
            qs = [A.bf16(nqc * 128).rearrange("p (c t) -> p c t", c=nqc) for _ in range(2)]
            qbf = [Buf("q_%d_%d" % (si_, j)) for j in range(2)]
            rings.append(dict(q=q.rearrange("(c p) t -> p c t", p=128), k=k.rearrange("(c p) t -> p c t", p=128), v=v, Hb=Hb, dmax=dmax, rs=rs,
                              ks=kslots, vs=vslots, kb=kb, vb=vb, qs=qs, qb=qbf, loaded=-1, NKT=(T + 2 * H) // 128,
                              nm=nm, si=si_, H=H, VW=VW, Fk=k.shape[0], nkc=nkc))

        def load_kv(rg, kt, sl):
            Hb_ = rg["Hb"]; H_ = rg["H"]; VW_ = rg["VW"]; Fk_ = rg["Fk"]; nkc_ = rg["nkc"]
            key = "d_ring_%d_%d" % (rg["si"], sl % 5)
            kdst = rg["ks"][sl]; vdst = rg["vs"][sl]; sb_ = rg["kb"][sl]
            if Hb_ <= kt < Hb_ + NQ:
                j = kt - Hb_
                S.dma("sp", kdst, rg["k"][:, :, j * 128:(j + 1) * 128], w=[sb_], key=key)
                S.dma("sp", vdst, rg["v"][j * 128:(j + 1) * 128, :], w=[sb_], key=key)
                return
            left = kt < Hb_
            j = kt if left else kt - Hb_ - NQ
            side = "R" if left else "L"
            blocks = (0, 1, 2) if left else (1, 2, 3)
            f0 = 0 if left else 3
            for ci, bj in enumerate(blocks):
                ck = candk[ci][:, 0:nkc_ * 128].rearrange("p (c t) -> p c t", c=nkc_)
                cv = candv[ci][:, 0:VW_]
                for c in range(nkc_):
                    g, og, n = self.unit[(rg["si"], "K", side, c)]
                    r0, nr = self.colls[g]
                    o = (4 * r0 + bj * nr) * 512 + og
                    S.dma("sp", ck[:, c, :], recv_flat[o:o + n].rearrange("(p t) -> p t", t=H_)[:, j * 128:(j + 1) * 128], w=[candb[ci]])
                g, og, n = self.unit[(rg["si"], "V", side, j)]
                r0, nr = self.colls[g]
                o = (4 * r0 + bj * nr) * 512 + og
                S.dma("sp", cv, recv_flat[o:o + n].rearrange("(t w) -> t w", w=VW_), w=[candb[ci]])
                fcol = fl[:, f0 + ci:f0 + ci + 1]
                if ci == 0:
                    S.op("dve", lambda e, o_=kdst, i=ck, f=fcol: e.tensor_scalar(out=o_, in0=i, scalar1=f, scalar2=None, op0=ALU.mult), r=[candb[ci], flb], w=[sb_])
                    S.op("dve", lambda e, o_=vdst, i=cv, f=fcol: e.tensor_scalar(out=o_, in0=i, scalar1=f, scalar2=None, op0=ALU.mult), r=[candb[ci], flb], w=[sb_])
                else:
                    S.op("dve", lambda e, o_=kdst, i=ck, f=fcol: e.scalar_tensor_tensor(out=o_, in0=i, scalar=f, in1=o_, op0=ALU.mult, op1=ALU.add), r=[candb[ci], flb], w=[sb_])
                    S.op("dve", lambda e, o_=vdst, i=cv, f=fcol: e.scalar_tensor_tensor(out=o_, in0=i, scalar=f, in1=o_, op0=ALU.mult, op1=ALU.add), r=[candb[ci], flb], w=[sb_])

        pts = [A.bf16(512) for _ in range(4)]; ptb = [Buf("pt%d" % j) for j in range(4)]
        ob = [A.bf16(Fo) for _ in range(2)]; obb = [Buf("o%d" % j) for j in range(2)]
        oT = A.bf16(nco * 128).rearrange("p (c t) -> p c t", c=nco); oTb = Buf("oT")
        rd = A.f32(8); rdb = Buf("rd")
        hts = [A.f32(1024) for _ in range(3)]; hbs = [Buf("hq0"), Buf("hq1"), Buf("hq2")]
        hd = Buf("hscr")
        mflat = self.msk.rearrange("p m q -> p (m q)")
        eflat = self.cbias.rearrange("p m q -> p (m q)") if mx == 2 else None
        sc_i = 0; pt_i = 0; og_i = 0
        pending_pv = []
        pending_fin = []
        for i in range(NQ):
            ht = hts[i % 3]; hb = hbs[i % 3]
            for rg in rings:
                upto = min(rg["NKT"] - 1, i + rg["Hb"] + rg["dmax"] + 1)
                while rg["loaded"] < upto:
                    kt = rg["loaded"] + 1
                    sl = kt % rg["rs"]
                    load_kv(rg, kt, sl)
                    rg["loaded"] = kt
                S.dma("sp", rg["qs"][i % 2], rg["q"][:, :, i * 128:(i + 1) * 128], w=[rg["qb"][i % 2]])
            S.dma("sp", ht, self.h_pre_out[i * 128:(i + 1) * 128, :], w=[hb])
            o1 = ob[i % 2]; o1b = obb[i % 2]
            for hg_idx, heads in enumerate(self.head_plan(i, NQ)):
                og = og_i; og_i += 1
                orb = self.pb[3 + og % 2]
                oreg = self.bank(3 + og % 2)
                for hh, (h, tl) in enumerate(heads):
                    ocol = hh * 65
                    nt = len(tl)
                    for c0 in range(0, nt, 4):
                        grp = tl[c0:c0 + 4]
                        sb_ = sc_i % 3; sc_i += 1
                        sbank = self.bank(sb_); sbb = self.pb[sb_]
                        reads_kv = []
                        for j, (si, qc, half, kc, vcol, d, mts) in enumerate(grp):
                            rg = rings[si]
                            kt = i + rg["Hb"] + d
                            sl = kt % rg["rs"]
                            hp = slice(half * 64, half * 64 + 64)
                            kap = rg["ks"][sl][hp, kc, :]
                            qap = rg["qs"][i % 2][hp, qc, :]
                            dst = sbank[:, j * 128:(j + 1) * 128]
                            S.op("pe", lambda e, dst=dst, kap=kap, qap=qap: e.matmul(dst, lhsT=kap, rhs=qap, start=True, stop=True),
                                 r=[rg["kb"][sl], rg["qb"][i % 2]], w=[sbb])
                        n = len(grp)
                        pi = pt_i % 4; pt_i += 1
                        pt = pts[pi]; ptbuf = ptb[pi]
                        S.op("act", lambda e, pt=pt, sbank=sbank, n=n: e.activation(out=pt[:, 0:n * 128], in_=sbank[:, 0:n * 128], func=AF.Exp, scale=0.125),
                             r=[sbb], w=[ptbuf])
                        for fi in range(len(grp[0][6])):
                            nm_, i0 = grp[0][6][fi]
                            for jj_, tt_ in enumerate(grp):
                                assert tt_[6][fi] == (nm_, i0 + jj_)
                            strip = mflat if nm_ == "m" else eflat
                            S.op("dve", lambda e, pt=pt, n=n, strip=strip, i0=i0: e.tensor_tensor(out=pt[:, 0:n * 128], in0=pt[:, 0:n * 128],
                                                                                              in1=strip[:, i0 * 128:(i0 + n) * 128], op=ALU.mult),
                                 r=[ptbuf, mb, cbb], w=[ptbuf])
                        def emit_pv(grp=grp, pt=pt, ptbuf=ptbuf, c0=c0, nt=nt, oreg=oreg, ocol=ocol, orb=orb):
                            for j, (si, qc, half, kc, vcol, d, mts) in enumerate(grp):
                                rg = rings[si]
                                kt = i + rg["Hb"] + d
                                sl = kt % rg["rs"]
                                first = (c0 + j == 0); last = (c0 + j == nt - 1)
                                S.op("pe", lambda e, pt=pt, j=j, vap=rg["vs"][sl][:, vcol:vcol + 65], oreg=oreg, ocol=ocol, first=first, last=last:
                                     e.matmul(oreg[:, ocol:ocol + 65], lhsT=pt[:, j * 128:(j + 1) * 128], rhs=vap, start=first, stop=last),
                                     r=[ptbuf, rg["vb"][sl]], w=[orb])
                        if len(pending_pv) >= 2:
                            pending_pv.pop(0)()
                        pending_pv.append(emit_pv)
                while pending_pv:
                    pending_pv.pop(0)()
                nh = len(heads)
                den = oreg[:, 0:nh * 65].rearrange("p (h d) -> p h d", h=nh)[:, :, 64]
                if es is not None:
                    h0 = heads[0][0]
                    S.op("dve", lambda e, den=den, h0=h0, nh=nh: e.tensor_tensor(out=rd[:, 0:nh], in0=den, in1=es[:, h0:h0 + nh], op=ALU.add), r=[orb, esb], w=[rdb])
                    S.op("dve", lambda e, nh=nh: e.reciprocal(out=rd[:, 0:nh], in_=rd[:, 0:nh]), r=[rdb], w=[rdb])
                else:
                    S.op("dve", lambda e, den=den, nh=nh: e.reciprocal(out=rd[:, 0:nh], in_=den), r=[orb], w=[rdb])
                for hh, (h, tl) in enumerate(heads):
                    S.op("dve", lambda e, hh=hh, h=h, oreg=oreg, o1=o1: e.tensor_scalar(out=o1[:, h * 64:(h + 1) * 64], in0=oreg[:, hh * 65:hh * 65 + 64],
                                                                                 scalar1=rd[:, hh:hh + 1], scalar2=None, op0=ALU.mult), r=[orb, rdb], w=[o1b])
                if pending_fin and hg_idx < 2:
                    pending_fin.pop(0)()
            def fin_a(o1=o1, o1b=o1b):
                tv = self.bank(5).bitcast(BF16)
                for c in range(nco):
                    S.op("pe", lambda e, c=c, o1=o1, tv=tv: e.transpose(out=tv[:, c * 128:(c + 1) * 128], in_=o1[:, c * 128:(c + 1) * 128], identity=self.ident),
                         r=[o1b, self.cb], w=[self.pb[5]])
                S.op("act", lambda e, tv=tv: e.copy(out=oT, in_=tv[:, 0:nco * 128].rearrange("p (c t) -> p c t", c=nco)), r=[self.pb[5]], w=[oTb])

            def fin_b(i=i, ht=ht, hb=hb):
                for dh in range(2):
                    for c in range(nco):
                        S.op("pe", lambda e, c=c, dh=dh: e.matmul(self.bank(6 + dh), lhsT=oT[:, c, :], rhs=wo[:, c, dh * 512:(dh + 1) * 512], start=(c == 0), stop=(c == nco - 1)),
                             r=[oTb, wob], w=[self.pb[6 + dh]])
                self.post_norm_res(self.ps[:, 6 * 512:8 * 512], [self.pb[6], self.pb[7]], g3, g3b, 1.0, ht, hb, R)
                S.dma("pool", self.h_att_out[i * 128:(i + 1) * 128, :], ht, r=[hb], w=[hd], key=S.dram_key("hs"))
            pending_fin.extend([fin_a, fin_b])
        while pending_fin:
            pending_fin.pop(0)()
        S.barrier()

    def phase_mid(self):
        S = self.S; A = self.A
        A.reset()
        R = self.alloc_common()
        F = self.alloc_ffn()
        hts = [A.f32(4 * 1024).rearrange("p (s d) -> p s d", s=4) for _ in range(2)]
        hbs = [Buf("ht0"), Buf("ht1")]
        wg = A.bf16(8 * 1024).rearrange("p (k n) -> p k n", k=8); wgb = Buf("wgate")
        wp = A.bf16(2 * 1024).rearrange("p (k n) -> p k n", k=2); wpb = Buf("wproj")
        g6 = A.f32(1024); g6b = Buf("g6"); g7 = A.f32(1024); g7b = Buf("g7")
        sgt = A.f32(1024); sgtb = Buf("sgt")
        pt_ = A.f32(256); ptb_ = Buf("ptile")
        pbf = A.bf16(256); pbfb = Buf("pbf")
        pT = A.bf16(256).rearrange("p (k t) -> p k t", k=2); pTb = Buf("pT")
        S.dma("sp", wg, self.w16["ple_gate"].rearrange("(k p) n -> p k n", p=128), r=[self.wbuf], w=[wgb])
        S.dma("sp", wp, self.w16["ple_proj"].rearrange("(k p) n -> p k n", p=128), r=[self.wbuf], w=[wpb])
        self.ffn_load_resident(F, 0, 1, 4, 5)
        self.load_gvec(0, 6, g6, g6b)
        self.load_gvec(0, 7, g7, g7b)
        od = Buf("out_dram")
        xT = F["xT"]; xTb = F["xTb"]
        eg = R["tmp"]
        for t in range(self.cfg.NT):
            ht = hts[t % 2]; hb = hbs[t % 2]
            tok = slice(t * 512, (t + 1) * 512)
            S.dma("sp", ht, self.h_att_out[tok, :].rearrange("(s p) d -> p s d", p=128), w=[hb])
            self.ffn_tile(F, R, 0, 1, ht, hb)
            for s4 in range(4):
                r0 = t * 512 + s4 * 128
                self.pre_norm_T(ht[:, s4, :], hb, g6, g6b, xT, xTb, s4 * 128, R)
                for dh in range(2):
                    for k in range(8):
                        S.op("pe", lambda e, k=k, dh=dh, s4=s4: e.matmul(self.bank(4 + dh), lhsT=xT[:, k, s4 * 128:(s4 + 1) * 128], rhs=wg[:, k, dh * 512:(dh + 1) * 512],
                                                                      start=(k == 0), stop=(k == 7)), r=[xTb, wgb], w=[self.pb[4 + dh]])
                S.op("act", lambda e: e.activation(out=sgt, in_=self.ps[:, 4 * 512:6 * 512], func=AF.Sigmoid), r=[self.pb[4], self.pb[5]], w=[sgtb])
                S.dma("sp", pt_, self.p[r0:r0 + 128, :], w=[ptb_])
                S.op("dve", lambda e: e.tensor_copy(out=pbf, in_=pt_), r=[ptb_], w=[pbfb])
                tv = self.bank(6).bitcast(BF16)
                for k in range(2):
                    S.op("pe", lambda e, k=k, tv=tv: e.transpose(out=tv[:, k * 128:(k + 1) * 128], in_=pbf[:, k * 128:(k + 1) * 128], identity=self.ident),
                         r=[pbfb, self.cb], w=[self.pb[6]])
                S.op("act", lambda e, tv=tv: e.copy(out=pT, in_=tv[:, 0:256].rearrange("p (k t) -> p k t", k=2)), r=[self.pb[6]], w=[pTb])
                for dh in range(2):
                    for k in range(2):
                        S.op("pe", lambda e, k=k, dh=dh: e.matmul(self.bank(dh), lhsT=pT[:, k, :], rhs=wp[:, k, dh * 512:(dh + 1) * 512], start=(k == 0), stop=(k == 1)),
                             r=[pTb, wpb], w=[self.pb[dh]])
                S.op("dve", lambda e: e.tensor_tensor(out=sgt, in0=self.ps[:, 0:1024], in1=sgt, op=ALU.mult), r=[self.pb[0], self.pb[1], sgtb], w=[sgtb])
                self.post_norm_res(sgt, sgtb, g7, g7b, 1.0, ht[:, s4, :], hb, R)
            S.dma("pool", self.h_mid_out[tok, :].rearrange("(s p) d -> p s d", p=128), ht, r=[hb], w=[od], key=S.dram_key("h"))
        S.barrier()

    def scale_gvec(self, gt, gb, coef):
        self.S.op("dve", lambda e: e.tensor_scalar(out=gt, in0=gt, scalar1=float(coef), scalar2=None, op0=ALU.mult), r=[gb], w=[gb])

    def phase_exchange(self):
        S = self.S
        T = self.cfg.TOK
        send_flat = self.send.rearrange("r c -> (r c)")
        sb = Buf("sendbuf"); rb = Buf("recvbuf")
        for (si, kind, side, idx), (g, og, n) in self.unit.items():
            nm, q, k, v, H, VW = self.srcs[si]
            o = self.colls[g][0] * 512 + og
            t0 = 0 if side == "L" else T - H
            if kind == "K":
                src = k[idx * 128:(idx + 1) * 128, t0:t0 + H]
                dst = send_flat[o:o + n].rearrange("(p t) -> p t", t=H)
            else:
                src = v[t0 + idx * 128:t0 + (idx + 1) * 128, :]
                dst = send_flat[o:o + n].rearrange("(t w) -> t w", w=VW)
            S.dma("sp", dst, src, w=[sb], key=S.dram_key("xs", 2))
        for (r0, nr) in self.colls:
            S.custom("pool", lambda e, r0=r0, nr=nr: e.collective_compute("AllGather", ALU.bypass, replica_groups=[[0, 1, 2, 3], [4, 5, 6, 7]],
                                                                     ins=[self.send[r0:r0 + nr, :]], outs=[self.recv[4 * r0:4 * r0 + 4 * nr, :]]),
                     r=[sb], w=[rb], key="cc", inc=1)
        S.barrier()

    def emit_all(self):
        self.emit_consts()
        self.wbuf = Buf("w16")
        jobs0 = self.conv_jobs(0)
        nfirst = self.n_conv_first
        self.emit_conv(jobs0[:nfirst])
        self.S.barrier()
        import os
        nstop = int(os.environ.get("FUSE_STOP", "999"))
        n = 0
        for l in range(self.cfg.DEPTH):
            self.set_layer(l)
            self.pending_conv = (jobs0[nfirst:] if l == 0 else []) + (self.conv_jobs(l + 1) if l + 1 < self.cfg.DEPTH else [])
            for ph in (self.phase_pre, self.phase_exchange, self.phase_att, self.phase_mid):
                if n < nstop:
                    ph()
                n += 1
            self.emit_conv(self.pending_conv)
            self.pending_conv = []
        self.S.barrier()


_PROG_CACHE = {}


def _get_prog(seq, depth):
    key = (seq, depth)
    if key not in _PROG_CACHE:
        P = Prog(Cfg(seq=seq, depth=depth))
        nc = P.build()
        _PROG_CACHE[key] = (P, nc)
    return _PROG_CACHE[key]


def _rope_tables(T, pos0):
    pos = (pos0 + np.arange(T)).astype(np.float32)
    inv = (np.float32(500000.0) ** (-np.arange(0, 16, 2, dtype=np.float32) / np.float32(16))).astype(np.float32)
    ang = pos[None, :] * inv[:, None]
    c = np.cos(ang).astype(np.float32); s = np.sin(ang).astype(np.float32)
    C = np.ones((128, T), np.float32); Sg = np.zeros((128, T), np.float32)
    for p in range(128):
        f = p % 64
        if f < 8:
            C[p] = c[f]; Sg[p] = -s[f]
        elif f < 16:
            C[p] = c[f - 8]; Sg[p] = s[f - 8]
    return C, Sg


def _masks(mx, rank, T, seq):
    kk = np.arange(128)[:, None]; qq = np.arange(128)[None, :]
    def tile(valid):
        return np.where(valid, 1.0, 0.0).astype(np.float32)
    out = []
    if mx == 0:
        for d in (-1, 0, 1):
            rel = 128 * d + kk - qq
            out.append(tile(np.abs(rel) <= 64))
        for d in range(-2, 3):
            rel = 128 * d + kk - qq
            out.append(tile((rel % 4 == 0) & (np.abs(rel) <= 256)))
        for d in range(-8, 9):
            rel = 128 * d + kk - qq
            out.append(tile((rel % 16 == 0) & (np.abs(rel) <= 1024)))
    elif mx == 1:
        for d in (-1, 0, 1):
            rel = 128 * d + kk - qq
            out.append(tile(np.abs(rel) <= 128))
    else:
        NQ = T // 128
        rows_total = seq // 64
        row0 = (rank % 4) * (T // 64)
        kr = kk // 64; kc = kk % 64; qr = qq // 64; qc = qq % 64
        cstart = np.clip(qc - 8, 0, 64 - 16)
        colv = (kc >= cstart) & (kc < cstart + 16)
        for i in (2, 0, 1, NQ - 2, NQ - 1):
            for d in range(-3, 4):
                Rq = row0 + 2 * i + qr
                Rk = row0 + 2 * (i + d) + kr
                rs = np.clip(Rq - 4, 0, rows_total - 8)
                rowv = (Rk >= rs) & (Rk < rs + 8)
                out.append(tile(rowv & colv))
    return np.stack(out, 0)


def _c_bias_gather(rpb):
    kk = np.arange(128)[:, None]; qq = np.arange(128)[None, :]
    kr = kk // 64; kc = kk % 64; qr = qq // 64; qc = qq % 64
    out = np.zeros((16, 7, 128, 128), np.float32)
    for d in range(-3, 4):
        dr = 2 * d + kr - qr
        dc = kc - qc
        ok = (np.abs(dr) <= 7) & (np.abs(dc) <= 15)
        g = rpb[:, np.clip(dr + 7, 0, 14), np.clip(dc + 15, 0, 30)]
        out[:, d + 3] = np.where(ok[None], g, 0.0)
    return out


def kernel(x, p, norm_g, ffn_wi, ffn_wo, ple_proj, ple_gate, a_wqkv, a_wo, b_wqkv, b_wo, b_sink, c_wqkv, c_wo, c_rpb):
    x = np.asarray(x, np.float32)
    B, SEQ, _ = x.shape
    depth = np.asarray(norm_g).shape[0]
    T = SEQ * B // 8
    NC = 8
    f = lambda a: np.ascontiguousarray(np.asarray(a, np.float32))
    p = f(p)
    n_b = len(range(1, depth, 3)); n_c = len(range(2, depth, 3))
    shared = {"norm_g": f(norm_g), "ffn_wi": f(ffn_wi), "ffn_wo": f(ffn_wo), "ple_proj": f(ple_proj), "ple_gate": f(ple_gate),
              "a_wqkv": f(a_wqkv), "a_wo": f(a_wo)}
    if n_b:
        shared.update({"b_wqkv": f(b_wqkv)[:n_b], "b_wo": f(b_wo)[:n_b], "b_sink": f(b_sink)[:n_b]})
    if n_c:
        shared.update({"c_wqkv": f(c_wqkv)[:n_c], "c_wo": f(c_wo)[:n_c],
                       "c_biasg": np.stack([_c_bias_gather(f(c_rpb)[j]) for j in range(n_c)], 0)})
    xf = x.reshape(B * SEQ, D)
    pf = p.reshape(depth, B * SEQ, PLE)
    P, nc = _get_prog(SEQ, depth)
    in_maps = []
    for r in range(NC):
        q = r % 4
        m = dict(shared)
        m["x"] = np.ascontiguousarray(xf[r * T:(r + 1) * T])
        m["p"] = np.ascontiguousarray(pf[:, r * T:(r + 1) * T])
        C_, S_ = _rope_tables(T, q * T)
        m["rope_c"] = C_; m["rope_s"] = S_
        fl = np.zeros((128, 6), np.float32)
        if q >= 1:
            fl[:, q - 1] = 1.0
        if q <= 2:
            fl[:, 3 + q] = 1.0
        m["flags"] = fl
        m["masks_a"] = _masks(0, r, T, SEQ)
        if n_b:
            m["masks_b"] = _masks(1, r, T, SEQ)
        if n_c:
            m["masks_c"] = _masks(2, r, T, SEQ)
        in_maps.append(m)
    res = run_bass_kernel_spmd(nc, in_maps, core_ids=list(range(NC))).results
    out = np.concatenate([np.asarray(res[r]["out"]) for r in range(NC)], 0).reshape(B, SEQ, D).astype(np.float32)
    return out
```
